# Optimizing a Trainium2 kernel written in Bass

```python
import math
import jax, jax.numpy as jnp
from jax import lax
import numpy as np

D_MODEL = 2048
BATCH = 4
SEQ = 4096
DEPTH = 1
DEC_BATCH = 32
DEC_SEQ = 64
PAST_LEN = 1024

CHUNK = 64
D_SSM = D_MODEL // 2
GROUP_CH = 16
N_GROUPS = D_SSM // GROUP_CH
STATE = 64
HEAD_DIM = 64
N_HEADS = (D_MODEL // 2) // HEAD_DIM
N_KV = 4
Q_PER_KV = N_HEADS // N_KV
D_ATTN = N_HEADS * HEAD_DIM
D_KV = N_KV * HEAD_DIM
IN_COLS = D_SSM + D_ATTN + 2 * D_KV
WINDOW = 128
BAND = WINDOW // CHUNK
ROT_DIM = HEAD_DIM // 4
ROPE_THETA = 500000.0
D_FF = ((8 * D_MODEL // 3 + 255) // 256) * 256
EPS = 1e-6
NEG = -1e30

kernel_name = 'chunk_causal_s5_swa_sink_hybrid'


def rmsnorm(x, g):
    xf = x.astype(jnp.float32)
    return xf * lax.rsqrt(jnp.mean(xf * xf, axis=-1, keepdims=True) + EPS) * g.astype(jnp.float32)


def rope_partial(x, pos):
    half = ROT_DIM // 2
    inv = ROPE_THETA ** (-jnp.arange(half, dtype=jnp.float32) * 2.0 / ROT_DIM)
    ang = pos.astype(jnp.float32)[:, None] * inv[None, :]
    cos = jnp.cos(ang)[:, None, :]
    sin = jnp.sin(ang)[:, None, :]
    x1 = x[..., :half]
    x2 = x[..., half:ROT_DIM]
    return jnp.concatenate([x1 * cos - x2 * sin, x2 * cos + x1 * sin, x[..., ROT_DIM:]], axis=-1)


def ssm_discretise(a_re, a_im, log_dt, b_re, b_im):
    a_re = a_re.astype(jnp.float32)
    a_im = a_im.astype(jnp.float32)
    dt = jnp.exp(log_dt.astype(jnp.float32))[:, None]
    mag = jnp.exp(a_re * dt)
    abar_re = mag * jnp.cos(a_im * dt)
    abar_im = mag * jnp.sin(a_im * dt)
    fr, fi = abar_re - 1.0, abar_im
    den = a_re * a_re + a_im * a_im
    cr = (fr * a_re + fi * a_im) / den
    ci = (fi * a_re - fr * a_im) / den
    b_re = b_re.astype(jnp.float32)
    b_im = b_im.astype(jnp.float32)
    bbar_re = cr[..., None] * b_re - ci[..., None] * b_im
    bbar_im = cr[..., None] * b_im + ci[..., None] * b_re
    return abar_re, abar_im, bbar_re, bbar_im


def _complex_affine_combine(e1, e2):
    a1r, a1i, b1r, b1i = e1
    a2r, a2i, b2r, b2i = e2
    return (a2r * a1r - a2i * a1i,
            a2r * a1i + a2i * a1r,
            a2r * b1r - a2i * b1i + b2r,
            a2r * b1i + a2i * b1r + b2i)


def ssm_mixer(u, h0r, h0i, a_re, a_im, log_dt, b_re, b_im, c_re, c_im, d_skip):
    B, L, _ = u.shape
    blk = min(L, CHUNK)
    nb = L // blk
    abr, abi, bbr, bbi = ssm_discretise(a_re, a_im, log_dt, b_re, b_im)
    c_re = c_re.astype(jnp.float32)
    c_im = c_im.astype(jnp.float32)
    a_r = jnp.broadcast_to(abr, (B, blk, N_GROUPS, STATE))
    a_i = jnp.broadcast_to(abi, (B, blk, N_GROUPS, STATE))
    ug = u.reshape(B, nb, blk, N_GROUPS, GROUP_CH).transpose(1, 0, 2, 3, 4)

    def body(carry, ub):
        hr0, hi0 = carry
        br = jnp.einsum('blgc,gnc->blgn', ub, bbr)
        bi = jnp.einsum('blgc,gnc->blgn', ub, bbi)
        ar_c, ai_c, br_c, bi_c = lax.associative_scan(_complex_affine_combine, (a_r, a_i, br, bi), axis=1)
        hr = br_c + ar_c * hr0[:, None] - ai_c * hi0[:, None]
        hi = bi_c + ar_c * hi0[:, None] + ai_c * hr0[:, None]
        y = jnp.einsum('blgn,gcn->blgc', hr, c_re) - jnp.einsum('blgn,gcn->blgc', hi, c_im)
        return (hr[:, -1], hi[:, -1]), y

    (hr, hi), ys = lax.scan(body, (h0r.astype(jnp.float32), h0i.astype(jnp.float32)), ug)
    y = ys.transpose(1, 0, 2, 3, 4).reshape(B, L, D_SSM) + d_skip.astype(jnp.float32) * u
    return y, hr, hi


def sink_attention(q, k, v, sinks, mask):
    scale = HEAD_DIM ** -0.5
    s = jnp.einsum('...qkgd,...skd->...kgqs', q, k.astype(jnp.float32)) * scale
    s = jnp.where(mask, s, NEG)
    sk = sinks.astype(jnp.float32).reshape(N_KV, Q_PER_KV)[:, :, None, None]
    m = jnp.maximum(jnp.max(s, axis=-1, keepdims=True), sk)
    p = jnp.exp(s - m)
    den = jnp.sum(p, axis=-1, keepdims=True) + jnp.exp(sk - m)
    return jnp.einsum('...kgqs,...skd->...qkgd', p / den, v.astype(jnp.float32))


def attn_prompt(q, k, v, sinks):
    B, L = q.shape[0], q.shape[1]
    nc = L // CHUNK
    S = (BAND + 1) * CHUNK
    qc = q.reshape(B, nc, CHUNK, N_KV, Q_PER_KV, HEAD_DIM)
    pad = ((0, 0), (BAND, 0), (0, 0), (0, 0), (0, 0))
    kp = jnp.pad(k.reshape(B, nc, CHUNK, N_KV, HEAD_DIM), pad)
    vp = jnp.pad(v.reshape(B, nc, CHUNK, N_KV, HEAD_DIM), pad)
    kb = jnp.concatenate([kp[:, i:i + nc] for i in range(BAND + 1)], axis=2)
    vb = jnp.concatenate([vp[:, i:i + nc] for i in range(BAND + 1)], axis=2)
    src_chunk = jnp.arange(nc)[:, None] + (jnp.arange(S) // CHUNK)[None, :] - BAND
    mask = (src_chunk >= 0)[:, None, None, None, :]
    o = sink_attention(qc, kb, vb, sinks, mask)
    return o.reshape(B, L, D_ATTN)


def trunk_layer(x, pos, k_cache, v_cache, h0r, h0i, p):
    (norm1, w_in, q_norm, k_norm, sinks, a_re, a_im, log_dt, b_re, b_im, c_re, c_im, d_skip,
     w_glu, w_br_ssm, w_br_attn, w_gate, w_out, norm2, w_fg, w_fu, w_fd) = p
    B, L, _ = x.shape
    xf = x.astype(jnp.float32)
    xn = rmsnorm(xf, norm1)
    z = xn @ w_in
    u = z[..., :D_SSM]
    q = z[..., D_SSM:D_SSM + D_ATTN]
    k = z[..., D_SSM + D_ATTN:D_SSM + D_ATTN + D_KV]
    v = z[..., D_SSM + D_ATTN + D_KV:]

    ys, hr, hi = ssm_mixer(u, h0r, h0i, a_re, a_im, log_dt, b_re, b_im, c_re, c_im, d_skip)
    ys = jax.nn.gelu(ys)
    ys = ys * jax.nn.sigmoid(ys @ w_glu)

    q = rope_partial(rmsnorm(q.reshape(B, L, N_HEADS, HEAD_DIM), q_norm), pos)
    k = rope_partial(rmsnorm(k.reshape(B, L, N_KV, HEAD_DIM), k_norm), pos)
    v = v.reshape(B, L, N_KV, HEAD_DIM)
    if k_cache is None:
        o = attn_prompt(q, k, v, sinks)
        win = min(WINDOW, L)
        k_all, v_all = k, v
    else:
        win = k_cache.shape[1]
        k_all = jnp.concatenate([k_cache.astype(jnp.float32), k], axis=1)
        v_all = jnp.concatenate([v_cache.astype(jnp.float32), v], axis=1)
        qs = q.reshape(B, L, N_KV, Q_PER_KV, HEAD_DIM)
        o = sink_attention(qs, k_all, v_all, sinks, True).reshape(B, L, D_ATTN)
    new_k = k_all[:, -win:].astype(x.dtype)
    new_v = v_all[:, -win:].astype(x.dtype)

    g = jax.nn.sigmoid(xn @ w_gate)
    mixed = g[..., :D_MODEL] * (ys @ w_br_ssm) + g[..., D_MODEL:] * (o @ w_br_attn)
    h = xf + mixed @ w_out

    hn = rmsnorm(h, norm2)
    ff = (jax.nn.silu(hn @ w_fg) * (hn @ w_fu)) @ w_fd
    out = (h + ff).astype(x.dtype)
    return out, new_k, new_v, hr.astype(x.dtype), hi.astype(x.dtype)


def setup_inputs(seed: int = 0) -> dict:
    key = jax.random.key(seed)
    ks = iter(jax.random.split(key, 32))
    f32 = jnp.float32

    def nrm(shape, scale):
        return jax.random.normal(next(ks), shape, f32) * scale

    win = min(WINDOW, PAST_LEN)
    x_prompt = nrm((BATCH, SEQ, D_MODEL), 1.0)
    x_sample = nrm((DEC_BATCH, DEC_SEQ, D_MODEL), 1.0)
    cache_k = nrm((DEPTH, DEC_BATCH, win, N_KV, HEAD_DIM), 1.0)
    cache_v = nrm((DEPTH, DEC_BATCH, win, N_KV, HEAD_DIM), 1.0)
    state_ssm_re = nrm((DEPTH, DEC_BATCH, N_GROUPS, STATE), 0.1)
    state_ssm_im = nrm((DEPTH, DEC_BATCH, N_GROUPS, STATE), 0.1)
    norm1 = 1.0 + nrm((DEPTH, D_MODEL), 0.02)
    w_in = nrm((DEPTH, D_MODEL, IN_COLS), D_MODEL ** -0.5)
    q_norm = 1.0 + nrm((DEPTH, HEAD_DIM), 0.02)
    k_norm = 1.0 + nrm((DEPTH, HEAD_DIM), 0.02)
    sinks = nrm((DEPTH, N_HEADS), 0.5)
    n_idx = jnp.arange(STATE, dtype=f32)
    ssm_a_re = -0.5 + nrm((DEPTH, N_GROUPS, STATE), 0.01)
    ssm_a_im = math.pi * n_idx + nrm((DEPTH, N_GROUPS, STATE), 0.01)
    ssm_log_dt = jax.random.uniform(next(ks), (DEPTH, N_GROUPS), f32, math.log(1e-3), math.log(1e-1))
    ssm_b_re = nrm((DEPTH, N_GROUPS, STATE, GROUP_CH), (2 * GROUP_CH) ** -0.5)
    ssm_b_im = nrm((DEPTH, N_GROUPS, STATE, GROUP_CH), (2 * GROUP_CH) ** -0.5)
    ssm_c_re = nrm((DEPTH, N_GROUPS, GROUP_CH, STATE), (2 * STATE) ** -0.5)
    ssm_c_im = nrm((DEPTH, N_GROUPS, GROUP_CH, STATE), (2 * STATE) ** -0.5)
    ssm_d = nrm((DEPTH, D_SSM), 1.0)
    w_glu = nrm((DEPTH, D_SSM, D_SSM), D_SSM ** -0.5)
    w_br_ssm = nrm((DEPTH, D_SSM, D_MODEL), D_SSM ** -0.5)
    w_br_attn = nrm((DEPTH, D_ATTN, D_MODEL), D_ATTN ** -0.5)
    w_gate = nrm((DEPTH, D_MODEL, 2 * D_MODEL), D_MODEL ** -0.5)
    w_out = nrm((DEPTH, D_MODEL, D_MODEL), D_MODEL ** -0.5)
    norm2 = 1.0 + nrm((DEPTH, D_MODEL), 0.02)
    w_ffn_gate = nrm((DEPTH, D_MODEL, D_FF), D_MODEL ** -0.5)
    w_ffn_up = nrm((DEPTH, D_MODEL, D_FF), D_MODEL ** -0.5)
    w_ffn_down = nrm((DEPTH, D_FF, D_MODEL), D_FF ** -0.5)
    return {'x_prompt': x_prompt, 'x_sample': x_sample,
            'cache_k': cache_k, 'cache_v': cache_v,
            'state_ssm_re': state_ssm_re, 'state_ssm_im': state_ssm_im,
            'norm1': norm1, 'w_in': w_in, 'q_norm': q_norm, 'k_norm': k_norm, 'sinks': sinks,
            'ssm_a_re': ssm_a_re, 'ssm_a_im': ssm_a_im, 'ssm_log_dt': ssm_log_dt,
            'ssm_b_re': ssm_b_re, 'ssm_b_im': ssm_b_im, 'ssm_c_re': ssm_c_re, 'ssm_c_im': ssm_c_im,
            'ssm_d': ssm_d, 'w_glu': w_glu, 'w_br_ssm': w_br_ssm, 'w_br_attn': w_br_attn,
            'w_gate': w_gate, 'w_out': w_out, 'norm2': norm2,
            'w_ffn_gate': w_ffn_gate, 'w_ffn_up': w_ffn_up, 'w_ffn_down': w_ffn_down}


def reference(x_prompt, x_sample, cache_k, cache_v, state_ssm_re, state_ssm_im,
              norm1, w_in, q_norm, k_norm, sinks,
              ssm_a_re, ssm_a_im, ssm_log_dt, ssm_b_re, ssm_b_im, ssm_c_re, ssm_c_im, ssm_d,
              w_glu, w_br_ssm, w_br_attn, w_gate, w_out, norm2,
              w_ffn_gate, w_ffn_up, w_ffn_down):
    Bp, Lp = x_prompt.shape[0], x_prompt.shape[1]
    Ls = x_sample.shape[1]
    pos_p = jnp.arange(Lp)
    pos_s = PAST_LEN + jnp.arange(Ls)
    h0p = jnp.zeros((Bp, N_GROUPS, STATE), jnp.float32)
    yp, ys = x_prompt, x_sample
    kp_l, vp_l, rp_l, ip_l, ks_l, vs_l, rs_l, is_l = [], [], [], [], [], [], [], []
    for l in range(DEPTH):
        p = (norm1[l], w_in[l], q_norm[l], k_norm[l], sinks[l],
             ssm_a_re[l], ssm_a_im[l], ssm_log_dt[l], ssm_b_re[l], ssm_b_im[l],
             ssm_c_re[l], ssm_c_im[l], ssm_d[l], w_glu[l], w_br_ssm[l], w_br_attn[l],
             w_gate[l], w_out[l], norm2[l], w_ffn_gate[l], w_ffn_up[l], w_ffn_down[l])
        yp, kp, vp, rp, ip = trunk_layer(yp, pos_p, None, None, h0p, h0p, p)
        ys, kS, vS, rS, iS = trunk_layer(ys, pos_s, cache_k[l], cache_v[l],
                                         state_ssm_re[l], state_ssm_im[l], p)
        kp_l.append(kp); vp_l.append(vp); rp_l.append(rp); ip_l.append(ip)
        ks_l.append(kS); vs_l.append(vS); rs_l.append(rS); is_l.append(iS)
    k_win_prompt = jnp.stack(kp_l)
    v_win_prompt = jnp.stack(vp_l)
    ssm_re_prompt = jnp.stack(rp_l)
    ssm_im_prompt = jnp.stack(ip_l)
    k_win_sample = jnp.stack(ks_l)
    v_win_sample = jnp.stack(vs_l)
    ssm_re_sample = jnp.stack(rs_l)
    ssm_im_sample = jnp.stack(is_l)
    return (yp, ys, k_win_prompt, v_win_prompt, ssm_re_prompt, ssm_im_prompt,
            k_win_sample, v_win_sample, ssm_re_sample, ssm_im_sample)
```

```python
import numpy as np
from contextlib import ExitStack
import concourse.bass as bass
import concourse.mybir as mybir
from concourse.bass_utils import run_bass_kernel_spmd

F32 = mybir.dt.float32
BF16 = mybir.dt.bfloat16
ALU = mybir.AluOpType
AF = mybir.ActivationFunctionType
AX = mybir.AxisListType

NCORES = 8
D = 2048
DS = 1024
DFF = 5632
T = 512
NTT = T // 128
TS = 256
NCOL = TS // 8
NMAIN = 2304
NCTX = 2048
NPASS_P = 2048 // T
NPASS_C = NCTX // T
EPS = 1e-6
BIG = -30000.0


class Op:
    __slots__ = ("eng", "fn", "deps", "signaled", "value", "sem", "is_dma", "idx", "phase")

    def __init__(self, eng, fn, is_dma=False, sem=None):
        self.eng = eng
        self.fn = fn
        self.deps = []
        self.signaled = False
        self.value = 0
        self.sem = sem
        self.is_dma = is_dma


class Res:
    __slots__ = ("w", "r")

    def __init__(self):
        self.w = None
        self.r = {}


class Sched:
    ENGS = ("pe", "act", "dve", "pool", "sp")

    def __init__(self):
        self.ops = {e: [] for e in self.ENGS}
        self.res = {}
        self.n = 0
        self.pending = {e: [] for e in self.ENGS}
        self.phase = "setup"

    def barrier(self):
        lasts = []
        for e in self.ENGS:
            comp = [o for o in self.ops[e] if not o.is_dma]
            if comp:
                lasts.append(comp[-1])
        for e in self.ENGS:
            self.pending[e] = list(lasts)

    def _res(self, k):
        r = self.res.get(k)
        if r is None:
            r = self.res[k] = Res()
        return r

    def op(self, eng, fn, reads=(), writes=(), dma_sem=None):
        o = Op(eng, fn, is_dma=dma_sem is not None, sem=dma_sem)
        o.idx = self.n
        o.phase = self.phase
        self.n += 1
        deps = {}
        for k in reads:
            r = self._res(k)
            if r.w is not None:
                deps[id(r.w)] = r.w
        for k in writes:
            r = self._res(k)
            if r.w is not None:
                deps[id(r.w)] = r.w
            for d in r.r.values():
                deps[id(d)] = d
        if self.pending[eng]:
            for d in self.pending[eng]:
                deps[id(d)] = d
            self.pending[eng] = []
        o.deps = list(deps.values())
        for k in reads:
            r = self._res(k)
            key = ("dma", o.idx) if o.is_dma else eng
            r.r[key] = o
        for k in writes:
            r = self._res(k)
            r.w = o
            r.r = {}
        self.ops[eng].append(o)
        return o

    def finalize(self):
        for e in self.ENGS:
            for o in self.ops[e]:
                for d in o.deps:
                    if d.is_dma:
                        d.signaled = True
                    elif d.eng == "pe" and o.eng == "pe" and not o.is_dma:
                        pass
                    else:
                        d.signaled = True
        cnt = {e: 0 for e in self.ENGS}
        dcnt = {}
        allops = sorted((o for e in self.ENGS for o in self.ops[e]), key=lambda o: o.idx)
        for o in allops:
            if o.is_dma:
                dcnt[o.sem] = dcnt.get(o.sem, 0) + 16
                o.value = dcnt[o.sem]
        for e in self.ENGS:
            for o in self.ops[e]:
                if (not o.is_dma) and o.signaled:
                    cnt[e] += 1
                    o.value = cnt[e]
        self.final_dma = dict(dcnt)

    def emit(self, eng_name, eng, sems, dsems):
        seen = {}
        for o in self.ops[eng_name]:
            need = {}
            for d in o.deps:
                if d.is_dma:
                    key = ("d", d.sem)
                    sem = dsems[d.sem]
                else:
                    if d.eng == "pe" and eng_name == "pe" and not o.is_dma:
                        continue
                    key = ("e", d.eng)
                    sem = sems[d.eng]
                if need.get(key, (None, 0))[1] < d.value:
                    need[key] = (sem, d.value)
            for key, (sem, v) in need.items():
                if seen.get(key, 0) >= v:
                    continue
                eng.wait_ge(sem, v)
                seen[key] = v
            ins = o.fn(eng)
            if o.is_dma:
                ins.then_inc(dsems[o.sem], 16)
            elif o.signaled:
                ins.then_inc(sems[eng_name], 1)


_NC_CACHE = {}
DBG = {"ctx": None, "main": None, "stop": None}


class _Stop(Exception):
    pass


def stage(name):
    if DBG["stop"] == name:
        raise _Stop()


def build_program():
    nc = bass.Bass("TRN2", target_bir_lowering=False)

    def din(name, shape, dt=F32):
        return nc.dram_tensor(name, list(shape), dt, kind="ExternalInput").ap()

    def dout(name, shape, dt=F32):
        return nc.dram_tensor(name, list(shape), dt, kind="ExternalOutput").ap()

    xm = din("xm", [NMAIN, D])
    xc = din("xc", [NCTX, D])
    rope = din("rope", [128, 19, 16])
    ck = din("ck", [4, 128, 256])
    cv = din("cv", [4, 128, 256])
    st0 = din("st0", [128, 2 * 32 * 4])
    maskb = din("maskb", [128, 4])
    g1T_d = din("g1T", [128, 16])
    g2T_d = din("g2T", [128, 16])
    qnb_d = din("qnb", [128, 64])
    knb_d = din("knb", [128, 64])
    skl_d = din("skl", [128, 8])
    are_d = din("are", [128, 32])
    aim_d = din("aim", [128, 32])
    ldt_d = din("ldt", [128, 32])
    bre_d = din("bre", [128, 32 * 16])
    bim_d = din("bim", [128, 32 * 16])
    cre_d = din("cre", [32, 32 * 64])
    cim_d = din("cim", [32, 32 * 64])
    dsk_d = din("dsk", [128, 8])
    w_in = din("w_in", [D, 2560])
    w_glu = din("w_glu", [DS, DS])
    w_brs = din("w_brs", [DS, D])
    w_bra = din("w_bra", [DS, D])
    w_gate = din("w_gate", [D, 2 * D])
    w_out = din("w_out", [D, D])
    w_fg = din("w_fg", [D, DFF])
    w_fu = din("w_fu", [D, DFF])
    w_fd = din("w_fd", [DFF, D])

    ym = dout("ym", [NMAIN, D])
    kwin = dout("kwin", [128, 256])
    vwin = dout("vwin", [128, 256])
    hfin = dout("hfin", [128, 64])
    ks_o = dout("ks_o", [4, 128, 256])
    vs_o = dout("vs_o", [4, 128, 256])
    hsfin = dout("hsfin", [128, 2 * 32 * 4])

    WS_d = nc.dram_tensor("WS_d", [8, 128, 16 * 128], BF16).ap()
    YK_d = nc.dram_tensor("YK_d", [8, 128, 3072], BF16).ap()
    WB_d = nc.dram_tensor("WB_d", [60, 128, 8192], BF16).ap()

    S = Sched()
    es = ExitStack()
    with es:
        def sb(name, shape, dt):
            return es.enter_context(nc.sbuf_tensor(name, list(shape), dt))

        def PE(fn, r=(), w=()):
            return S.op("pe", fn, r, w)

        def ACT(fn, r=(), w=()):
            return S.op("act", fn, r, w)

        def DVE(fn, r=(), w=()):
            return S.op("dve", fn, r, w)

        def POOL(fn, r=(), w=()):
            return S.op("pool", fn, r, w)

        dsem_names = []

        def DMA(eng, sem, fn, r=(), w=()):
            if sem not in dsem_names:
                dsem_names.append(sem)
            return S.op(eng, fn, r, w, dma_sem=sem)

        xh = sb("xh", [128, NTT, D], F32)
        nT = sb("nT", [128, 16, T], BF16)
        xnb = sb("xnb", [128, D], BF16)
        wpf = [sb("wp%d" % i, [128, 8192], BF16) for i in range(2)]
        uT = sb("uT", [128, 8, T], BF16)
        geluT = sb("geluT", [128, 8, T], BF16)
        oT = geluT
        maskl = sb("maskl", [128, 7, 8], BF16)
        um = sb("um", [128, 7, TS], BF16)
        um4s = [sb("um4_%d" % i, [128, 4, TS], BF16) for i in range(2)]
        um4 = um4s[0]
        rmask = sb("rmask", [128, 4], F32)
        mixp = sb("mixp", [128, 4, T], BF16)
        tA = sb("tA", [128, 4, T], BF16)
        t1 = sb("t1", [128, 4, T], BF16)
        t2 = sb("t2", [128, T], F32)
        qfs = [sb("qf%d" % i, [128, 512], F32) for i in range(2)]
        qkbs = [sb("qkb%d" % i, [128, 512], BF16) for i in range(2)]
        tmpq = sb("tmpq", [128, 512], F32)
        rt = sb("rt", [128, 4, 64], F32)
        st = sb("st", [128, 8, 20], F32)
        qT = sb("qT", [64, T // 64, 16, 64], BF16)
        kT = sb("kT", [64, 4, 128 + T], BF16)
        vb = sb("vb", [128, 1 + NTT, 256], BF16)
        kTs = tA[0:64].rearrange("p a b -> p (a b)")[:, 0:2048].rearrange("p (g s t) -> p g s t", g=4, s=4)
        vcs = mixp[:, :, 0:256]
        ckb = um4
        PTs = [sb("PT%d" % i, [128, 2, 256], BF16) for i in range(2)]
        dtmps = [sb("dtmp%d" % i, [128, 128], F32) for i in range(2)]
        Ssb = sb("Ssb", [128, 2, 32, NCOL], F32)
        Hf = sb("Hf", [128, 2 * 32 * 36], F32)
        Hb = sb("Hb", [128, 2, 32, NCOL], BF16)
        hc = sb("hc", [128, 2, 32], F32)
        hst = sb("hst", [128, 2 * 32 * 4], F32)
        sc = sb("sc", [128, 4, 128], F32)
        wsb = [sb("wsb%d" % i, [128, 16, 128], BF16) for i in range(2)]
        ykb = [sb("ykb%d" % i, [128, 3072], BF16) for i in range(2)]
        wyb = [ykb[i][:, 0:2048].rearrange("p (a b c) -> p a b c", a=4, b=16) for i in range(2)]
        ktb = [ykb[i][:, 2048:3072].rearrange("p (a b) -> p a b", a=8) for i in range(2)]
        ws_slots = [(wsb[0], "wsb0"), (wsb[1], "wsb1"),
                    (wpf[0][:, 4096:6144].rearrange("p (a b) -> p a b", a=16), "wp0h"), (wpf[1][:, 4096:6144].rearrange("p (a b) -> p a b", a=16), "wp1h")]
        ws_slots_ctx = [ws_slots[0], ws_slots[1],
                        (ykb[0][:, 0:2048].rearrange("p (a b) -> p a b", a=16), "ykb0"), (ykb[1][:, 0:2048].rearrange("p (a b) -> p a b", a=16), "ykb1")]
        yk_slots = [(ykb[0][:, :], "ykb0"), (ykb[1][:, :], "ykb1"), (wpf[0][:, 4096:7168], "wp0h"), (wpf[1][:, 4096:7168], "wp1h")]
        identf = sb("identf", [128, 128], F32)
        ident = sb("ident", [128, 128], BF16)
        ones_b = sb("ones_b", [128, 64], BF16)
        ropeT = sb("ropeT", [128, 19, 16], F32)
        g1T = sb("g1T_s", [128, 16], F32)
        g2T = sb("g2T_s", [128, 16], F32)
        qnb = sb("qnb_s", [128, 64], F32)
        knb = sb("knb_s", [128, 64], F32)
        skx = sb("skx", [128, 8], F32)
        bias5 = sb("bias5", [128, 5], F32)
        cst = sb("cst", [128, 4], F32)
        dsk = sb("dsk_s", [128, 8], F32)
        A8 = sb("A8", [128, 2, 32], F32)
        P8t = sb("P8t", [128, 2, 8, 32], F32)
        A64 = sb("A64", [128, 2, 32], F32)
        Pw = Hf[:, 0:576].rearrange("p (k a g) -> p k a g", k=9, a=2)
        dz = Hf[:, 576:960].rearrange("p (k g) -> p k g", k=12)
        Ssf = Ssb[:].rearrange("p a g j -> p (a g j)")
        Bn = Ssf[:, 0:1024].rearrange("p (a g c) -> p a g c", a=2, g=32)
        Bb = Ssf[:, 1024:2048].rearrange("p (a g c) -> p a g c", a=2, g=32)
        xhf = xh[:].rearrange("p t d -> p (t d)")
        Cn = xh[0:32].rearrange("p t d -> p (t d)")[:, 0:4096].rearrange("p (a g n) -> p a g n", a=2, g=32)
        ZC = wpf[0][0:32, 0:8192].rearrange("p (a g n) -> p a g n", a=2, g=32)
        CT = xhf[:, 4096:6144].rearrange("p (a g c) -> p a g c", a=2, g=32)
        CTb = wpf[1][:, 0:2048].rearrange("p (a g c) -> p a g c", a=2, g=32)
        Zt = wpf[1][:, 2048:4096].rearrange("p (k c) -> p k c", k=16)
        pbt4 = xhf[:, 6144:7168].rearrange("p (a k q c) -> p a k q c", a=4, k=4, q=4)
        wyt2 = xhf[:, 7168:8192].rearrange("p (a r q c) -> p a r q c", a=4, r=2, q=4)
        ktf = t2[:, 0:128]
        mblk = t2[:, 128:160]
        BD = t2[:, 256:384]

        pbank = [es.enter_context(nc.psum_tensor("pb%d" % i, [128, 512], F32)) for i in range(6)]
        ptb = [es.enter_context(nc.psum_tensor("pt%d" % i, [128, 1024], BF16)) for i in range(2)]
        state = {"b": 0, "t": 0, "w": 0, "ws": 0, "wy": 0, "kt": 0, "a": 0, "q": 0, "u": 0, "ws2": 0, "yk2": 0}

        def nb():
            i = state["b"] % 6
            state["b"] += 1
            return pbank[i], "pb%d" % i

        def nt():
            i = state["t"] % 2
            state["t"] += 1
            return ptb[i], "pt%d" % i

        pidx = {}
        ptodo = []
        pq = {"q": "sp"}

        def panel_key(W, r0, kc, c0, nw):
            return (id(W), r0, kc, c0, nw)

        def register_panels():
            lst = [(w_in, 0, 16, 0, 512), (w_in, 0, 16, 512, 512), (w_glu, 0, 8, 0, 512), (w_glu, 0, 8, 512, 512)]
            lst += [(w_in, 0, 16, 1024 + pn * 512, 512) for pn in range(3)]
            for mp in range(4):
                lst += [(w_gate, 0, 16, mp * 512, 512), (w_brs, 0, 8, mp * 512, 512), (w_gate, 0, 16, D + mp * 512, 512), (w_bra, 0, 8, mp * 512, 512), (w_out, mp * 4, 4, 0, D)]
            for fp in range(11):
                lst += [(w_fg, 0, 16, fp * 512, 512), (w_fu, 0, 16, fp * 512, 512), (w_fd, fp * 4, 4, 0, D)]
            for it in lst:
                pidx[panel_key(*it)] = len(pidx)
                ptodo.append(it)

        def convert_panels(n):
            for _ in range(min(n, len(ptodo))):
                W, r0, kc, c0, nw = ptodo.pop(0)
                idx = pidx[panel_key(W, r0, kc, c0, nw)]
                src = W[r0 * 128:(r0 + kc) * 128, c0:c0 + nw].rearrange("(k p) n -> p k n", p=128)
                dst = WB_d[idx, :, 0:kc * nw].rearrange("p (k n) -> p k n", n=nw)
                DMA("pool", "cv%d" % (idx % 16), lambda e, src=src, dst=dst: e.dma_start(out=dst, in_=src), r=(["WB%d" % (idx - 16)] if idx >= 16 else []), w=["WB%d" % idx])

        def load_panel(W, r0, kc, c0, nw):
            i = state["w"] % 2
            state["w"] += 1
            idx = pidx[panel_key(W, r0, kc, c0, nw)]
            view = wpf[i][:, 0:kc * nw].rearrange("p (k n) -> p k n", n=nw)
            names = ["wp%d" % i] + (["wp%dh" % i] if kc * nw > 4096 else [])
            flat = wpf[i][:, 0:kc * nw]
            src = WB_d[idx, :, 0:kc * nw]
            DMA(pq["q"], "wp%d_%s" % (i, pq["q"]), lambda e: e.dma_start(out=flat, in_=src), r=["WB%d" % idx], w=names)
            return view, names

        ckf = tmpq[:, 0:512].rearrange("p (s f) -> p s f", s=2)
        DMA("sp", "c0", lambda e: e.dma_start(out=ropeT[:], in_=rope), w=["ropeT"])
        DMA("sp", "c1", lambda e: e.dma_start(out=g1T[:], in_=g1T_d), w=["g1T"])
        DMA("sp", "c2", lambda e: e.dma_start(out=g2T[:], in_=g2T_d), w=["g2T"])
        DMA("sp", "c3", lambda e: e.dma_start(out=qnb[:], in_=qnb_d), w=["qnb"])
        DMA("sp", "c4", lambda e: e.dma_start(out=knb[:], in_=knb_d), w=["knb"])
        DMA("sp", "c5", lambda e: e.dma_start(out=skx[:], in_=skl_d), w=["skx"])
        DMA("sp", "c6", lambda e: e.dma_start(out=bias5[:, 0:4], in_=maskb), w=["bias5"])
        DMA("sp", "c7", lambda e: e.dma_start(out=dsk[:], in_=dsk_d), w=["dsk"])
        DMA("sp", "c8", lambda e: e.dma_start(out=dz[:, 0, :], in_=are_d), w=["dz0"])
        DMA("sp", "c9", lambda e: e.dma_start(out=dz[:, 1, :], in_=aim_d), w=["dz1"])
        DMA("sp", "c10", lambda e: e.dma_start(out=dz[:, 2, :], in_=ldt_d), w=["dz2"])
        DMA("sp", "c11", lambda e: e.dma_start(out=Bn[:, 0].rearrange("p a b -> p (a b)"), in_=bre_d), w=["Bn"])
        DMA("sp", "c12", lambda e: e.dma_start(out=Bn[:, 1].rearrange("p a b -> p (a b)"), in_=bim_d), w=["Bn"])
        DMA("sp", "c13", lambda e: e.dma_start(out=Cn[:, 0].rearrange("p a b -> p (a b)"), in_=cre_d), w=["Cn"])
        DMA("sp", "c14", lambda e: e.dma_start(out=Cn[:, 1].rearrange("p a b -> p (a b)"), in_=cim_d), w=["Cn"])

        POOL(lambda e: e.memset(identf[:], 0.0), w=["identf"])
        POOL(lambda e: e.affine_select(out=identf[:], in_=identf[:], pattern=[[-1, 128]], compare_op=ALU.not_equal, fill=1.0, base=0, channel_multiplier=1), r=["identf"], w=["identf"])
        POOL(lambda e: e.memset(ones_b[:], 1.0), w=["ones_b"])
        POOL(lambda e: e.memset(maskl[:], 0.0), w=["maskl"])
        POOL(lambda e: e.memset(rmask[:], 0.0), w=["rmask"])
        for q in range(3):
            POOL(lambda e, q=q: e.memset(rmask[32 * q:32 * q + 32, q:q + 1], 1.0), r=["rmask"], w=["rmask"])
        POOL(lambda e: e.memset(rmask[64:128, 3:4], 1.0), r=["rmask"], w=["rmask"])
        POOL(lambda e: e.memset(rmask[64:96, 3:4], 0.0), r=["rmask"], w=["rmask"])
        for tau in range(1, 8):
            POOL(lambda e, tau=tau: e.memset(maskl[:, tau - 1, 0:8 - tau], 1.0), r=["maskl"], w=["maskl"])
        POOL(lambda e: e.memset(cst[:, 0:1], EPS), w=["cst0"])
        POOL(lambda e: e.memset(cst[:, 1:2], float(np.pi / 2)), w=["cst1"])
        POOL(lambda e: e.memset(hc[:], 0.0), w=["hc"])
        POOL(lambda e: e.memset(mblk[:], 0.0), w=["mblk"])
        POOL(lambda e: e.memset(mblk[0:64, 0:16], 1.0), r=["mblk"], w=["mblk"])
        POOL(lambda e: e.memset(mblk[64:128, 16:32], 1.0), r=["mblk"], w=["mblk"])
        POOL(lambda e: e.memset(BD[:], 0.0), w=["BD"])
        for q in range(4):
            if q < 3:
                POOL(lambda e, q=q: e.memset(BD[32 * q:32 * q + 32, 32 * q:32 * q + 32], 1.0), r=["BD"], w=["BD"])
        POOL(lambda e: e.memset(BD[64:128, 96:128], 1.0), r=["BD"], w=["BD"])
        POOL(lambda e: e.memset(BD[64:96, 96:128], 0.0), r=["BD"], w=["BD"])
        POOL(lambda e: e.memset(Zt[:], 0.0), w=["Zt"])
        DVE(lambda e: e.tensor_copy(out=ident[:], in_=identf[:]), r=["identf"], w=["ident"])

        DVE(lambda e: e.tensor_tensor(out=tmpq[:, 0:64], in0=qnb[:], in1=qnb[:], op=ALU.mult), r=["qnb"], w=["tmpq"])
        DVE(lambda e: e.tensor_reduce(out=st[:, 0, 0:1], in_=tmpq[:, 0:64], axis=AX.X, op=ALU.max), r=["tmpq"], w=["st"])
        DVE(lambda e: e.tensor_tensor(out=tmpq[:, 64:128], in0=knb[:], in1=knb[:], op=ALU.mult), r=["knb"], w=["tmpq"])
        DVE(lambda e: e.tensor_reduce(out=st[:, 0, 1:2], in_=tmpq[:, 64:128], axis=AX.X, op=ALU.max), r=["tmpq", "st"], w=["st"])
        DVE(lambda e: e.tensor_tensor(out=st[:, 0, 2:3], in0=st[:, 0, 0:1], in1=st[:, 0, 1:2], op=ALU.mult), r=["st"], w=["st"])
        ACT(lambda e: e.activation(out=st[:, 0, 3:4], in_=st[:, 0, 2:3], func=AF.Ln, scale=64.0), r=["st"], w=["st"])
        ACT(lambda e: e.activation(out=st[:, 0, 4:5], in_=st[:, 0, 3:4], func=AF.Exp, scale=0.5), r=["st"], w=["st"])
        DVE(lambda e: e.tensor_scalar(out=cst[:, 2:3], in0=st[:, 0, 4:5], scalar1=-1.0, scalar2=None, op0=ALU.mult), r=["st"], w=["cst2"])
        DVE(lambda e: e.tensor_copy(out=bias5[:, 4:5], in_=cst[:, 2:3]), r=["cst2", "bias5"], w=["bias5"])
        DVE(lambda e: e.tensor_scalar(out=bias5[:, 0:4], in0=bias5[:, 0:4], scalar1=cst[:, 2:3], scalar2=None, op0=ALU.add), r=["cst2", "bias5"], w=["bias5"])
        ACT(lambda e: e.activation(out=skx[:], in_=skx[:], func=AF.Exp, bias=cst[:, 2:3], scale=1.0), r=["skx", "cst2"], w=["skx"])

        ACT(lambda e: e.activation(out=dz[:, 3, :], in_=dz[:, 2, :], func=AF.Exp), r=["dz2"], w=["dz3"])
        DVE(lambda e: e.tensor_tensor(out=dz[:, 4, :], in0=dz[:, 0, :], in1=dz[:, 3, :], op=ALU.mult), r=["dz0", "dz3"], w=["dz4"])
        DVE(lambda e: e.tensor_tensor(out=dz[:, 5, :], in0=dz[:, 1, :], in1=dz[:, 3, :], op=ALU.mult), r=["dz1", "dz3"], w=["dz5"])
        ACT(lambda e: e.activation(out=dz[:, 6, :], in_=dz[:, 4, :], func=AF.Exp, scale=1.0 / 32), r=["dz4"], w=["dz6"])
        ACT(lambda e: e.activation(out=dz[:, 7, :], in_=dz[:, 5, :], func=AF.Sin, scale=1.0 / 32), r=["dz5"], w=["dz7"])
        ACT(lambda e: e.activation(out=dz[:, 8, :], in_=dz[:, 5, :], func=AF.Sin, scale=1.0 / 32, bias=cst[:, 1:2]), r=["dz5", "cst1"], w=["dz8"])
        DVE(lambda e: e.tensor_tensor(out=Pw[:, 1, 0, :], in0=dz[:, 6, :], in1=dz[:, 8, :], op=ALU.mult), r=["dz6", "dz8"], w=["Pw"])
        DVE(lambda e: e.tensor_tensor(out=Pw[:, 1, 1, :], in0=dz[:, 6, :], in1=dz[:, 7, :], op=ALU.mult), r=["dz6", "dz7", "Pw"], w=["Pw"])

        def cmul(o_re, o_im, a_re, a_im, b_re, b_im, res):
            DVE(lambda e: e.tensor_tensor(out=dz[:, 9, :], in0=a_re, in1=b_re, op=ALU.mult), r=res, w=["dz9"])
            DVE(lambda e: e.tensor_tensor(out=dz[:, 10, :], in0=a_im, in1=b_im, op=ALU.mult), r=res, w=["dz10"])
            DVE(lambda e: e.tensor_tensor(out=o_re, in0=dz[:, 9, :], in1=dz[:, 10, :], op=ALU.subtract), r=["dz9", "dz10"] + res, w=res)
            DVE(lambda e: e.tensor_tensor(out=dz[:, 9, :], in0=a_re, in1=b_im, op=ALU.mult), r=res, w=["dz9"])
            DVE(lambda e: e.tensor_tensor(out=dz[:, 10, :], in0=a_im, in1=b_re, op=ALU.mult), r=res, w=["dz10"])
            DVE(lambda e: e.tensor_tensor(out=o_im, in0=dz[:, 9, :], in1=dz[:, 10, :], op=ALU.add), r=["dz9", "dz10"] + res, w=res)

        cur, oth = 1, 2
        for _ in range(5):
            cmul(Pw[:, oth, 0, :], Pw[:, oth, 1, :], Pw[:, cur, 0, :], Pw[:, cur, 1, :], Pw[:, cur, 0, :], Pw[:, cur, 1, :], ["Pw"])
            cur, oth = oth, cur
        DVE(lambda e: e.tensor_copy(out=Pw[:, 1], in_=Pw[:, 2]), r=["Pw"], w=["Pw"])
        DVE(lambda e: e.memset(Pw[:, 0, 0, :], 1.0), r=["Pw"], w=["Pw"])
        DVE(lambda e: e.memset(Pw[:, 0, 1, :], 0.0), r=["Pw"], w=["Pw"])
        for k in range(2, 9):
            cmul(Pw[:, k, 0, :], Pw[:, k, 1, :], Pw[:, k - 1, 0, :], Pw[:, k - 1, 1, :], Pw[:, 1, 0, :], Pw[:, 1, 1, :], ["Pw"])
        DVE(lambda e: e.tensor_copy(out=A8[:], in_=Pw[:, 8]), r=["Pw"], w=["A8"])
        DVE(lambda e: e.memset(P8t[:, 0, 0, :], 1.0), w=["P8t"])
        DVE(lambda e: e.memset(P8t[:, 1, 0, :], 0.0), r=["P8t"], w=["P8t"])
        DVE(lambda e: e.tensor_copy(out=P8t[:, :, 1, :], in_=A8[:]), r=["A8", "P8t"], w=["P8t"])
        for i in range(2, 8):
            cmul(P8t[:, 0, i, :], P8t[:, 1, i, :], P8t[:, 0, i - 1, :], P8t[:, 1, i - 1, :], A8[:, 0, :], A8[:, 1, :], ["P8t", "A8"])
        cmul(A64[:, 0, :], A64[:, 1, :], P8t[:, 0, 7, :], P8t[:, 1, 7, :], A8[:, 0, :], A8[:, 1, :], ["P8t", "A8", "A64"])
        DVE(lambda e: e.tensor_scalar(out=dz[:, 2, :], in0=Pw[:, 1, 0, :], scalar1=-1.0, scalar2=None, op0=ALU.add), r=["Pw", "dz2", "dz3"], w=["dz2"])
        DVE(lambda e: e.tensor_tensor(out=dz[:, 3, :], in0=dz[:, 0, :], in1=dz[:, 0, :], op=ALU.mult), r=["dz0", "dz3"], w=["dz3"])
        DVE(lambda e: e.tensor_tensor(out=dz[:, 4, :], in0=dz[:, 1, :], in1=dz[:, 1, :], op=ALU.mult), r=["dz1", "dz4", "dz6"], w=["dz4"])
        DVE(lambda e: e.tensor_tensor(out=dz[:, 3, :], in0=dz[:, 3, :], in1=dz[:, 4, :], op=ALU.add), r=["dz3", "dz4"], w=["dz3"])
        DVE(lambda e: e.reciprocal(out=dz[:, 3, :], in_=dz[:, 3, :]), r=["dz3"], w=["dz3"])
        DVE(lambda e: e.tensor_tensor(out=dz[:, 9, :], in0=dz[:, 2, :], in1=dz[:, 0, :], op=ALU.mult), r=["dz2", "dz0", "dz9"], w=["dz9"])
        DVE(lambda e: e.tensor_tensor(out=dz[:, 10, :], in0=Pw[:, 1, 1, :], in1=dz[:, 1, :], op=ALU.mult), r=["Pw", "dz1", "dz10"], w=["dz10"])
        DVE(lambda e: e.tensor_tensor(out=dz[:, 9, :], in0=dz[:, 9, :], in1=dz[:, 10, :], op=ALU.add), r=["dz9", "dz10"], w=["dz9"])
        DVE(lambda e: e.tensor_tensor(out=dz[:, 6, :], in0=dz[:, 9, :], in1=dz[:, 3, :], op=ALU.mult), r=["dz9", "dz3", "dz6", "dz7", "dz8"], w=["dz6"])
        DVE(lambda e: e.tensor_tensor(out=dz[:, 9, :], in0=Pw[:, 1, 1, :], in1=dz[:, 0, :], op=ALU.mult), r=["Pw", "dz0", "dz9", "dz6"], w=["dz9"])
        DVE(lambda e: e.tensor_tensor(out=dz[:, 10, :], in0=dz[:, 2, :], in1=dz[:, 1, :], op=ALU.mult), r=["dz2", "dz1", "dz10"], w=["dz10"])
        DVE(lambda e: e.tensor_tensor(out=dz[:, 9, :], in0=dz[:, 9, :], in1=dz[:, 10, :], op=ALU.subtract), r=["dz9", "dz10"], w=["dz9"])
        DVE(lambda e: e.tensor_tensor(out=dz[:, 7, :], in0=dz[:, 9, :], in1=dz[:, 3, :], op=ALU.mult), r=["dz9", "dz3", "dz7"], w=["dz7"])

        def bc16(ap2):
            return ap2.unsqueeze(2).broadcast_to([128, 32, 16])

        DVE(lambda e: e.tensor_tensor(out=Bb[:, 0], in0=Bn[:, 0], in1=bc16(dz[:, 6, :]), op=ALU.mult), r=["Bn", "dz6"], w=["Bb"])
        DVE(lambda e: e.tensor_tensor(out=Bb[:, 1], in0=Bn[:, 1], in1=bc16(dz[:, 7, :]), op=ALU.mult), r=["Bn", "dz7", "Bb"], w=["Bb"])
        DVE(lambda e: e.tensor_tensor(out=Bb[:, 0], in0=Bb[:, 0], in1=Bb[:, 1], op=ALU.subtract), r=["Bb"], w=["Bb"])
        DVE(lambda e: e.tensor_tensor(out=Bb[:, 1], in0=Bn[:, 1], in1=bc16(dz[:, 6, :]), op=ALU.mult), r=["Bn", "dz6", "Bb"], w=["Bb"])
        DVE(lambda e: e.tensor_tensor(out=Bn[:, 0], in0=Bn[:, 0], in1=bc16(dz[:, 7, :]), op=ALU.mult), r=["Bn", "dz7"], w=["Bn"])
        DVE(lambda e: e.tensor_tensor(out=Bb[:, 1], in0=Bb[:, 1], in1=Bn[:, 0], op=ALU.add), r=["Bb", "Bn"], w=["Bb"])

        for ri in range(2):
            ACT(lambda e, ri=ri: e.copy(out=ZC[:, ri, :, 0:64], in_=Cn[:, ri]), r=["Cn"], w=["ZC"])
            ACT(lambda e, ri=ri: e.copy(out=ZC[:, ri, :, 64:128], in_=Cn[:, ri]), r=["Cn", "ZC"], w=["ZC"])
        for ri in range(2):
            for half in range(2):
                pt_, ptn = nt()
                for g in range(16):
                    gh = half * 16 + g
                    PE(lambda e, ri=ri, gh=gh, g=g, pt_=pt_: e.transpose(out=pt_[:, g * 32:(g + 1) * 32], in_=ZC[:, ri, gh, :], identity=ident[0:32, 0:32]), r=["ZC", "ident"], w=[ptn])
                DVE(lambda e, ri=ri, half=half, pt_=pt_: e.tensor_tensor(
                    out=CT[:, ri, half * 16:(half + 1) * 16, :], in0=pt_[:, 0:512].rearrange("p (g c) -> p g c", c=32),
                    in1=mblk[:].unsqueeze(1).broadcast_to([128, 16, 32]), op=ALU.mult), r=[ptn, "mblk"], w=["CT"])
        ACT(lambda e: e.copy(out=CTb[:, 0], in_=CT[:, 0]), r=["CT"], w=["CTb"])
        DVE(lambda e: e.tensor_scalar(out=CTb[:, 1], in0=CT[:, 1], scalar1=-1.0, scalar2=None, op0=ALU.mult), r=["CT", "CTb"], w=["CTb"])

        for ct in range(8):
            gsl = slice(4 * ct, 4 * ct + 4)
            Ztv = Zt.rearrange("p (k r) c -> p k r c", r=2)
            for k4 in range(2):
                ksl = slice(4 * k4, 4 * k4 + 4)
                pr = Pw[:, ksl, 0, gsl].unsqueeze(3).broadcast_to([128, 4, 4, 16])
                pi = Pw[:, ksl, 1, gsl].unsqueeze(3).broadcast_to([128, 4, 4, 16])
                bre = Bb[:, 0, gsl, :].unsqueeze(1).broadcast_to([128, 4, 4, 16])
                bim = Bb[:, 1, gsl, :].unsqueeze(1).broadcast_to([128, 4, 4, 16])
                DVE(lambda e, pr=pr, bre=bre: e.tensor_tensor(out=pbt4[:, 0], in0=bre, in1=pr, op=ALU.mult), r=["Bb", "Pw"], w=["pbt"])
                DVE(lambda e, pi=pi, bim=bim: e.tensor_tensor(out=pbt4[:, 1], in0=bim, in1=pi, op=ALU.mult), r=["Bb", "Pw", "pbt"], w=["pbt"])
                DVE(lambda e, pi=pi, bre=bre: e.tensor_tensor(out=pbt4[:, 2], in0=bre, in1=pi, op=ALU.mult), r=["Bb", "Pw", "pbt"], w=["pbt"])
                DVE(lambda e, pr=pr, bim=bim: e.tensor_tensor(out=pbt4[:, 3], in0=bim, in1=pr, op=ALU.mult), r=["Bb", "Pw", "pbt"], w=["pbt"])
                for gl in range(2):
                    ps_ = slice(64 * gl, 64 * gl + 64)
                    zre = Ztv[ps_, ksl, 0, :].rearrange("p k (q g c) -> p k q g c", q=4, g=2)[:, :, :, gl, :]
                    zim = Ztv[ps_, ksl, 1, :].rearrange("p k (q g c) -> p k q g c", q=4, g=2)[:, :, :, gl, :]
                    DVE(lambda e, ps_=ps_, zre=zre: e.tensor_tensor(out=zre, in0=pbt4[ps_, 0], in1=pbt4[ps_, 1], op=ALU.subtract), r=["pbt", "Zt"], w=["Zt"])
                    DVE(lambda e, ps_=ps_, zim=zim: e.tensor_tensor(out=zim, in0=pbt4[ps_, 2], in1=pbt4[ps_, 3], op=ALU.add), r=["pbt", "Zt"], w=["Zt"])
            wsi = state["ws"] % 2
            state["ws"] += 1
            for h8 in range(2):
                pt_, ptn = nt()
                for j in range(8):
                    kri = h8 * 8 + j
                    PE(lambda e, kri=kri, j=j, pt_=pt_: e.transpose(out=pt_[:, j * 128:(j + 1) * 128], in_=Zt[:, kri, :], identity=ident[:]), r=["Zt", "ident"], w=[ptn])
                ACT(lambda e, h8=h8, pt_=pt_, wsi=wsi: e.copy(out=wsb[wsi][:, h8 * 8:(h8 + 1) * 8, :].rearrange("p a b -> p (a b)"), in_=pt_[:, :]), r=[ptn], w=["wsb%d" % wsi])
            DMA("sp", "wsd%d" % wsi, lambda e, ct=ct, wsi=wsi: e.dma_start(out=WS_d[ct], in_=wsb[wsi][:].rearrange("p a b -> p (a b)")), r=["wsb%d" % wsi], w=["WS_d"])
            kti = state["kt"] % 2
            state["kt"] += 1
            for tau in range(8):
                bk, bkn = nb()
                PE(lambda e, tau=tau, bk=bk, gsl=gsl: e.matmul(bk[:, 0:128], lhsT=Zt[:, 2 * tau, :], rhs=CTb[:, 0, gsl, :].rearrange("p a b -> p (a b)"), start=True, stop=False), r=["Zt", "CTb"], w=[bkn])
                PE(lambda e, tau=tau, bk=bk, gsl=gsl: e.matmul(bk[:, 0:128], lhsT=Zt[:, 2 * tau + 1, :], rhs=CTb[:, 1, gsl, :].rearrange("p a b -> p (a b)"), start=False, stop=True), r=["Zt", "CTb"], w=[bkn])
                if tau == 0:
                    DVE(lambda e, bk=bk: e.tensor_tensor(out=ktf[:], in0=bk[:, 0:128], in1=BD[:], op=ALU.mult), r=[bkn, "BD"], w=["ktf"])
                    DVE(lambda e, ct=ct, kti=kti: e.scalar_tensor_tensor(out=ktb[kti][:, 0, :], in0=identf[:], scalar=dsk[:, ct:ct + 1], in1=ktf[:], op0=ALU.mult, op1=ALU.add), r=["ktf", "identf", "dsk"], w=["ktb%d" % kti])
                else:
                    DVE(lambda e, bk=bk, tau=tau, kti=kti: e.tensor_tensor(out=ktb[kti][:, tau, :], in0=bk[:, 0:128], in1=BD[:], op=ALU.mult), r=[bkn, "BD"], w=["ktb%d" % kti])
            DMA("sp", "ktd%d" % kti, lambda e, ct=ct, kti=kti: e.dma_start(out=YK_d[ct, :, 2048:3072], in_=ktb[kti].rearrange("p a b -> p (a b)")), r=["ktb%d" % kti, "ykb%d" % kti], w=["KT_d"])
            wyi = state["wy"] % 2
            state["wy"] += 1
            wyv = wyb[wyi].rearrange("p q (r i) c -> p q r i c", i=2)
            for r2 in range(4):
                rsl = slice(2 * r2, 2 * r2 + 2)
                ksl = slice(2 * r2 + 1, 2 * r2 + 3)
                pr = Pw[:, ksl, 0, gsl].unsqueeze(3).broadcast_to([128, 2, 4, 32])
                pi = Pw[:, ksl, 1, gsl].unsqueeze(3).broadcast_to([128, 2, 4, 32])
                cre = CT[:, 0, gsl, :].unsqueeze(1).broadcast_to([128, 2, 4, 32])
                cim = CT[:, 1, gsl, :].unsqueeze(1).broadcast_to([128, 2, 4, 32])
                DVE(lambda e, pr=pr, cre=cre: e.tensor_tensor(out=wyt2[:, 0], in0=cre, in1=pr, op=ALU.mult), r=["CT", "Pw"], w=["wyt"])
                DVE(lambda e, pi=pi, cim=cim: e.tensor_tensor(out=wyt2[:, 1], in0=cim, in1=pi, op=ALU.mult), r=["CT", "Pw", "wyt"], w=["wyt"])
                DVE(lambda e, pi=pi, cre=cre: e.tensor_tensor(out=wyt2[:, 2], in0=cre, in1=pi, op=ALU.mult), r=["CT", "Pw", "wyt"], w=["wyt"])
                DVE(lambda e, pr=pr, cim=cim: e.tensor_tensor(out=wyt2[:, 3], in0=cim, in1=pr, op=ALU.mult), r=["CT", "Pw", "wyt"], w=["wyt"])
                o_re = wyv[:, :, rsl, 0, :].rearrange("p q r c -> p r q c")
                o_im = wyv[:, :, rsl, 1, :].rearrange("p q r c -> p r q c")
                DVE(lambda e, o_re=o_re: e.tensor_tensor(out=o_re, in0=wyt2[:, 0], in1=wyt2[:, 1], op=ALU.subtract), r=["wyt"], w=["wyb%d" % wyi])
                DVE(lambda e: e.tensor_tensor(out=wyt2[:, 2], in0=wyt2[:, 2], in1=wyt2[:, 3], op=ALU.add), r=["wyt"], w=["wyt"])
                DVE(lambda e, o_im=o_im: e.tensor_scalar(out=o_im, in0=wyt2[:, 2], scalar1=-1.0, scalar2=None, op0=ALU.mult), r=["wyt"], w=["wyb%d" % wyi])
            DMA("sp", "wyd%d" % wyi, lambda e, ct=ct, wyi=wyi: e.dma_start(out=YK_d[ct, :, 0:2048], in_=wyb[wyi].rearrange("p a b c -> p (a b c)")), r=["wyb%d" % wyi, "ykb%d" % wyi], w=["WY_d"])

        def rstd_act(x_ap, scale, y_ap, t_ap, res):
            ACT(lambda e: e.activation(out=t_ap, in_=x_ap, func=AF.Ln, scale=scale, bias=cst[:, 0:1]), r=res + ["cst0"], w=res)
            ACT(lambda e: e.activation(out=y_ap, in_=t_ap, func=AF.Exp, scale=-0.5), r=res, w=res)

        def norm_to_T(gT, gname, ntt):
            S.phase = "norm"
            DVE(lambda e: e.memset(st[:, 1, :], 0.0), r=["st"], w=["st"])
            for tt in range(ntt):
                ACT(lambda e, tt=tt: e.activation(out=xnb[:], in_=xh[:, tt, :], func=AF.Square, accum_out=st[:, 1, tt:tt + 1]), r=["xh"], w=["xnb", "st"])
            rstd_act(st[:, 1, 0:ntt], 1.0 / D, st[:, 3, 0:ntt], st[:, 4, 0:ntt], ["st"])
            for tt in range(ntt):
                DVE(lambda e, tt=tt: e.tensor_scalar(out=xnb[:], in0=xh[:, tt, :], scalar1=st[:, 3, tt:tt + 1], scalar2=None, op0=ALU.mult), r=["xh", "st"], w=["xnb"])
                for h8 in range(2):
                    pt_, ptn = nt()
                    for j in range(8):
                        k = h8 * 8 + j
                        PE(lambda e, k=k, j=j, pt_=pt_: e.transpose(out=pt_[:, j * 128:(j + 1) * 128], in_=xnb[:, k * 128:(k + 1) * 128], identity=ident[:]), r=["xnb", "ident"], w=[ptn])
                    DVE(lambda e, tt=tt, h8=h8, pt_=pt_: e.tensor_tensor(
                        out=nT[:, h8 * 8:(h8 + 1) * 8, tt * 128:(tt + 1) * 128], in0=pt_[:, :].rearrange("p (k t) -> p k t", t=128),
                        in1=gT[:, h8 * 8:(h8 + 1) * 8].unsqueeze(2).broadcast_to([128, 8, 128]), op=ALU.mult), r=[ptn, gname], w=["nT"])

        def u_proj(Tp):
            S.phase = "uproj"
            for pn in range(2):
                wv, wn = load_panel(w_in, 0, 16, pn * 512, 512)
                for m in range(4):
                    bk, bkn = nb()
                    for k in range(16):
                        PE(lambda e, k=k, m=m, bk=bk, wv=wv: e.matmul(bk[:, 0:Tp], lhsT=wv[:, k, m * 128:(m + 1) * 128], rhs=nT[:, k, 0:Tp], start=(k == 0), stop=(k == 15)), r=["nT"] + wn, w=[bkn])
                    ct = pn * 4 + m
                    ACT(lambda e, ct=ct, bk=bk: e.copy(out=uT[:, ct, 0:Tp], in_=bk[:, 0:Tp]), r=[bkn], w=["uT"])

        def ssm_S(c0, deep=False, lay="ajg"):
            S.phase = "ssmS"
            slots = ws_slots if deep else ws_slots_ctx
            for ct in range(8):
                si = state["ws2"] % 4
                state["ws2"] += 1
                wt, wtn = slots[si]
                DMA("sp", "wsl_" + wtn, lambda e, ct=ct, wt=wt: e.dma_start(out=wt.rearrange("p a b -> p (a b)"), in_=WS_d[ct]), r=["WS_d"], w=[wtn])
                ui = state["u"] % 2
                state["u"] += 1
                u4, u4n = um4s[ui], "um4_%d" % ui
                DVE(lambda e, ct=ct, u4=u4: e.tensor_tensor(out=u4[:].rearrange("p q (r j) -> p q r j", r=8),
                                                         in0=uT[:, ct, c0:c0 + TS].rearrange("p (j r) -> p r j", r=8).unsqueeze(1).broadcast_to([128, 4, 8, NCOL]),
                                                         in1=rmask[:, :].unsqueeze(2).unsqueeze(3).broadcast_to([128, 4, 8, NCOL]), op=ALU.mult), r=["uT", "rmask"], w=[u4n])
                bk, bkn = nb()
                for ri in range(2):
                    for r_ in range(8):
                        kri = (7 - r_) * 2 + ri
                        PE(lambda e, ri=ri, r_=r_, kri=kri, bk=bk, wt=wt, u4=u4: e.matmul(
                            bk[:, ri * 4 * NCOL:(ri + 1) * 4 * NCOL], lhsT=wt[:, kri, :],
                            rhs=u4[:].rearrange("p q (r j) -> p q r j", r=8)[:, :, r_, :],
                            start=(r_ == 0), stop=(r_ == 7)), r=[u4n, wtn], w=[bkn])
                if lay == "agj":
                    so = Ssb[:, :, 4 * ct:4 * ct + 4, :]
                else:
                    so = Ssb[:].rearrange("p a g j -> p (a g j)").rearrange("p (a j g) -> p a j g", a=2, g=32)[:, :, :, 4 * ct:4 * ct + 4].rearrange("p a j q -> p a q j")
                ACT(lambda e, ct=ct, bk=bk, so=so: e.copy(out=so, in_=bk[:, 0:8 * NCOL].rearrange("p (a q j) -> p a q j", a=2, q=4)), r=[bkn], w=["Ssb"])

        def ssm_scan(nseq, J):
            S.phase = "scan"
            hv = Hf[:, 0:2 * 32 * nseq * (J + 1)].rearrange("p (a g s j) -> p a g s j", a=2, g=32, s=nseq)
            sv = Ssb[:].rearrange("p a g (s j) -> p a g s j", s=nseq)
            a8r = A8[:, 0, :].unsqueeze(1).unsqueeze(3).broadcast_to([128, 2, 32, nseq])
            a8i = A8[:, 1, :].unsqueeze(1).unsqueeze(3).broadcast_to([128, 2, 32, nseq])
            ta = sc[:, 0:2, 0:32 * nseq].rearrange("p a (g s) -> p a g s", s=nseq)
            tb = sc[:, 2:4, 0:32 * nseq].rearrange("p a (g s) -> p a g s", s=nseq)
            for j in range(J):
                hj = hv[:, :, :, :, j]
                DVE(lambda e, hj=hj: e.tensor_tensor(out=ta, in0=hj, in1=a8r, op=ALU.mult), r=["Hf", "A8"], w=["sc0"])
                DVE(lambda e, hj=hj: e.tensor_tensor(out=tb, in0=hj, in1=a8i, op=ALU.mult), r=["Hf", "A8"], w=["sc1"])
                DVE(lambda e: e.tensor_tensor(out=ta[:, 0], in0=ta[:, 0], in1=tb[:, 1], op=ALU.subtract), r=["sc0", "sc1"], w=["sc0"])
                DVE(lambda e: e.tensor_tensor(out=ta[:, 1], in0=ta[:, 1], in1=tb[:, 0], op=ALU.add), r=["sc0", "sc1"], w=["sc0"])
                DVE(lambda e, j=j: e.tensor_tensor(out=hv[:, :, :, :, j + 1], in0=ta, in1=sv[:, :, :, :, j], op=ALU.add), r=["sc0", "Ssb", "Hf"], w=["Hf"])
            return hv

        def ssm_scan_blocked(with_hist):
            S.phase = "scan"
            hv = Hf[:, 0:2 * (NCOL + 1) * 32].rearrange("p (a j g) -> p a j g", a=2, g=32)
            Lv = hv[:, :, 0:NCOL, :].rearrange("p a (b i) g -> p a b i g", i=8)
            Sv = Ssb[:].rearrange("p a g j -> p (a g j)").rearrange("p (a j g) -> p a j g", a=2, g=32).rearrange("p a (b i) g -> p a b i g", i=8)
            LT = hst[:, 0:256].rearrange("p (a b g) -> p a b g", a=2, b=4)
            RS = ["scA"]
            a8r = A8[:, 0, :].unsqueeze(1).unsqueeze(2).broadcast_to([128, 2, 4, 32])
            a8i = A8[:, 1, :].unsqueeze(1).unsqueeze(2).broadcast_to([128, 2, 4, 32])
            scf = sc[:].rearrange("p a b -> p (a b)")
            ta = scf[:, 0:256].rearrange("p (a b g) -> p a b g", a=2, b=4)
            tb = scf[:, 256:512].rearrange("p (a b g) -> p a b g", a=2, b=4)
            DVE(lambda e: e.memset(Lv[:, :, :, 0, :], 0.0), r=["Hf", "Hb"], w=["Hf"])
            DVE(lambda e: e.tensor_copy(out=Lv[:, :, :, 1, :], in_=Sv[:, :, :, 0, :]), r=["Ssb", "Hf"], w=["Hf"])
            for i in range(1, 8):
                X = Lv[:, :, :, i, :]
                o_re = Lv[:, 0, :, i + 1, :] if i < 7 else LT[:, 0]
                o_im = Lv[:, 1, :, i + 1, :] if i < 7 else LT[:, 1]
                orn = ["Hf"] if i < 7 else ["hst"]
                DVE(lambda e, X=X: e.tensor_tensor(out=ta, in0=X, in1=a8r, op=ALU.mult), r=["Hf", "A8"] + RS, w=RS)
                DVE(lambda e, X=X: e.tensor_tensor(out=tb, in0=X, in1=a8i, op=ALU.mult), r=["Hf", "A8"] + RS, w=["scB"])
                DVE(lambda e, i=i: e.tensor_tensor(out=ta, in0=ta, in1=Sv[:, :, :, i, :], op=ALU.add), r=RS + ["Ssb"], w=RS)
                DVE(lambda e, o_re=o_re: e.tensor_tensor(out=o_re, in0=ta[:, 0], in1=tb[:, 1], op=ALU.subtract), r=RS + ["scB"] + orn, w=orn)
                DVE(lambda e, o_im=o_im: e.tensor_tensor(out=o_im, in0=ta[:, 1], in1=tb[:, 0], op=ALU.add), r=RS + ["scB"] + orn, w=orn)
            C = scf[:, 0:320].rearrange("p (a b g) -> p a b g", a=2, b=5)
            ca = scf[:, 320:384].rearrange("p (a g) -> p a g", a=2)
            cb = scf[:, 384:448].rearrange("p (a g) -> p a g", a=2)
            a64r = A64[:, 0, :].unsqueeze(1).broadcast_to([128, 2, 32])
            a64i = A64[:, 1, :].unsqueeze(1).broadcast_to([128, 2, 32])
            DVE(lambda e: e.tensor_copy(out=C[:, :, 0, :], in_=hc[:]), r=["hc", "scB", "hst"] + RS, w=RS)
            for b_ in range(4):
                X = C[:, :, b_, :]
                DVE(lambda e, X=X: e.tensor_tensor(out=ca, in0=X, in1=a64r, op=ALU.mult), r=RS + ["A64"], w=["scC"])
                DVE(lambda e, X=X: e.tensor_tensor(out=cb, in0=X, in1=a64i, op=ALU.mult), r=RS + ["A64"], w=["scD"])
                DVE(lambda e, b_=b_: e.tensor_tensor(out=ca, in0=ca, in1=LT[:, :, b_, :], op=ALU.add), r=["scC", "hst"], w=["scC"])
                DVE(lambda e, b_=b_: e.tensor_tensor(out=C[:, 0, b_ + 1, :], in0=ca[:, 0], in1=cb[:, 1], op=ALU.subtract), r=["scC", "scD"] + RS, w=RS)
                DVE(lambda e, b_=b_: e.tensor_tensor(out=C[:, 1, b_ + 1, :], in0=ca[:, 1], in1=cb[:, 0], op=ALU.add), r=["scC", "scD"] + RS, w=RS)
            DVE(lambda e: e.tensor_copy(out=hc[:], in_=C[:, :, 4, :]), r=RS, w=["hc"])
            if with_hist:
                Ssf_ = Ssb[:].rearrange("p a g j -> p (a g j)")
                T1 = Ssf_[:, 0:1024].rearrange("p (b i g) -> p b i g", b=4, i=8)
                T2 = Ssf_[:, 1024:2048].rearrange("p (b i g) -> p b i g", b=4, i=8)
                Cr = C[:, 0, 0:4, :].unsqueeze(2).broadcast_to([128, 4, 8, 32])
                Ci = C[:, 1, 0:4, :].unsqueeze(2).broadcast_to([128, 4, 8, 32])
                Pr = P8t[:, 0, :, :].unsqueeze(1).broadcast_to([128, 4, 8, 32])
                Pi = P8t[:, 1, :, :].unsqueeze(1).broadcast_to([128, 4, 8, 32])
                Lr = hv[:, 0, 0:NCOL, :].rearrange("p (b i) g -> p b i g", i=8)
                Li = hv[:, 1, 0:NCOL, :].rearrange("p (b i) g -> p b i g", i=8)
                DVE(lambda e: e.tensor_tensor(out=T1, in0=Pr, in1=Cr, op=ALU.mult), r=RS + ["P8t", "Ssb", "Hf"], w=["Ssb"])
                DVE(lambda e: e.tensor_tensor(out=T2, in0=Pi, in1=Ci, op=ALU.mult), r=RS + ["P8t", "Ssb"], w=["Ssb"])
                DVE(lambda e: e.tensor_tensor(out=Lr, in0=Lr, in1=T1, op=ALU.add), r=["Ssb", "Hf"], w=["Hf"])
                DVE(lambda e: e.tensor_tensor(out=Lr, in0=Lr, in1=T2, op=ALU.subtract), r=["Ssb", "Hf"], w=["Hf"])
                DVE(lambda e: e.tensor_tensor(out=T1, in0=Pr, in1=Ci, op=ALU.mult), r=RS + ["P8t", "Ssb", "Hf"], w=["Ssb"])
                DVE(lambda e: e.tensor_tensor(out=T2, in0=Pi, in1=Cr, op=ALU.mult), r=RS + ["P8t", "Ssb"], w=["Ssb"])
                DVE(lambda e: e.tensor_tensor(out=Li, in0=Li, in1=T1, op=ALU.add), r=["Ssb", "Hf"], w=["Hf"])
                DVE(lambda e: e.tensor_tensor(out=Li, in0=Li, in1=T2, op=ALU.add), r=["Ssb", "Hf"], w=["Hf"])
                for ri in range(2):
                    ACT(lambda e, ri=ri: e.copy(out=Hb[:, ri], in_=hv[:, ri, 0:NCOL, :].rearrange("p j g -> p g j")), r=["Hf"], w=["Hb"])

        def ssm_y(c0, deep=False):
            S.phase = "ssmY"
            nsl = 4 if deep else 2
            for ct in range(8):
                si = state["yk2"] % nsl
                state["yk2"] += 1
                yt, ytn = yk_slots[si]
                wy = yt[:, 0:2048].rearrange("p (a b c) -> p a b c", a=4, b=16)
                kt = yt[:, 2048:3072].rearrange("p (a b) -> p a b", a=8)
                DMA("sp", "ykl_" + ytn, lambda e, ct=ct, yt=yt: e.dma_start(out=yt, in_=YK_d[ct]), r=["WY_d", "KT_d"], w=[ytn])
                bk, bkn = nb()
                PE(lambda e, ct=ct, bk=bk, kt=kt: e.matmul(bk[:, 0:TS], lhsT=kt[:, 0, :], rhs=uT[:, ct, c0:c0 + TS], start=True, stop=False, skip_group_check=True), r=["uT", ytn], w=[bkn])
                DVE(lambda e, ct=ct: e.tensor_tensor(out=um[:].rearrange("p a (j r) -> p a j r", r=8),
                                                  in0=uT[:, ct, c0:c0 + TS].rearrange("p (j r) -> p j r", r=8).unsqueeze(1).broadcast_to([128, 7, NCOL, 8]),
                                                  in1=maskl[:].unsqueeze(2).broadcast_to([128, 7, NCOL, 8]), op=ALU.mult), r=["uT", "maskl"], w=["um"])
                for q in range(4):
                    for r_ in range(8):
                        for ri in range(2):
                            PE(lambda e, ct=ct, bk=bk, wy=wy, q=q, r_=r_, ri=ri: e.matmul(
                                bk[32 * q:32 * q + 32, 0:TS].rearrange("p (j r) -> p j r", r=8)[:, :, r_], lhsT=wy[:, q, 2 * r_ + ri, :],
                                rhs=Hb[:, ri, 4 * ct + q, :], start=False, stop=False, skip_group_check=True, tile_position=(0, 32 * q)), r=["Hb", ytn], w=[bkn])
                for tau in range(1, 8):
                    PE(lambda e, ct=ct, bk=bk, kt=kt, tau=tau: e.matmul(
                        bk[:, tau:TS], lhsT=kt[:, tau, :], rhs=um[:, tau - 1, 0:TS - tau], start=False, stop=(tau == 7), skip_group_check=True), r=["um", ytn], w=[bkn])
                ACT(lambda e, ct=ct, bk=bk: e.activation(out=geluT[:, ct, c0:c0 + TS], in_=bk[:, 0:TS], func=AF.Gelu_apprx_tanh), r=[bkn], w=["geluT"])

        def glu(Tp):
            S.phase = "glu"
            for pn in range(2):
                wv, wn = load_panel(w_glu, 0, 8, pn * 512, 512)
                for m in range(4):
                    bk, bkn = nb()
                    for k in range(8):
                        PE(lambda e, k=k, m=m, bk=bk, wv=wv: e.matmul(bk[:, 0:Tp], lhsT=wv[:, k, m * 128:(m + 1) * 128], rhs=geluT[:, k, 0:Tp], start=(k == 0), stop=(k == 7)), r=["geluT"] + wn, w=[bkn])
                    co = pn * 4 + m
                    ACT(lambda e, bk=bk: e.activation(out=t2[:, 0:Tp], in_=bk[:, 0:Tp], func=AF.Tanh, scale=0.5), r=[bkn], w=["t2"])
                    DVE(lambda e, co=co: e.scalar_tensor_tensor(out=uT[:, co, 0:Tp], in0=t2[:, 0:Tp], scalar=1.0, in1=geluT[:, co, 0:Tp], op0=ALU.add, op1=ALU.mult), r=["t2", "geluT"], w=["uT"])

        def qk_norm_rope(h0, nh, is_k, rtile, qf, qfn, qkb, qkbn):
            v3 = qf[:, h0 * 64:(h0 + nh) * 64].rearrange("p (h d) -> p h d", d=64)
            t3 = tmpq[:, 0:nh * 64].rearrange("p (h d) -> p h d", d=64)
            DVE(lambda e: e.tensor_tensor(out=t3, in0=v3, in1=v3, op=ALU.mult), r=[qfn], w=["tmpq"])
            DVE(lambda e: e.tensor_reduce(out=st[:, 5, 0:nh], in_=t3, axis=AX.X, op=ALU.add), r=["tmpq"], w=["st"])
            rstd_act(st[:, 5, 0:nh], 1.0 / 64, st[:, 6, 0:nh], st[:, 7, 0:nh], ["st"])
            DVE(lambda e: e.tensor_tensor(out=v3, in0=v3, in1=st[:, 6, 0:nh].unsqueeze(2).broadcast_to([128, nh, 64]), op=ALU.mult), r=[qfn, "st"], w=[qfn])
            gn = knb if is_k else qnb
            gname = "knb" if is_k else "qnb"
            DVE(lambda e: e.tensor_tensor(out=v3, in0=v3, in1=gn[:].unsqueeze(1).broadcast_to([128, nh, 64]), op=ALU.mult), r=[qfn, gname], w=[qfn])
            x1 = v3[:, :, 0:8]
            x2 = v3[:, :, 8:16]
            x12 = v3[:, :, 0:16].rearrange("p h (a d) -> p h a d", a=2)
            cs = ropeT[:, rtile, 0:8].unsqueeze(1).unsqueeze(2).broadcast_to([128, nh, 2, 8])
            sn = ropeT[:, rtile, 8:16].unsqueeze(1).unsqueeze(2).broadcast_to([128, nh, 2, 8])
            rc = rt[:, 0:2, 0:nh * 8].rearrange("p a (h d) -> p h a d", d=8)
            rs = rt[:, 2:4, 0:nh * 8].rearrange("p a (h d) -> p h a d", d=8)
            DVE(lambda e: e.tensor_tensor(out=rc, in0=x12, in1=cs, op=ALU.mult), r=[qfn, "ropeT"], w=["rt"])
            DVE(lambda e: e.tensor_tensor(out=rs, in0=x12, in1=sn, op=ALU.mult), r=[qfn, "ropeT", "rt"], w=["rt"])
            DVE(lambda e: e.tensor_tensor(out=x1, in0=rc[:, :, 0, :], in1=rs[:, :, 1, :], op=ALU.subtract), r=["rt", qfn], w=[qfn])
            DVE(lambda e: e.tensor_tensor(out=x2, in0=rc[:, :, 1, :], in1=rs[:, :, 0, :], op=ALU.add), r=["rt", qfn], w=[qfn])
            ACT(lambda e: e.copy(out=qkb[:, h0 * 64:(h0 + nh) * 64], in_=qf[:, h0 * 64:(h0 + nh) * 64]), r=[qfn], w=[qkbn])

        def qkv_stage(tts, panels, rtile0, kcol0, vtile0, out_fn=None):
            S.phase = "qkv"
            pending = [None]

            def flush():
                if pending[0] is not None:
                    pending[0]()
                    pending[0] = None

            for pn in panels:
                wv, wn = load_panel(w_in, 0, 16, 1024 + pn * 512, 512)
                for tt in tts:
                    qi = state["q"] % 2
                    state["q"] += 1
                    qf, qfn, qkb, qkbn = qfs[qi], "qf%d" % qi, qkbs[qi], "qkb%d" % qi
                    bk, bkn = nb()
                    for k in range(16):
                        PE(lambda e, k=k, tt=tt, bk=bk, wv=wv: e.matmul(bk[:, :], lhsT=nT[:, k, tt * 128:(tt + 1) * 128], rhs=wv[:, k, :], start=(k == 0), stop=(k == 15)), r=["nT"] + wn, w=[bkn])
                    flush()
                    ACT(lambda e, bk=bk, qf=qf: e.copy(out=qf[:], in_=bk[:, :]), r=[bkn], w=[qfn])
                    if pn < 2:
                        qk_norm_rope(0, 8, False, rtile0 + tt, qf, qfn, qkb, qkbn)

                        def tail(tt=tt, pn=pn, qkb=qkb, qkbn=qkbn):
                            pt_, ptn = nt()
                            for j in range(8):
                                PE(lambda e, j=j, pt_=pt_, qkb=qkb: e.transpose(out=pt_[0:64, j * 128:(j + 1) * 128], in_=qkb[:, j * 64:(j + 1) * 64], identity=ident[:]), r=[qkbn, "ident"], w=[ptn])
                            ACT(lambda e, tt=tt, pn=pn, pt_=pt_: e.copy(out=qT[:, 2 * tt:2 * tt + 2, pn * 8:(pn + 1) * 8, :].rearrange("p c h q -> p h c q"),
                                                                 in_=pt_[0:64, :].rearrange("p (h c q) -> p h c q", h=8, c=2)), r=[ptn], w=["qT"])
                        pending[0] = tail
                    else:
                        qk_norm_rope(0, 4, True, rtile0 + tt, qf, qfn, qkb, qkbn)
                        ACT(lambda e, tt=tt, qf=qf: e.copy(out=vb[:, vtile0 + tt, :], in_=qf[:, 256:512]), r=[qfn], w=["vb"])
                        if out_fn is not None:
                            out_fn(tt, qf, qfn)

                        def tail(tt=tt, qkb=qkb, qkbn=qkbn):
                            pt_, ptn = nt()
                            for g in range(4):
                                PE(lambda e, g=g, pt_=pt_, qkb=qkb: e.transpose(out=pt_[0:64, g * 128:(g + 1) * 128], in_=qkb[:, g * 64:(g + 1) * 64], identity=ident[:]), r=[qkbn, "ident"], w=[ptn])
                            ACT(lambda e, tt=tt, pt_=pt_: e.copy(out=kT[:, :, kcol0 + tt * 128:kcol0 + (tt + 1) * 128], in_=pt_[0:64, 0:512].rearrange("p (g t) -> p g t", g=4)), r=[ptn], w=["kT"])
                        pending[0] = tail
            flush()

        def attention_part1(c, g, kA, biasA, kB, biasB):
            S.phase = "attn"
            ai = state["a"] % 2
            state["a"] += 1
            PT, PTn = PTs[ai], "PT%d" % ai
            bk, bkn = nb()
            qv = qT[:, c, 4 * g:4 * g + 4, :].rearrange("p h q -> p (h q)")
            PE(lambda e: e.matmul(bk[:, 0:256], lhsT=kA(g), rhs=qv, start=True, stop=True), r=["qT", "kT", "tA0", "tA1", "tA2", "tA3"], w=[bkn])
            PE(lambda e: e.matmul(bk[:, 256:512], lhsT=kB(g), rhs=qv, start=True, stop=True), r=["qT", "kT"], w=[bkn])
            ACT(lambda e: e.activation(out=PT[:, 0, :], in_=bk[:, 0:256], func=AF.Exp, scale=0.125, bias=bias5[:, biasA:biasA + 1]), r=[bkn, "bias5"], w=[PTn])
            ACT(lambda e: e.activation(out=PT[:, 1, :], in_=bk[:, 256:512], func=AF.Exp, scale=0.125, bias=bias5[:, biasB:biasB + 1]), r=[bkn, "bias5", PTn], w=[PTn])
            return ai

        def attention_part2(c, g, ai, vA, vB):
            PT, PTn, dtmp, dtn = PTs[ai], "PT%d" % ai, dtmps[ai], "dtmp%d" % ai
            b2, b2n = nb()
            for ph in range(2):
                for X, vf in ((0, vA), (1, vB)):
                    rhs = PT[:, X, :].rearrange("p (i a q) -> p i a q", i=2, a=2)[:, :, ph, :]
                    PE(lambda e, ph=ph, X=X, vf=vf, rhs=rhs: e.matmul(b2[64 * ph:64 * ph + 64, 0:128], lhsT=vf(g), rhs=rhs, start=(X == 0), stop=(X == 1), tile_position=(0, 64 * ph)), r=[PTn, "vb", "mixp"], w=[b2n])
                for X in (0, 1):
                    rhs = PT[:, X, :].rearrange("p (i a q) -> p i a q", i=2, a=2)[:, :, ph, :]
                    PE(lambda e, ph=ph, X=X, rhs=rhs: e.matmul(b2[64 * ph:64 * ph + 64, 128:256], lhsT=ones_b[:, :], rhs=rhs, start=(X == 0), stop=(X == 1), tile_position=(0, 64 * ph)), r=[PTn, "ones_b"], w=[b2n])
            DVE(lambda e: e.tensor_tensor(out=dtmp[:].rearrange("p (i q) -> p i q", i=2), in0=b2[:, 128:256].rearrange("p (i q) -> p i q", i=2),
                                          in1=skx[:, 2 * g:2 * g + 2].unsqueeze(2).broadcast_to([128, 2, 64]), op=ALU.add), r=[b2n, "skx"], w=[dtn])
            DVE(lambda e: e.reciprocal(out=dtmp[:], in_=dtmp[:]), r=[dtn], w=[dtn])
            DVE(lambda e: e.tensor_tensor(out=oT[:, 2 * g:2 * g + 2, c * 64:(c + 1) * 64], in0=b2[:, 0:128].rearrange("p (i q) -> p i q", i=2),
                                          in1=dtmp[:].rearrange("p (i q) -> p i q", i=2), op=ALU.mult), r=[b2n, dtn], w=["geluT"])

        def mixer_and_h(Tp, ntt):
            S.phase = "mixer"
            for mp in range(4):
                wv, wn = load_panel(w_gate, 0, 16, mp * 512, 512)
                for m in range(4):
                    bk, bkn = nb()
                    for k in range(16):
                        PE(lambda e, k=k, m=m, bk=bk, wv=wv: e.matmul(bk[:, 0:Tp], lhsT=wv[:, k, m * 128:(m + 1) * 128], rhs=nT[:, k, 0:Tp], start=(k == 0), stop=(k == 15)), r=["nT"] + wn, w=[bkn])
                    ACT(lambda e, m=m, bk=bk: e.activation(out=tA[:, m, 0:Tp], in_=bk[:, 0:Tp], func=AF.Tanh, scale=0.5), r=[bkn], w=["tA%d" % m])
                wv, wn = load_panel(w_brs, 0, 8, mp * 512, 512)
                for m in range(4):
                    bk, bkn = nb()
                    for k in range(8):
                        PE(lambda e, k=k, m=m, bk=bk, wv=wv: e.matmul(bk[:, 0:Tp], lhsT=wv[:, k, m * 128:(m + 1) * 128], rhs=uT[:, k, 0:Tp], start=(k == 0), stop=(k == 7)), r=["uT"] + wn, w=[bkn])
                    DVE(lambda e, m=m, bk=bk: e.scalar_tensor_tensor(out=t1[:, m, 0:Tp], in0=tA[:, m, 0:Tp], scalar=1.0, in1=bk[:, 0:Tp], op0=ALU.add, op1=ALU.mult), r=[bkn, "tA%d" % m], w=["t1_%d" % m])
                wv, wn = load_panel(w_gate, 0, 16, D + mp * 512, 512)
                for m in range(4):
                    bk, bkn = nb()
                    for k in range(16):
                        PE(lambda e, k=k, m=m, bk=bk, wv=wv: e.matmul(bk[:, 0:Tp], lhsT=wv[:, k, m * 128:(m + 1) * 128], rhs=nT[:, k, 0:Tp], start=(k == 0), stop=(k == 15)), r=["nT"] + wn, w=[bkn])
                    ACT(lambda e, m=m, bk=bk: e.activation(out=tA[:, m, 0:Tp], in_=bk[:, 0:Tp], func=AF.Tanh, scale=0.5), r=[bkn], w=["tA%d" % m])
                wv, wn = load_panel(w_bra, 0, 8, mp * 512, 512)
                for m in range(4):
                    bk, bkn = nb()
                    for k in range(8):
                        PE(lambda e, k=k, m=m, bk=bk, wv=wv: e.matmul(bk[:, 0:Tp], lhsT=wv[:, k, m * 128:(m + 1) * 128], rhs=oT[:, k, 0:Tp], start=(k == 0), stop=(k == 7)), r=["geluT"] + wn, w=[bkn])
                    DVE(lambda e, m=m, bk=bk: e.scalar_tensor_tensor(out=t2[:, 0:Tp], in0=tA[:, m, 0:Tp], scalar=1.0, in1=bk[:, 0:Tp], op0=ALU.add, op1=ALU.mult), r=[bkn, "tA%d" % m], w=["t2"])
                    DVE(lambda e, m=m: e.scalar_tensor_tensor(out=mixp[:, m, 0:Tp], in0=t2[:, 0:Tp], scalar=2.0, in1=t1[:, m, 0:Tp], op0=ALU.mult, op1=ALU.add), r=["t2", "t1_%d" % m], w=["mixp"])
                wv, wn = load_panel(w_out, mp * 4, 4, 0, D)
                for tt in range(ntt):
                    for nn in range(4):
                        bk, bkn = nb()
                        for k in range(4):
                            PE(lambda e, k=k, tt=tt, nn=nn, bk=bk, wv=wv: e.matmul(bk[:, :], lhsT=mixp[:, k, tt * 128:(tt + 1) * 128], rhs=wv[:, k, nn * 512:(nn + 1) * 512], start=(k == 0), stop=(k == 3)), r=["mixp"] + wn, w=[bkn])
                        DVE(lambda e, tt=tt, nn=nn, bk=bk: e.scalar_tensor_tensor(out=xh[:, tt, nn * 512:(nn + 1) * 512], in0=bk[:, :], scalar=0.25, in1=xh[:, tt, nn * 512:(nn + 1) * 512], op0=ALU.mult, op1=ALU.add), r=[bkn, "xh"], w=["xh"])

        def ffn(Tp, ntt):
            S.phase = "ffn"
            for fp in range(11):
                wv, wn = load_panel(w_fg, 0, 16, fp * 512, 512)
                for m in range(4):
                    bk, bkn = nb()
                    for k in range(16):
                        PE(lambda e, k=k, m=m, bk=bk, wv=wv: e.matmul(bk[:, 0:Tp], lhsT=wv[:, k, m * 128:(m + 1) * 128], rhs=nT[:, k, 0:Tp], start=(k == 0), stop=(k == 15)), r=["nT"] + wn, w=[bkn])
                    ACT(lambda e, m=m, bk=bk: e.activation(out=tA[:, m, 0:Tp], in_=bk[:, 0:Tp], func=AF.Tanh, scale=0.5), r=[bkn], w=["tA%d" % m])
                    DVE(lambda e, m=m, bk=bk: e.scalar_tensor_tensor(out=t1[:, m, 0:Tp], in0=tA[:, m, 0:Tp], scalar=1.0, in1=bk[:, 0:Tp], op0=ALU.add, op1=ALU.mult), r=[bkn, "tA%d" % m], w=["t1_%d" % m])
                wv, wn = load_panel(w_fu, 0, 16, fp * 512, 512)
                for m in range(4):
                    bk, bkn = nb()
                    for k in range(16):
                        PE(lambda e, k=k, m=m, bk=bk, wv=wv: e.matmul(bk[:, 0:Tp], lhsT=wv[:, k, m * 128:(m + 1) * 128], rhs=nT[:, k, 0:Tp], start=(k == 0), stop=(k == 15)), r=["nT"] + wn, w=[bkn])
                    DVE(lambda e, m=m, bk=bk: e.tensor_tensor(out=mixp[:, m, 0:Tp], in0=t1[:, m, 0:Tp], in1=bk[:, 0:Tp], op=ALU.mult), r=[bkn, "t1_%d" % m], w=["mixp"])
                wv, wn = load_panel(w_fd, fp * 4, 4, 0, D)
                for tt in range(ntt):
                    for nn in range(4):
                        bk, bkn = nb()
                        for k in range(4):
                            PE(lambda e, k=k, tt=tt, nn=nn, bk=bk, wv=wv: e.matmul(bk[:, :], lhsT=mixp[:, k, tt * 128:(tt + 1) * 128], rhs=wv[:, k, nn * 512:(nn + 1) * 512], start=(k == 0), stop=(k == 3)), r=["mixp"] + wn, w=[bkn])
                        DVE(lambda e, tt=tt, nn=nn, bk=bk: e.scalar_tensor_tensor(out=xh[:, tt, nn * 512:(nn + 1) * 512], in0=bk[:, :], scalar=0.5, in1=xh[:, tt, nn * 512:(nn + 1) * 512], op0=ALU.mult, op1=ALU.add), r=[bkn, "xh"], w=["xh"])

        xsem = {"n": 0}

        def load_x(src, tok0, Tp, ntt):
            i = xsem["n"] % 2
            xsem["n"] += 1
            DMA("sp", "xl%d" % i, lambda e: e.dma_start(out=xh[:, 0:ntt, :], in_=src[tok0:tok0 + Tp, :].rearrange("(t p) d -> p t d", p=128)), w=["xh"])

        def hb_cast(hv, nseq, J):
            for ri in range(2):
                ACT(lambda e, ri=ri: e.copy(out=Hb[:, ri].rearrange("p g (s j) -> p g s j", s=nseq), in_=hv[:, ri, :, :, 0:J]), r=["Hf"], w=["Hb"])

        def ssm_prompt_half(c0, with_y):
            ssm_S(c0, deep=with_y)
            ssm_scan_blocked(with_y)
            if with_y:
                ssm_y(c0, deep=True)

        try:
            register_panels()
            convert_panels(16)
            S.barrier()
            nctx = len(list(range(NPASS_C) if DBG["ctx"] is None else DBG["ctx"]))
            for ci in (range(NPASS_C) if DBG["ctx"] is None else DBG["ctx"]):
                load_x(xc, ci * T, T, NTT)
                stage("c_load")
                norm_to_T(g1T, "g1T", NTT)
                stage("c_norm")
                u_proj(T)
                convert_panels((44 + nctx - 1) // max(nctx, 1))
                stage("c_u")
                for hh in range(T // TS):
                    ssm_prompt_half(hh * TS, False)
                stage("c_scan")
                if ci == NPASS_C - 1:
                    qkv_stage([NTT - 1], [2], 18 - (NTT - 1), 128 - 128 * NTT, 1 - NTT)
                    stage("c_kv")

            convert_panels(len(ptodo))
            pq["q"] = "pool"
            osem = {"n": 0}
            for pi in (range(NPASS_P + 1) if DBG["main"] is None else DBG["main"]):
                sample = (pi == NPASS_P)
                Tp = 256 if sample else T
                ntt = Tp // 128
                tok0 = pi * T
                load_x(xm, tok0, Tp, ntt)
                norm_to_T(g1T, "g1T", ntt)
                u_proj(Tp)
                if not sample:
                    for hh in range(T // TS):
                        ssm_prompt_half(hh * TS, True)
                    if pi == NPASS_P - 1:
                        DMA("sp", "hfin", lambda e: e.dma_start(out=hfin, in_=hc[:].rearrange("p a g -> p (a g)")), r=["hc"])
                else:
                    ssm_S(0, deep=True, lay="agj")
                    hv = Hf[:, 0:2 * 32 * 4 * 9].rearrange("p (a g s j) -> p a g s j", a=2, g=32, s=4)
                    DMA("sp", "st0", lambda e: e.dma_start(out=hst[:], in_=st0), w=["hst"])
                    DVE(lambda e, hv=hv: e.tensor_copy(out=hv[:, :, :, :, 0], in_=hst[:].rearrange("p (a g s) -> p a g s", a=2, g=32)), r=["hst", "Hf", "Hb"], w=["Hf"])
                    hv = ssm_scan(4, 8)
                    hb_cast(hv, 4, 8)
                    DVE(lambda e, hv=hv: e.tensor_copy(out=hst[:].rearrange("p (a g s) -> p a g s", a=2, g=32), in_=hv[:, :, :, :, 8]), r=["Hf", "hst"], w=["hst"])
                    DMA("sp", "hsfin", lambda e: e.dma_start(out=hsfin, in_=hst[:]), r=["hst"])
                    ssm_y(0, deep=True)
                glu(Tp)
                stage("m_glu")

                def out_fn(tt, qf, qfn, pi=pi, sample=sample, ntt=ntt):
                    if (not sample) and pi == NPASS_P - 1 and tt == ntt - 1:
                        DMA("sp", "kwo", lambda e: e.dma_start(out=kwin, in_=qf[:, 0:256]), r=[qfn])
                        DMA("sp", "vwo", lambda e: e.dma_start(out=vwin, in_=qf[:, 256:512]), r=[qfn])
                    if sample:
                        for half in range(2):
                            s = 2 * tt + half
                            DMA("sp", "kso%d" % s, lambda e, half=half, s=s: e.dma_start(out=ks_o[s, 64:128, :], in_=qf[64 * half:64 * half + 64, 0:256]), r=[qfn])
                            DMA("sp", "vso%d" % s, lambda e, half=half, s=s: e.dma_start(out=vs_o[s, 64:128, :], in_=qf[64 * half:64 * half + 64, 256:512]), r=[qfn])
                rt0 = 16 if sample else pi * NTT
                qkv_stage(list(range(ntt)), [0, 1, 2], rt0, 128, 1, out_fn)
                stage("m_qkv")
                if sample:
                    TA_ALL = ["tA0", "tA1", "tA2", "tA3"]
                    for hf in range(2):
                        DMA("sp", "ckl", lambda e, hf=hf: e.dma_start(out=ckf[:], in_=ck[2 * hf:2 * hf + 2].rearrange("s p f -> p s f")), w=["tmpq"])
                        ACT(lambda e, hf=hf: e.copy(out=ckb[:, 2 * hf:2 * hf + 2, :], in_=ckf[:]), r=["tmpq"], w=["um4_0"])
                    for s in range(4):
                        pt_, ptn = nt()
                        for g in range(4):
                            PE(lambda e, s=s, g=g, pt_=pt_: e.transpose(out=pt_[0:64, g * 128:(g + 1) * 128], in_=ckb[:, s, g * 64:(g + 1) * 64], identity=ident[:]), r=["um4_0", "ident"], w=[ptn])
                        ACT(lambda e, s=s, pt_=pt_: e.copy(out=kTs[:, :, s, :], in_=pt_[0:64, 0:512].rearrange("p (g t) -> p g t", g=4)), r=[ptn] + TA_ALL, w=TA_ALL)
                    for hf in range(2):
                        DMA("sp", "cvl", lambda e, hf=hf: e.dma_start(out=ckf[:], in_=cv[2 * hf:2 * hf + 2].rearrange("s p f -> p s f")), r=["um4_0"], w=["tmpq"])
                        ACT(lambda e, hf=hf: e.copy(out=vcs[:, 2 * hf:2 * hf + 2, :], in_=ckf[:]), r=["tmpq"], w=["mixp"])
                    for s in range(4):
                        DMA("sp", "kcp%d" % s, lambda e, s=s: e.dma_start(out=ks_o[s, 0:64, :], in_=ck[s, 64:128, :]))
                        DMA("sp", "vcp%d" % s, lambda e, s=s: e.dma_start(out=vs_o[s, 0:64, :], in_=cv[s, 64:128, :]))
                apend = [None]
                for c in range(Tp // 64):
                    tt, par = c // 2, c % 2
                    if not sample:
                        gc = pi * (T // 64) + c
                        kA = (lambda g, tt=tt: kT[:, g, tt * 128:(tt + 1) * 128])
                        vA = (lambda g, tt=tt: vb[:, tt, g * 64:(g + 1) * 64])
                        if gc == 0:
                            bA = 0
                        elif gc == 1:
                            bA = 1
                        else:
                            bA = 4 if par == 0 else 3
                        bB = 2 if par == 0 else 4
                    else:
                        s = c
                        kA = (lambda g, s=s: kTs[:, g, s, :])
                        vA = (lambda g, s=s: vcs[:, s, g * 64:(g + 1) * 64])
                        bA = 4
                        bB = 2 if par == 0 else 3
                    kB = (lambda g, tt=tt: kT[:, g, 128 + tt * 128:128 + (tt + 1) * 128])
                    vB = (lambda g, tt=tt: vb[:, 1 + tt, g * 64:(g + 1) * 64])
                    for g in range(4):
                        ai = attention_part1(c, g, kA, bA, kB, bB)
                        if apend[0] is not None:
                            apend[0]()
                        apend[0] = (lambda c=c, g=g, ai=ai, vA=vA, vB=vB: attention_part2(c, g, ai, vA, vB))
                if apend[0] is not None:
                    apend[0]()
                    apend[0] = None
                stage("m_attn")
                if not sample:
                    ACT(lambda e: e.copy(out=kT[:, :, 0:128], in_=kT[:, :, T:T + 128]), r=["kT"], w=["kT"])
                    ACT(lambda e: e.copy(out=vb[:, 0, :], in_=vb[:, NTT, :]), r=["vb"], w=["vb"])
                mixer_and_h(Tp, ntt)
                stage("m_mix")
                norm_to_T(g2T, "g2T", ntt)
                ffn(Tp, ntt)
                i = osem["n"] % 2
                osem["n"] += 1
                DMA("sp", "yo%d" % i, lambda e, tok0=tok0, Tp=Tp, ntt=ntt: e.dma_start(out=ym[tok0:tok0 + Tp, :].rearrange("(t p) d -> p t d", p=128), in_=xh[:, 0:ntt, :]), r=["xh"])
        except _Stop:
            pass

        S.finalize()
        sems = {e: es.enter_context(nc.semaphore("s_" + e)) for e in Sched.ENGS}
        dsems = {k: es.enter_context(nc.semaphore("d_" + k)) for k in dsem_names}
        with nc.Block() as block:
            @block.tensor
            def _(e):
                S.emit("pe", e, sems, dsems)

            @block.scalar
            def _(e):
                S.emit("act", e, sems, dsems)

            @block.vector
            def _(e):
                S.emit("dve", e, sems, dsems)

            @block.gpsimd
            def _(e):
                S.emit("pool", e, sems, dsems)

            @block.sync
            def _(e):
                S.emit("sp", e, sems, dsems)
                for k, v in S.final_dma.items():
                    e.wait_ge(dsems[k], v)
    _NC_CACHE["S"] = S
    return nc


def _rope_table(pos):
    half = 8
    inv = (np.float32(500000.0) ** (-np.arange(half, dtype=np.float32) * np.float32(2.0) / np.float32(16))).astype(np.float32)
    ang = pos.astype(np.float32)[:, None] * inv[None, :]
    return np.concatenate([np.cos(ang), np.sin(ang)], axis=1).astype(np.float32)


def _prep(x_prompt, x_sample, cache_k, cache_v, state_ssm_re, state_ssm_im,
           norm1, w_in, q_norm, k_norm, sinks,
           ssm_a_re, ssm_a_im, ssm_log_dt, ssm_b_re, ssm_b_im, ssm_c_re, ssm_c_im, ssm_d,
           w_glu, w_br_ssm, w_br_attn, w_gate, w_out, norm2,
           w_ffn_gate, w_ffn_up, w_ffn_down):
    f = lambda a: np.ascontiguousarray(np.asarray(a, dtype=np.float32))
    x_prompt, x_sample = f(x_prompt), f(x_sample)
    cache_k, cache_v = f(cache_k)[0], f(cache_v)[0]
    sre, sim = f(state_ssm_re)[0], f(state_ssm_im)[0]

    def gl_layout(a):
        a = a.reshape((32, 2, 64) + a.shape[2:])
        a = np.moveaxis(a, 0, 2)
        return np.ascontiguousarray(a.reshape((128, 32) + a.shape[3:]))

    shared = {
        "g1T": f(norm1)[0].reshape(16, 128).T.copy(),
        "g2T": f(norm2)[0].reshape(16, 128).T.copy(),
        "qnb": np.ascontiguousarray(np.broadcast_to(f(q_norm)[0][None, :], (128, 64))),
        "knb": np.ascontiguousarray(np.broadcast_to(f(k_norm)[0][None, :], (128, 64))),
        "are": gl_layout(f(ssm_a_re)[0]),
        "aim": gl_layout(f(ssm_a_im)[0]),
        "ldt": gl_layout(np.ascontiguousarray(np.broadcast_to(f(ssm_log_dt)[0][:, None], (64, 64)))),
        "bre": gl_layout(f(ssm_b_re)[0]).reshape(128, 512),
        "bim": gl_layout(f(ssm_b_im)[0]).reshape(128, 512),
        "dsk": f(ssm_d)[0].reshape(8, 128).T.copy(),
        "w_in": f(w_in)[0], "w_glu": f(w_glu)[0], "w_brs": f(w_br_ssm)[0], "w_bra": f(w_br_attn)[0],
        "w_gate": f(w_gate)[0], "w_out": f(w_out)[0], "w_fg": f(w_ffn_gate)[0], "w_fu": f(w_ffn_up)[0], "w_fd": f(w_ffn_down)[0],
    }
    sk = f(sinks)[0]
    skl = np.zeros((128, 8), np.float32)
    skl[0:64, :] = sk[0::2][None, :]
    skl[64:128, :] = sk[1::2][None, :]
    shared["skl"] = skl

    def c_layout(a):
        a = a.reshape(32, 2, 16, 64)
        a = np.transpose(a, (1, 2, 0, 3))
        return np.ascontiguousarray(a.reshape(32, 32 * 64))

    shared["cre"] = c_layout(f(ssm_c_re)[0])
    shared["cim"] = c_layout(f(ssm_c_im)[0])

    in_maps = []
    for c in range(NCORES):
        b, half = c // 2, c % 2
        m = dict(shared)
        m["xm"] = np.concatenate([x_prompt[b, half * 2048:(half + 1) * 2048], x_sample[4 * c:4 * c + 4].reshape(256, D)], axis=0)
        m["xc"] = x_prompt[b, 0:2048] if half == 1 else np.zeros((NCTX, D), np.float32)
        pos = np.concatenate([half * 2048 + np.arange(2048), np.tile(1024 + np.arange(64), 4), half * 2048 - 128 + np.arange(128)])
        tab = _rope_table(pos)
        m["rope"] = np.ascontiguousarray(tab.reshape(19, 128, 16).transpose(1, 0, 2))
        m["ck"] = np.ascontiguousarray(cache_k[4 * c:4 * c + 4].reshape(4, 128, 256))
        m["cv"] = np.ascontiguousarray(cache_v[4 * c:4 * c + 4].reshape(4, 128, 256))
        s0 = np.stack([gl_layout(np.moveaxis(sre[4 * c:4 * c + 4], 0, 2)), gl_layout(np.moveaxis(sim[4 * c:4 * c + 4], 0, 2))], axis=1)
        m["st0"] = np.ascontiguousarray(s0.reshape(128, 256))
        mb = np.zeros((128, 4), np.float32)
        if half == 0:
            mb[:, 0] = BIG
            mb[:, 1] = BIG
        else:
            mb[0:64, 1] = BIG
        mb[64:128, 2] = BIG
        mb[0:64, 3] = BIG
        m["maskb"] = mb
        in_maps.append(m)

    return in_maps


def _gather(R):
    def ungl(a):
        a = a.reshape((2, 64, 32) + a.shape[2:])
        a = np.moveaxis(a, 2, 0)
        return np.ascontiguousarray(a.reshape((64, 64) + a.shape[3:]))

    y_prompt = np.zeros((4, 4096, D), np.float32)
    y_sample = np.zeros((32, 64, D), np.float32)
    kwp = np.zeros((1, 4, 128, 4, 64), np.float32)
    vwp = np.zeros((1, 4, 128, 4, 64), np.float32)
    srp = np.zeros((1, 4, 64, 64), np.float32)
    sip = np.zeros((1, 4, 64, 64), np.float32)
    kws = np.zeros((1, 32, 128, 4, 64), np.float32)
    vws = np.zeros((1, 32, 128, 4, 64), np.float32)
    srs = np.zeros((1, 32, 64, 64), np.float32)
    sis = np.zeros((1, 32, 64, 64), np.float32)
    for c in range(NCORES):
        b, half = c // 2, c % 2
        r = R[c]
        y_prompt[b, half * 2048:(half + 1) * 2048] = r["ym"][0:2048]
        y_sample[4 * c:4 * c + 4] = r["ym"][2048:2304].reshape(4, 64, D)
        if half == 1:
            kwp[0, b] = r["kwin"].reshape(128, 4, 64)
            vwp[0, b] = r["vwin"].reshape(128, 4, 64)
            hf = r["hfin"].reshape(128, 2, 32)
            srp[0, b] = ungl(hf[:, 0])
            sip[0, b] = ungl(hf[:, 1])
        kws[0, 4 * c:4 * c + 4] = r["ks_o"].reshape(4, 128, 4, 64)
        vws[0, 4 * c:4 * c + 4] = r["vs_o"].reshape(4, 128, 4, 64)
        hs = r["hsfin"].reshape(128, 2, 32, 4)
        srs[0, 4 * c:4 * c + 4] = np.moveaxis(ungl(hs[:, 0]), 2, 0)
        sis[0, 4 * c:4 * c + 4] = np.moveaxis(ungl(hs[:, 1]), 2, 0)
    return (y_prompt, y_sample, kwp, vwp, srp, sip, kws, vws, srs, sis)


def kernel(x_prompt, x_sample, cache_k, cache_v, state_ssm_re, state_ssm_im,
           norm1, w_in, q_norm, k_norm, sinks,
           ssm_a_re, ssm_a_im, ssm_log_dt, ssm_b_re, ssm_b_im, ssm_c_re, ssm_c_im, ssm_d,
           w_glu, w_br_ssm, w_br_attn, w_gate, w_out, norm2,
           w_ffn_gate, w_ffn_up, w_ffn_down):
    in_maps = _prep(x_prompt, x_sample, cache_k, cache_v, state_ssm_re, state_ssm_im, norm1, w_in, q_norm, k_norm, sinks, ssm_a_re, ssm_a_im, ssm_log_dt, ssm_b_re, ssm_b_im, ssm_c_re, ssm_c_im, ssm_d, w_glu, w_br_ssm, w_br_attn, w_gate, w_out, norm2, w_ffn_gate, w_ffn_up, w_ffn_down)
    if "nc" not in _NC_CACHE:
        _NC_CACHE["nc"] = build_program()
    nc = _NC_CACHE["nc"]
    res = run_bass_kernel_spmd(nc, in_maps, core_ids=list(range(NCORES)))
    return _gather(res.results)
```

```python
import numpy as np
from contextlib import ExitStack
import concourse.bass as bass
import concourse.mybir as mybir
from concourse.bass_utils import run_bass_kernel_spmd

F32 = mybir.dt.float32
BF16 = mybir.dt.bfloat16
ALU = mybir.AluOpType
AF = mybir.ActivationFunctionType
AX = mybir.AxisListType

NCORES = 8
D = 2048
DS = 1024
DFF = 5632
T = 512
NTT = T // 128
TS = 256
NCOL = TS // 8
NMAIN = 2304
NCTX = 2048
NPASS_P = 2048 // T
NPASS_C = NCTX // T
EPS = 1e-6
BIG = -30000.0


class Op:
    __slots__ = ("eng", "fn", "deps", "signaled", "value", "sem", "is_dma", "idx", "phase")

    def __init__(self, eng, fn, is_dma=False, sem=None):
        self.eng = eng
        self.fn = fn
        self.deps = []
        self.signaled = False
        self.value = 0
        self.sem = sem
        self.is_dma = is_dma


class Res:
    __slots__ = ("w", "r")

    def __init__(self):
        self.w = None
        self.r = {}


class Sched:
    ENGS = ("pe", "act", "dve", "pool", "sp")

    def __init__(self):
        self.ops = {e: [] for e in self.ENGS}
        self.res = {}
        self.n = 0
        self.pending = {e: [] for e in self.ENGS}
        self.phase = "setup"

    def barrier(self):
        lasts = []
        for e in self.ENGS:
            comp = [o for o in self.ops[e] if not o.is_dma]
            if comp:
                lasts.append(comp[-1])
        for e in self.ENGS:
            self.pending[e] = list(lasts)

    def _res(self, k):
        r = self.res.get(k)
        if r is None:
            r = self.res[k] = Res()
        return r

    def op(self, eng, fn, reads=(), writes=(), dma_sem=None):
        o = Op(eng, fn, is_dma=dma_sem is not None, sem=dma_sem)
        o.idx = self.n
        o.phase = self.phase
        self.n += 1
        deps = {}
        for k in reads:
            r = self._res(k)
            if r.w is not None:
                deps[id(r.w)] = r.w
        for k in writes:
            r = self._res(k)
            if r.w is not None:
                deps[id(r.w)] = r.w
            for d in r.r.values():
                deps[id(d)] = d
        if self.pending[eng]:
            for d in self.pending[eng]:
                deps[id(d)] = d
            self.pending[eng] = []
        o.deps = list(deps.values())
        for k in reads:
            r = self._res(k)
            key = ("dma", o.idx) if o.is_dma else eng
            r.r[key] = o
        for k in writes:
            r = self._res(k)
            r.w = o
            r.r = {}
        self.ops[eng].append(o)
        return o

    def finalize(self):
        for e in self.ENGS:
            for o in self.ops[e]:
                for d in o.deps:
                    if d.is_dma:
                        d.signaled = True
                    elif d.eng == "pe" and o.eng == "pe" and not o.is_dma:
                        pass
                    else:
                        d.signaled = True
        cnt = {e: 0 for e in self.ENGS}
        dcnt = {}
        allops = sorted((o for e in self.ENGS for o in self.ops[e]), key=lambda o: o.idx)
        for o in allops:
            if o.is_dma:
                dcnt[o.sem] = dcnt.get(o.sem, 0) + 16
                o.value = dcnt[o.sem]
        for e in self.ENGS:
            for o in self.ops[e]:
                if (not o.is_dma) and o.signaled:
                    cnt[e] += 1
                    o.value = cnt[e]
        self.final_dma = dict(dcnt)

    def emit(self, eng_name, eng, sems, dsems):
        seen = {}
        for o in self.ops[eng_name]:
            need = {}
            for d in o.deps:
                if d.is_dma:
                    key = ("d", d.sem)
                    sem = dsems[d.sem]
                else:
                    if d.eng == "pe" and eng_name == "pe" and not o.is_dma:
                        continue
                    key = ("e", d.eng)
                    sem = sems[d.eng]
                if need.get(key, (None, 0))[1] < d.value:
                    need[key] = (sem, d.value)
            for key, (sem, v) in need.items():
                if seen.get(key, 0) >= v:
                    continue
                eng.wait_ge(sem, v)
                seen[key] = v
            ins = o.fn(eng)
            if o.is_dma:
                ins.then_inc(dsems[o.sem], 16)
            elif o.signaled:
                ins.then_inc(sems[eng_name], 1)


_NC_CACHE = {}
DBG = {"ctx": None, "main": None, "stop": None}


class _Stop(Exception):
    pass


def stage(name):
    if DBG["stop"] == name:
        raise _Stop()


def build_program():
    nc = bass.Bass("TRN2", target_bir_lowering=False)

    def din(name, shape, dt=F32):
        return nc.dram_tensor(name, list(shape), dt, kind="ExternalInput").ap()

    def dout(name, shape, dt=F32):
        return nc.dram_tensor(name, list(shape), dt, kind="ExternalOutput").ap()

    xm = din("xm", [NMAIN, D])
    xc = din("xc", [NCTX, D])
    rope = din("rope", [128, 19, 16])
    ck = din("ck", [4, 128, 256])
    cv = din("cv", [4, 128, 256])
    st0 = din("st0", [128, 2 * 32 * 4])
    maskb = din("maskb", [128, 4])
    g1T_d = din("g1T", [128, 16])
    g2T_d = din("g2T", [128, 16])
    qnb_d = din("qnb", [128, 64])
    knb_d = din("knb", [128, 64])
    skl_d = din("skl", [128, 8])
    are_d = din("are", [128, 32])
    aim_d = din("aim", [128, 32])
    ldt_d = din("ldt", [128, 32])
    bre_d = din("bre", [128, 32 * 16])
    bim_d = din("bim", [128, 32 * 16])
    cre_d = din("cre", [32, 32 * 64])
    cim_d = din("cim", [32, 32 * 64])
    dsk_d = din("dsk", [128, 8])
    w_in = din("w_in", [D, 2560])
    w_glu = din("w_glu", [DS, DS])
    w_brs = din("w_brs", [DS, D])
    w_bra = din("w_bra", [DS, D])
    w_gate = din("w_gate", [D, 2 * D])
    w_out = din("w_out", [D, D])
    w_fg = din("w_fg", [D, DFF])
    w_fu = din("w_fu", [D, DFF])
    w_fd = din("w_fd", [DFF, D])

    ym = dout("ym", [NMAIN, D])
    kwin = dout("kwin", [128, 256])
    vwin = dout("vwin", [128, 256])
    hfin = dout("hfin", [128, 64])
    ks_o = dout("ks_o", [4, 128, 256])
    vs_o = dout("vs_o", [4, 128, 256])
    hsfin = dout("hsfin", [128, 2 * 32 * 4])

    WS_d = nc.dram_tensor("WS_d", [8, 128, 16 * 128], BF16).ap()
    YK_d = nc.dram_tensor("YK_d", [8, 128, 3072], BF16).ap()
    WB_d = nc.dram_tensor("WB_d", [60, 128, 8192], BF16).ap()

    S = Sched()
    es = ExitStack()
    with es:
        def sb(name, shape, dt):
            return es.enter_context(nc.sbuf_tensor(name, list(shape), dt))

        def PE(fn, r=(), w=()):
            return S.op("pe", fn, r, w)

        def ACT(fn, r=(), w=()):
            return S.op("act", fn, r, w)

        def DVE(fn, r=(), w=()):
            return S.op("dve", fn, r, w)

        def POOL(fn, r=(), w=()):
            return S.op("pool", fn, r, w)

        dsem_names = []

        def DMA(eng, sem, fn, r=(), w=()):
            if sem not in dsem_names:
                dsem_names.append(sem)
            return S.op(eng, fn, r, w, dma_sem=sem)

        xh = sb("xh", [128, NTT, D], F32)
        nT = sb("nT", [128, 16, T], BF16)
        xnb = sb("xnb", [128, D], BF16)
        wpf = [sb("wp%d" % i, [128, 8192], BF16) for i in range(2)]
        uT = sb("uT", [128, 8, T], BF16)
        geluT = sb("geluT", [128, 8, T], BF16)
        oT = geluT
        maskl = sb("maskl", [128, 7, 8], BF16)
        um = sb("um", [128, 7, TS], BF16)
        um4s = [sb("um4_%d" % i, [128, 4, TS], BF16) for i in range(2)]
        um4 = um4s[0]
        rmask = sb("rmask", [128, 4], F32)
        mixp = sb("mixp", [128, 4, T], BF16)
        tA = sb("tA", [128, 4, T], BF16)
        t1 = sb("t1", [128, 4, T], BF16)
        t2 = sb("t2", [128, T], F32)
        qfs = [sb("qf%d" % i, [128, 512], F32) for i in range(2)]
        qkbs = [sb("qkb%d" % i, [128, 512], BF16) for i in range(2)]
        tmpq = sb("tmpq", [128, 512], F32)
        rt = sb("rt", [128, 4, 64], F32)
        st = sb("st", [128, 8, 20], F32)
        qT = sb("qT", [64, T // 64, 16, 64], BF16)
        kT = sb("kT", [64, 4, 128 + T], BF16)
        vb = sb("vb", [128, 1 + NTT, 256], BF16)
        kTs = tA[0:64].rearrange("p a b -> p (a b)")[:, 0:2048].rearrange("p (g s t) -> p g s t", g=4, s=4)
        vcs = mixp[:, :, 0:256]
        ckb = um4
        PTs = [sb("PT%d" % i, [128, 2, 256], BF16) for i in range(2)]
        dtmps = [sb("dtmp%d" % i, [128, 128], F32) for i in range(2)]
        Ssb = sb("Ssb", [128, 2, 32, NCOL], F32)
        Hf = sb("Hf", [128, 2 * 32 * 36], F32)
        Hb = sb("Hb", [128, 2, 32, NCOL], BF16)
        hc = sb("hc", [128, 2, 32], F32)
        hst = sb("hst", [128, 2 * 32 * 4], F32)
        sc = sb("sc", [128, 4, 128], F32)
        wsb = [sb("wsb%d" % i, [128, 16, 128], BF16) for i in range(2)]
        ykb = [sb("ykb%d" % i, [128, 3072], BF16) for i in range(2)]
        wyb = [ykb[i][:, 0:2048].rearrange("p (a b c) -> p a b c", a=4, b=16) for i in range(2)]
        ktb = [ykb[i][:, 2048:3072].rearrange("p (a b) -> p a b", a=8) for i in range(2)]
        ws_slots = [(wsb[0], "wsb0"), (wsb[1], "wsb1"),
                    (wpf[0][:, 4096:6144].rearrange("p (a b) -> p a b", a=16), "wp0h"), (wpf[1][:, 4096:6144].rearrange("p (a b) -> p a b", a=16), "wp1h")]
        ws_slots_ctx = [ws_slots[0], ws_slots[1],
                        (ykb[0][:, 0:2048].rearrange("p (a b) -> p a b", a=16), "ykb0"), (ykb[1][:, 0:2048].rearrange("p (a b) -> p a b", a=16), "ykb1")]
        yk_slots = [(ykb[0][:, :], "ykb0"), (ykb[1][:, :], "ykb1"), (wpf[0][:, 4096:7168], "wp0h"), (wpf[1][:, 4096:7168], "wp1h")]
        identf = sb("identf", [128, 128], F32)
        ident = sb("ident", [128, 128], BF16)
        ones_b = sb("ones_b", [128, 64], BF16)
        ropeT = sb("ropeT", [128, 19, 16], F32)
        g1T = sb("g1T_s", [128, 16], F32)
        g2T = sb("g2T_s", [128, 16], F32)
        qnb = sb("qnb_s", [128, 64], F32)
        knb = sb("knb_s", [128, 64], F32)
        skx = sb("skx", [128, 8], F32)
        bias5 = sb("bias5", [128, 5], F32)
        cst = sb("cst", [128, 4], F32)
        dsk = sb("dsk_s", [128, 8], F32)
        A8 = sb("A8", [128, 2, 32], F32)
        P8t = sb("P8t", [128, 2, 8, 32], F32)
        A64 = sb("A64", [128, 2, 32], F32)
        Pw = Hf[:, 0:576].rearrange("p (k a g) -> p k a g", k=9, a=2)
        dz = Hf[:, 576:960].rearrange("p (k g) -> p k g", k=12)
        Ssf = Ssb[:].rearrange("p a g j -> p (a g j)")
        Bn = Ssf[:, 0:1024].rearrange("p (a g c) -> p a g c", a=2, g=32)
        Bb = Ssf[:, 1024:2048].rearrange("p (a g c) -> p a g c", a=2, g=32)
        xhf = xh[:].rearrange("p t d -> p (t d)")
        Cn = xh[0:32].rearrange("p t d -> p (t d)")[:, 0:4096].rearrange("p (a g n) -> p a g n", a=2, g=32)
        ZC = wpf[0][0:32, 0:8192].rearrange("p (a g n) -> p a g n", a=2, g=32)
        CT = xhf[:, 4096:6144].rearrange("p (a g c) -> p a g c", a=2, g=32)
        CTb = wpf[1][:, 0:2048].rearrange("p (a g c) -> p a g c", a=2, g=32)
        Zt = wpf[1][:, 2048:4096].rearrange("p (k c) -> p k c", k=16)
        pbt4 = xhf[:, 6144:7168].rearrange("p (a k q c) -> p a k q c", a=4, k=4, q=4)
        wyt2 = xhf[:, 7168:8192].rearrange("p (a r q c) -> p a r q c", a=4, r=2, q=4)
        ktf = t2[:, 0:128]
        mblk = t2[:, 128:160]
        BD = t2[:, 256:384]

        pbank = [es.enter_context(nc.psum_tensor("pb%d" % i, [128, 512], F32)) for i in range(6)]
        ptb = [es.enter_context(nc.psum_tensor("pt%d" % i, [128, 1024], BF16)) for i in range(2)]
        state = {"b": 0, "t": 0, "w": 0, "ws": 0, "wy": 0, "kt": 0, "a": 0, "q": 0, "u": 0, "ws2": 0, "yk2": 0}

        def nb():
            i = state["b"] % 6
            state["b"] += 1
            return pbank[i], "pb%d" % i

        def nt():
            i = state["t"] % 2
            state["t"] += 1
            return ptb[i], "pt%d" % i

        pidx = {}
        ptodo = []
        pq = {"q": "sp"}

        def panel_key(W, r0, kc, c0, nw):
            return (id(W), r0, kc, c0, nw)

        def register_panels():
            lst = [(w_in, 0, 16, 0, 512), (w_in, 0, 16, 512, 512), (w_glu, 0, 8, 0, 512), (w_glu, 0, 8, 512, 512)]
            lst += [(w_in, 0, 16, 1024 + pn * 512, 512) for pn in range(3)]
            for mp in range(4):
                lst += [(w_gate, 0, 16, mp * 512, 512), (w_brs, 0, 8, mp * 512, 512), (w_gate, 0, 16, D + mp * 512, 512), (w_bra, 0, 8, mp * 512, 512), (w_out, mp * 4, 4, 0, D)]
            for fp in range(11):
                lst += [(w_fg, 0, 16, fp * 512, 512), (w_fu, 0, 16, fp * 512, 512), (w_fd, fp * 4, 4, 0, D)]
            for it in lst:
                pidx[panel_key(*it)] = len(pidx)
                ptodo.append(it)

        def convert_panels(n):
            for _ in range(min(n, len(ptodo))):
                W, r0, kc, c0, nw = ptodo.pop(0)
                idx = pidx[panel_key(W, r0, kc, c0, nw)]
                src = W[r0 * 128:(r0 + kc) * 128, c0:c0 + nw].rearrange("(k p) n -> p k n", p=128)
                dst = WB_d[idx, :, 0:kc * nw].rearrange("p (k n) -> p k n", n=nw)
                DMA("pool", "cv%d" % (idx % 16), lambda e, src=src, dst=dst: e.dma_start(out=dst, in_=src), r=(["WB%d" % (idx - 16)] if idx >= 16 else []), w=["WB%d" % idx])

        def load_panel(W, r0, kc, c0, nw):
            i = state["w"] % 2
            state["w"] += 1
            idx = pidx[panel_key(W, r0, kc, c0, nw)]
            view = wpf[i][:, 0:kc * nw].rearrange("p (k n) -> p k n", n=nw)
            names = ["wp%d" % i] + (["wp%dh" % i] if kc * nw > 4096 else [])
            flat = wpf[i][:, 0:kc * nw]
            src = WB_d[idx, :, 0:kc * nw]
            DMA(pq["q"], "wp%d_%s" % (i, pq["q"]), lambda e: e.dma_start(out=flat, in_=src), r=["WB%d" % idx], w=names)
            return view, names

        ckf = tmpq[:, 0:512].rearrange("p (s f) -> p s f", s=2)
        DMA("sp", "c0", lambda e: e.dma_start(out=ropeT[:], in_=rope), w=["ropeT"])
        DMA("sp", "c1", lambda e: e.dma_start(out=g1T[:], in_=g1T_d), w=["g1T"])
        DMA("sp", "c2", lambda e: e.dma_start(out=g2T[:], in_=g2T_d), w=["g2T"])
        DMA("sp", "c3", lambda e: e.dma_start(out=qnb[:], in_=qnb_d), w=["qnb"])
        DMA("sp", "c4", lambda e: e.dma_start(out=knb[:], in_=knb_d), w=["knb"])
        DMA("sp", "c5", lambda e: e.dma_start(out=skx[:], in_=skl_d), w=["skx"])
        DMA("sp", "c6", lambda e: e.dma_start(out=bias5[:, 0:4], in_=maskb), w=["bias5"])
        DMA("sp", "c7", lambda e: e.dma_start(out=dsk[:], in_=dsk_d), w=["dsk"])
        DMA("sp", "c8", lambda e: e.dma_start(out=dz[:, 0, :], in_=are_d), w=["dz0"])
        DMA("sp", "c9", lambda e: e.dma_start(out=dz[:, 1, :], in_=aim_d), w=["dz1"])
        DMA("sp", "c10", lambda e: e.dma_start(out=dz[:, 2, :], in_=ldt_d), w=["dz2"])
        DMA("sp", "c11", lambda e: e.dma_start(out=Bn[:, 0].rearrange("p a b -> p (a b)"), in_=bre_d), w=["Bn"])
        DMA("sp", "c12", lambda e: e.dma_start(out=Bn[:, 1].rearrange("p a b -> p (a b)"), in_=bim_d), w=["Bn"])
        DMA("sp", "c13", lambda e: e.dma_start(out=Cn[:, 0].rearrange("p a b -> p (a b)"), in_=cre_d), w=["Cn"])
        DMA("sp", "c14", lambda e: e.dma_start(out=Cn[:, 1].rearrange("p a b -> p (a b)"), in_=cim_d), w=["Cn"])

        POOL(lambda e: e.memset(identf[:], 0.0), w=["identf"])
        POOL(lambda e: e.affine_select(out=identf[:], in_=identf[:], pattern=[[-1, 128]], compare_op=ALU.not_equal, fill=1.0, base=0, channel_multiplier=1), r=["identf"], w=["identf"])
        POOL(lambda e: e.memset(ones_b[:], 1.0), w=["ones_b"])
        POOL(lambda e: e.memset(maskl[:], 0.0), w=["maskl"])
        POOL(lambda e: e.memset(rmask[:], 0.0), w=["rmask"])
        for q in range(3):
            POOL(lambda e, q=q: e.memset(rmask[32 * q:32 * q + 32, q:q + 1], 1.0), r=["rmask"], w=["rmask"])
        POOL(lambda e: e.memset(rmask[64:128, 3:4], 1.0), r=["rmask"], w=["rmask"])
        POOL(lambda e: e.memset(rmask[64:96, 3:4], 0.0), r=["rmask"], w=["rmask"])
        for tau in range(1, 8):
            POOL(lambda e, tau=tau: e.memset(maskl[:, tau - 1, 0:8 - tau], 1.0), r=["maskl"], w=["maskl"])
        POOL(lambda e: e.memset(cst[:, 0:1], EPS), w=["cst0"])
        POOL(lambda e: e.memset(cst[:, 1:2], float(np.pi / 2)), w=["cst1"])
        POOL(lambda e: e.memset(hc[:], 0.0), w=["hc"])
        POOL(lambda e: e.memset(mblk[:], 0.0), w=["mblk"])
        POOL(lambda e: e.memset(mblk[0:64, 0:16], 1.0), r=["mblk"], w=["mblk"])
        POOL(lambda e: e.memset(mblk[64:128, 16:32], 1.0), r=["mblk"], w=["mblk"])
        POOL(lambda e: e.memset(BD[:], 0.0), w=["BD"])
        for q in range(4):
            if q < 3:
                POOL(lambda e, q=q: e.memset(BD[32 * q:32 * q + 32, 32 * q:32 * q + 32], 1.0), r=["BD"], w=["BD"])
        POOL(lambda e: e.memset(BD[64:128, 96:128], 1.0), r=["BD"], w=["BD"])
        POOL(lambda e: e.memset(BD[64:96, 96:128], 0.0), r=["BD"], w=["BD"])
        POOL(lambda e: e.memset(Zt[:], 0.0), w=["Zt"])
        DVE(lambda e: e.tensor_copy(out=ident[:], in_=identf[:]), r=["identf"], w=["ident"])

        DVE(lambda e: e.tensor_tensor(out=tmpq[:, 0:64], in0=qnb[:], in1=qnb[:], op=ALU.mult), r=["qnb"], w=["tmpq"])
        DVE(lambda e: e.tensor_reduce(out=st[:, 0, 0:1], in_=tmpq[:, 0:64], axis=AX.X, op=ALU.max), r=["tmpq"], w=["st"])
        DVE(lambda e: e.tensor_tensor(out=tmpq[:, 64:128], in0=knb[:], in1=knb[:], op=ALU.mult), r=["knb"], w=["tmpq"])
        DVE(lambda e: e.tensor_reduce(out=st[:, 0, 1:2], in_=tmpq[:, 64:128], axis=AX.X, op=ALU.max), r=["tmpq", "st"], w=["st"])
        DVE(lambda e: e.tensor_tensor(out=st[:, 0, 2:3], in0=st[:, 0, 0:1], in1=st[:, 0, 1:2], op=ALU.mult), r=["st"], w=["st"])
        ACT(lambda e: e.activation(out=st[:, 0, 3:4], in_=st[:, 0, 2:3], func=AF.Ln, scale=64.0), r=["st"], w=["st"])
        ACT(lambda e: e.activation(out=st[:, 0, 4:5], in_=st[:, 0, 3:4], func=AF.Exp, scale=0.5), r=["st"], w=["st"])
        DVE(lambda e: e.tensor_scalar(out=cst[:, 2:3], in0=st[:, 0, 4:5], scalar1=-1.0, scalar2=None, op0=ALU.mult), r=["st"], w=["cst2"])
        DVE(lambda e: e.tensor_copy(out=bias5[:, 4:5], in_=cst[:, 2:3]), r=["cst2", "bias5"], w=["bias5"])
        DVE(lambda e: e.tensor_scalar(out=bias5[:, 0:4], in0=bias5[:, 0:4], scalar1=cst[:, 2:3], scalar2=None, op0=ALU.add), r=["cst2", "bias5"], w=["bias5"])
        ACT(lambda e: e.activation(out=skx[:], in_=skx[:], func=AF.Exp, bias=cst[:, 2:3], scale=1.0), r=["skx", "cst2"], w=["skx"])

        ACT(lambda e: e.activation(out=dz[:, 3, :], in_=dz[:, 2, :], func=AF.Exp), r=["dz2"], w=["dz3"])
        DVE(lambda e: e.tensor_tensor(out=dz[:, 4, :], in0=dz[:, 0, :], in1=dz[:, 3, :], op=ALU.mult), r=["dz0", "dz3"], w=["dz4"])
        DVE(lambda e: e.tensor_tensor(out=dz[:, 5, :], in0=dz[:, 1, :], in1=dz[:, 3, :], op=ALU.mult), r=["dz1", "dz3"], w=["dz5"])
        ACT(lambda e: e.activation(out=dz[:, 6, :], in_=dz[:, 4, :], func=AF.Exp, scale=1.0 / 32), r=["dz4"], w=["dz6"])
        ACT(lambda e: e.activation(out=dz[:, 7, :], in_=dz[:, 5, :], func=AF.Sin, scale=1.0 / 32), r=["dz5"], w=["dz7"])
        ACT(lambda e: e.activation(out=dz[:, 8, :], in_=dz[:, 5, :], func=AF.Sin, scale=1.0 / 32, bias=cst[:, 1:2]), r=["dz5", "cst1"], w=["dz8"])
        DVE(lambda e: e.tensor_tensor(out=Pw[:, 1, 0, :], in0=dz[:, 6, :], in1=dz[:, 8, :], op=ALU.mult), r=["dz6", "dz8"], w=["Pw"])
        DVE(lambda e: e.tensor_tensor(out=Pw[:, 1, 1, :], in0=dz[:, 6, :], in1=dz[:, 7, :], op=ALU.mult), r=["dz6", "dz7", "Pw"], w=["Pw"])

        def cmul(o_re, o_im, a_re, a_im, b_re, b_im, res):
            DVE(lambda e: e.tensor_tensor(out=dz[:, 9, :], in0=a_re, in1=b_re, op=ALU.mult), r=res, w=["dz9"])
            DVE(lambda e: e.tensor_tensor(out=dz[:, 10, :], in0=a_im, in1=b_im, op=ALU.mult), r=res, w=["dz10"])
            DVE(lambda e: e.tensor_tensor(out=o_re, in0=dz[:, 9, :], in1=dz[:, 10, :], op=ALU.subtract), r=["dz9", "dz10"] + res, w=res)
            DVE(lambda e: e.tensor_tensor(out=dz[:, 9, :], in0=a_re, in1=b_im, op=ALU.mult), r=res, w=["dz9"])
            DVE(lambda e: e.tensor_tensor(out=dz[:, 10, :], in0=a_im, in1=b_re, op=ALU.mult), r=res, w=["dz10"])
            DVE(lambda e: e.tensor_tensor(out=o_im, in0=dz[:, 9, :], in1=dz[:, 10, :], op=ALU.add), r=["dz9", "dz10"] + res, w=res)

        cur, oth = 1, 2
        for _ in range(5):
            cmul(Pw[:, oth, 0, :], Pw[:, oth, 1, :], Pw[:, cur, 0, :], Pw[:, cur, 1, :], Pw[:, cur, 0, :], Pw[:, cur, 1, :], ["Pw"])
            cur, oth = oth, cur
        DVE(lambda e: e.tensor_copy(out=Pw[:, 1], in_=Pw[:, 2]), r=["Pw"], w=["Pw"])
        DVE(lambda e: e.memset(Pw[:, 0, 0, :], 1.0), r=["Pw"], w=["Pw"])
        DVE(lambda e: e.memset(Pw[:, 0, 1, :], 0.0), r=["Pw"], w=["Pw"])
        for k in range(2, 9):
            cmul(Pw[:, k, 0, :], Pw[:, k, 1, :], Pw[:, k - 1, 0, :], Pw[:, k - 1, 1, :], Pw[:, 1, 0, :], Pw[:, 1, 1, :], ["Pw"])
        DVE(lambda e: e.tensor_copy(out=A8[:], in_=Pw[:, 8]), r=["Pw"], w=["A8"])
        DVE(lambda e: e.memset(P8t[:, 0, 0, :], 1.0), w=["P8t"])
        DVE(lambda e: e.memset(P8t[:, 1, 0, :], 0.0), r=["P8t"], w=["P8t"])
        DVE(lambda e: e.tensor_copy(out=P8t[:, :, 1, :], in_=A8[:]), r=["A8", "P8t"], w=["P8t"])
        for i in range(2, 8):
            cmul(P8t[:, 0, i, :], P8t[:, 1, i, :], P8t[:, 0, i - 1, :], P8t[:, 1, i - 1, :], A8[:, 0, :], A8[:, 1, :], ["P8t", "A8"])
        cmul(A64[:, 0, :], A64[:, 1, :], P8t[:, 0, 7, :], P8t[:, 1, 7, :], A8[:, 0, :], A8[:, 1, :], ["P8t", "A8", "A64"])
        DVE(lambda e: e.tensor_scalar(out=dz[:, 2, :], in0=Pw[:, 1, 0, :], scalar1=-1.0, scalar2=None, op0=ALU.add), r=["Pw", "dz2", "dz3"], w=["dz2"])
        DVE(lambda e: e.tensor_tensor(out=dz[:, 3, :], in0=dz[:, 0, :], in1=dz[:, 0, :], op=ALU.mult), r=["dz0", "dz3"], w=["dz3"])
        DVE(lambda e: e.tensor_tensor(out=dz[:, 4, :], in0=dz[:, 1, :], in1=dz[:, 1, :], op=ALU.mult), r=["dz1", "dz4", "dz6"], w=["dz4"])
        DVE(lambda e: e.tensor_tensor(out=dz[:, 3, :], in0=dz[:, 3, :], in1=dz[:, 4, :], op=ALU.add), r=["dz3", "dz4"], w=["dz3"])
        DVE(lambda e: e.reciprocal(out=dz[:, 3, :], in_=dz[:, 3, :]), r=["dz3"], w=["dz3"])
        DVE(lambda e: e.tensor_tensor(out=dz[:, 9, :], in0=dz[:, 2, :], in1=dz[:, 0, :], op=ALU.mult), r=["dz2", "dz0", "dz9"], w=["dz9"])
        DVE(lambda e: e.tensor_tensor(out=dz[:, 10, :], in0=Pw[:, 1, 1, :], in1=dz[:, 1, :], op=ALU.mult), r=["Pw", "dz1", "dz10"], w=["dz10"])
        DVE(lambda e: e.tensor_tensor(out=dz[:, 9, :], in0=dz[:, 9, :], in1=dz[:, 10, :], op=ALU.add), r=["dz9", "dz10"], w=["dz9"])
        DVE(lambda e: e.tensor_tensor(out=dz[:, 6, :], in0=dz[:, 9, :], in1=dz[:, 3, :], op=ALU.mult), r=["dz9", "dz3", "dz6", "dz7", "dz8"], w=["dz6"])
        DVE(lambda e: e.tensor_tensor(out=dz[:, 9, :], in0=Pw[:, 1, 1, :], in1=dz[:, 0, :], op=ALU.mult), r=["Pw", "dz0", "dz9", "dz6"], w=["dz9"])
        DVE(lambda e: e.tensor_tensor(out=dz[:, 10, :], in0=dz[:, 2, :], in1=dz[:, 1, :], op=ALU.mult), r=["dz2", "dz1", "dz10"], w=["dz10"])
        DVE(lambda e: e.tensor_tensor(out=dz[:, 9, :], in0=dz[:, 9, :], in1=dz[:, 10, :], op=ALU.subtract), r=["dz9", "dz10"], w=["dz9"])
        DVE(lambda e: e.tensor_tensor(out=dz[:, 7, :], in0=dz[:, 9, :], in1=dz[:, 3, :], op=ALU.mult), r=["dz9", "dz3", "dz7"], w=["dz7"])

        def bc16(ap2):
            return ap2.unsqueeze(2).broadcast_to([128, 32, 16])

        DVE(lambda e: e.tensor_tensor(out=Bb[:, 0], in0=Bn[:, 0], in1=bc16(dz[:, 6, :]), op=ALU.mult), r=["Bn", "dz6"], w=["Bb"])
        DVE(lambda e: e.tensor_tensor(out=Bb[:, 1], in0=Bn[:, 1], in1=bc16(dz[:, 7, :]), op=ALU.mult), r=["Bn", "dz7", "Bb"], w=["Bb"])
        DVE(lambda e: e.tensor_tensor(out=Bb[:, 0], in0=Bb[:, 0], in1=Bb[:, 1], op=ALU.subtract), r=["Bb"], w=["Bb"])
        DVE(lambda e: e.tensor_tensor(out=Bb[:, 1], in0=Bn[:, 1], in1=bc16(dz[:, 6, :]), op=ALU.mult), r=["Bn", "dz6", "Bb"], w=["Bb"])
        DVE(lambda e: e.tensor_tensor(out=Bn[:, 0], in0=Bn[:, 0], in1=bc16(dz[:, 7, :]), op=ALU.mult), r=["Bn", "dz7"], w=["Bn"])
        DVE(lambda e: e.tensor_tensor(out=Bb[:, 1], in0=Bb[:, 1], in1=Bn[:, 0], op=ALU.add), r=["Bb", "Bn"], w=["Bb"])

        for ri in range(2):
            ACT(lambda e, ri=ri: e.copy(out=ZC[:, ri, :, 0:64], in_=Cn[:, ri]), r=["Cn"], w=["ZC"])
            ACT(lambda e, ri=ri: e.copy(out=ZC[:, ri, :, 64:128], in_=Cn[:, ri]), r=["Cn", "ZC"], w=["ZC"])
        for ri in range(2):
            for half in range(2):
                pt_, ptn = nt()
                for g in range(16):
                    gh = half * 16 + g
                    PE(lambda e, ri=ri, gh=gh, g=g, pt_=pt_: e.transpose(out=pt_[:, g * 32:(g + 1) * 32], in_=ZC[:, ri, gh, :], identity=ident[0:32, 0:32]), r=["ZC", "ident"], w=[ptn])
                DVE(lambda e, ri=ri, half=half, pt_=pt_: e.tensor_tensor(
                    out=CT[:, ri, half * 16:(half + 1) * 16, :], in0=pt_[:, 0:512].rearrange("p (g c) -> p g c", c=32),
                    in1=mblk[:].unsqueeze(1).broadcast_to([128, 16, 32]), op=ALU.mult), r=[ptn, "mblk"], w=["CT"])
        ACT(lambda e: e.copy(out=CTb[:, 0], in_=CT[:, 0]), r=["CT"], w=["CTb"])
        DVE(lambda e: e.tensor_scalar(out=CTb[:, 1], in0=CT[:, 1], scalar1=-1.0, scalar2=None, op0=ALU.mult), r=["CT", "CTb"], w=["CTb"])

        for ct in range(8):
            gsl = slice(4 * ct, 4 * ct + 4)
            Ztv = Zt.rearrange("p (k r) c -> p k r c", r=2)
            for k4 in range(2):
                ksl = slice(4 * k4, 4 * k4 + 4)
                pr = Pw[:, ksl, 0, gsl].unsqueeze(3).broadcast_to([128, 4, 4, 16])
                pi = Pw[:, ksl, 1, gsl].unsqueeze(3).broadcast_to([128, 4, 4, 16])
                bre = Bb[:, 0, gsl, :].unsqueeze(1).broadcast_to([128, 4, 4, 16])
                bim = Bb[:, 1, gsl, :].unsqueeze(1).broadcast_to([128, 4, 4, 16])
                DVE(lambda e, pr=pr, bre=bre: e.tensor_tensor(out=pbt4[:, 0], in0=bre, in1=pr, op=ALU.mult), r=["Bb", "Pw"], w=["pbt"])
                DVE(lambda e, pi=pi, bim=bim: e.tensor_tensor(out=pbt4[:, 1], in0=bim, in1=pi, op=ALU.mult), r=["Bb", "Pw", "pbt"], w=["pbt"])
                DVE(lambda e, pi=pi, bre=bre: e.tensor_tensor(out=pbt4[:, 2], in0=bre, in1=pi, op=ALU.mult), r=["Bb", "Pw", "pbt"], w=["pbt"])
                DVE(lambda e, pr=pr, bim=bim: e.tensor_tensor(out=pbt4[:, 3], in0=bim, in1=pr, op=ALU.mult), r=["Bb", "Pw", "pbt"], w=["pbt"])
                for gl in range(2):
                    ps_ = slice(64 * gl, 64 * gl + 64)
                    zre = Ztv[ps_, ksl, 0, :].rearrange("p k (q g c) -> p k q g c", q=4, g=2)[:, :, :, gl, :]
                    zim = Ztv[ps_, ksl, 1, :].rearrange("p k (q g c) -> p k q g c", q=4, g=2)[:, :, :, gl, :]
                    DVE(lambda e, ps_=ps_, zre=zre: e.tensor_tensor(out=zre, in0=pbt4[ps_, 0], in1=pbt4[ps_, 1], op=ALU.subtract), r=["pbt", "Zt"], w=["Zt"])
                    DVE(lambda e, ps_=ps_, zim=zim: e.tensor_tensor(out=zim, in0=pbt4[ps_, 2], in1=pbt4[ps_, 3], op=ALU.add), r=["pbt", "Zt"], w=["Zt"])
            wsi = state["ws"] % 2
            state["ws"] += 1
            for h8 in range(2):
                pt_, ptn = nt()
                for j in range(8):
                    kri = h8 * 8 + j
                    PE(lambda e, kri=kri, j=j, pt_=pt_: e.transpose(out=pt_[:, j * 128:(j + 1) * 128], in_=Zt[:, kri, :], identity=ident[:]), r=["Zt", "ident"], w=[ptn])
                ACT(lambda e, h8=h8, pt_=pt_, wsi=wsi: e.copy(out=wsb[wsi][:, h8 * 8:(h8 + 1) * 8, :].rearrange("p a b -> p (a b)"), in_=pt_[:, :]), r=[ptn], w=["wsb%d" % wsi])
            DMA("sp", "wsd%d" % wsi, lambda e, ct=ct, wsi=wsi: e.dma_start(out=WS_d[ct], in_=wsb[wsi][:].rearrange("p a b -> p (a b)")), r=["wsb%d" % wsi], w=["WS_d"])
            kti = state["kt"] % 2
            state["kt"] += 1
            for tau in range(8):
                bk, bkn = nb()
                PE(lambda e, tau=tau, bk=bk, gsl=gsl: e.matmul(bk[:, 0:128], lhsT=Zt[:, 2 * tau, :], rhs=CTb[:, 0, gsl, :].rearrange("p a b -> p (a b)"), start=True, stop=False), r=["Zt", "CTb"], w=[bkn])
                PE(lambda e, tau=tau, bk=bk, gsl=gsl: e.matmul(bk[:, 0:128], lhsT=Zt[:, 2 * tau + 1, :], rhs=CTb[:, 1, gsl, :].rearrange("p a b -> p (a b)"), start=False, stop=True), r=["Zt", "CTb"], w=[bkn])
                if tau == 0:
                    DVE(lambda e, bk=bk: e.tensor_tensor(out=ktf[:], in0=bk[:, 0:128], in1=BD[:], op=ALU.mult), r=[bkn, "BD"], w=["ktf"])
                    DVE(lambda e, ct=ct, kti=kti: e.scalar_tensor_tensor(out=ktb[kti][:, 0, :], in0=identf[:], scalar=dsk[:, ct:ct + 1], in1=ktf[:], op0=ALU.mult, op1=ALU.add), r=["ktf", "identf", "dsk"], w=["ktb%d" % kti])
                else:
                    DVE(lambda e, bk=bk, tau=tau, kti=kti: e.tensor_tensor(out=ktb[kti][:, tau, :], in0=bk[:, 0:128], in1=BD[:], op=ALU.mult), r=[bkn, "BD"], w=["ktb%d" % kti])
            DMA("sp", "ktd%d" % kti, lambda e, ct=ct, kti=kti: e.dma_start(out=YK_d[ct, :, 2048:3072], in_=ktb[kti].rearrange("p a b -> p (a b)")), r=["ktb%d" % kti, "ykb%d" % kti], w=["KT_d"])
            wyi = state["wy"] % 2
            state["wy"] += 1
            wyv = wyb[wyi].rearrange("p q (r i) c -> p q r i c", i=2)
            for r2 in range(4):
                rsl = slice(2 * r2, 2 * r2 + 2)
                ksl = slice(2 * r2 + 1, 2 * r2 + 3)
                pr = Pw[:, ksl, 0, gsl].unsqueeze(3).broadcast_to([128, 2, 4, 32])
                pi = Pw[:, ksl, 1, gsl].unsqueeze(3).broadcast_to([128, 2, 4, 32])
                cre = CT[:, 0, gsl, :].unsqueeze(1).broadcast_to([128, 2, 4, 32])
                cim = CT[:, 1, gsl, :].unsqueeze(1).broadcast_to([128, 2, 4, 32])
                DVE(lambda e, pr=pr, cre=cre: e.tensor_tensor(out=wyt2[:, 0], in0=cre, in1=pr, op=ALU.mult), r=["CT", "Pw"], w=["wyt"])
                DVE(lambda e, pi=pi, cim=cim: e.tensor_tensor(out=wyt2[:, 1], in0=cim, in1=pi, op=ALU.mult), r=["CT", "Pw", "wyt"], w=["wyt"])
                DVE(lambda e, pi=pi, cre=cre: e.tensor_tensor(out=wyt2[:, 2], in0=cre, in1=pi, op=ALU.mult), r=["CT", "Pw", "wyt"], w=["wyt"])
                DVE(lambda e, pr=pr, cim=cim: e.tensor_tensor(out=wyt2[:, 3], in0=cim, in1=pr, op=ALU.mult), r=["CT", "Pw", "wyt"], w=["wyt"])
                o_re = wyv[:, :, rsl, 0, :].rearrange("p q r c -> p r q c")
                o_im = wyv[:, :, rsl, 1, :].rearrange("p q r c -> p r q c")
                DVE(lambda e, o_re=o_re: e.tensor_tensor(out=o_re, in0=wyt2[:, 0], in1=wyt2[:, 1], op=ALU.subtract), r=["wyt"], w=["wyb%d" % wyi])
                DVE(lambda e: e.tensor_tensor(out=wyt2[:, 2], in0=wyt2[:, 2], in1=wyt2[:, 3], op=ALU.add), r=["wyt"], w=["wyt"])
                DVE(lambda e, o_im=o_im: e.tensor_scalar(out=o_im, in0=wyt2[:, 2], scalar1=-1.0, scalar2=None, op0=ALU.mult), r=["wyt"], w=["wyb%d" % wyi])
            DMA("sp", "wyd%d" % wyi, lambda e, ct=ct, wyi=wyi: e.dma_start(out=YK_d[ct, :, 0:2048], in_=wyb[wyi].rearrange("p a b c -> p (a b c)")), r=["wyb%d" % wyi, "ykb%d" % wyi], w=["WY_d"])

        def rstd_act(x_ap, scale, y_ap, t_ap, res):
            ACT(lambda e: e.activation(out=t_ap, in_=x_ap, func=AF.Ln, scale=scale, bias=cst[:, 0:1]), r=res + ["cst0"], w=res)
            ACT(lambda e: e.activation(out=y_ap, in_=t_ap, func=AF.Exp, scale=-0.5), r=res, w=res)

        def norm_to_T(gT, gname, ntt):
            S.phase = "norm"
            DVE(lambda e: e.memset(st[:, 1, :], 0.0), r=["st"], w=["st"])
            for tt in range(ntt):
                ACT(lambda e, tt=tt: e.activation(out=xnb[:], in_=xh[:, tt, :], func=AF.Square, accum_out=st[:, 1, tt:tt + 1]), r=["xh%d" % tt], w=["xnb", "st"])
            rstd_act(st[:, 1, 0:ntt], 1.0 / D, st[:, 3, 0:ntt], st[:, 4, 0:ntt], ["st"])
            for tt in range(ntt):
                DVE(lambda e, tt=tt: e.tensor_scalar(out=xnb[:], in0=xh[:, tt, :], scalar1=st[:, 3, tt:tt + 1], scalar2=None, op0=ALU.mult), r=["xh%d" % tt, "st"], w=["xnb"])
                for h8 in range(2):
                    pt_, ptn = nt()
                    for j in range(8):
                        k = h8 * 8 + j
                        PE(lambda e, k=k, j=j, pt_=pt_: e.transpose(out=pt_[:, j * 128:(j + 1) * 128], in_=xnb[:, k * 128:(k + 1) * 128], identity=ident[:]), r=["xnb", "ident"], w=[ptn])
                    DVE(lambda e, tt=tt, h8=h8, pt_=pt_: e.tensor_tensor(
                        out=nT[:, h8 * 8:(h8 + 1) * 8, tt * 128:(tt + 1) * 128], in0=pt_[:, :].rearrange("p (k t) -> p k t", t=128),
                        in1=gT[:, h8 * 8:(h8 + 1) * 8].unsqueeze(2).broadcast_to([128, 8, 128]), op=ALU.mult), r=[ptn, gname], w=["nT"])

        def u_proj(Tp):
            S.phase = "uproj"
            for pn in range(2):
                wv, wn = load_panel(w_in, 0, 16, pn * 512, 512)
                for m in range(4):
                    bk, bkn = nb()
                    for k in range(16):
                        PE(lambda e, k=k, m=m, bk=bk, wv=wv: e.matmul(bk[:, 0:Tp], lhsT=wv[:, k, m * 128:(m + 1) * 128], rhs=nT[:, k, 0:Tp], start=(k == 0), stop=(k == 15)), r=["nT"] + wn, w=[bkn])
                    ct = pn * 4 + m
                    ACT(lambda e, ct=ct, bk=bk: e.copy(out=uT[:, ct, 0:Tp], in_=bk[:, 0:Tp]), r=[bkn], w=["uT"])

        def ssm_S(c0, deep=False, lay="ajg"):
            S.phase = "ssmS"
            slots = ws_slots if deep else ws_slots_ctx
            for ct in range(8):
                si = state["ws2"] % 4
                state["ws2"] += 1
                wt, wtn = slots[si]
                DMA("sp", "wsl_" + wtn, lambda e, ct=ct, wt=wt: e.dma_start(out=wt.rearrange("p a b -> p (a b)"), in_=WS_d[ct]), r=["WS_d"], w=[wtn])
                ui = state["u"] % 2
                state["u"] += 1
                u4, u4n = um4s[ui], "um4_%d" % ui
                DVE(lambda e, ct=ct, u4=u4: e.tensor_tensor(out=u4[:].rearrange("p q (r j) -> p q r j", r=8),
                                                         in0=uT[:, ct, c0:c0 + TS].rearrange("p (j r) -> p r j", r=8).unsqueeze(1).broadcast_to([128, 4, 8, NCOL]),
                                                         in1=rmask[:, :].unsqueeze(2).unsqueeze(3).broadcast_to([128, 4, 8, NCOL]), op=ALU.mult), r=["uT", "rmask"], w=[u4n])
                bk, bkn = nb()
                for ri in range(2):
                    for r_ in range(8):
                        kri = (7 - r_) * 2 + ri
                        PE(lambda e, ri=ri, r_=r_, kri=kri, bk=bk, wt=wt, u4=u4: e.matmul(
                            bk[:, ri * 4 * NCOL:(ri + 1) * 4 * NCOL], lhsT=wt[:, kri, :],
                            rhs=u4[:].rearrange("p q (r j) -> p q r j", r=8)[:, :, r_, :],
                            start=(r_ == 0), stop=(r_ == 7)), r=[u4n, wtn], w=[bkn])
                if lay == "agj":
                    so = Ssb[:, :, 4 * ct:4 * ct + 4, :]
                else:
                    so = Ssb[:].rearrange("p a g j -> p (a g j)").rearrange("p (a j g) -> p a j g", a=2, g=32)[:, :, :, 4 * ct:4 * ct + 4].rearrange("p a j q -> p a q j")
                ACT(lambda e, ct=ct, bk=bk, so=so: e.copy(out=so, in_=bk[:, 0:8 * NCOL].rearrange("p (a q j) -> p a q j", a=2, q=4)), r=[bkn], w=["Ssb"])

        def ssm_scan(nseq, J):
            S.phase = "scan"
            hv = Hf[:, 0:2 * 32 * nseq * (J + 1)].rearrange("p (a g s j) -> p a g s j", a=2, g=32, s=nseq)
            sv = Ssb[:].rearrange("p a g (s j) -> p a g s j", s=nseq)
            a8r = A8[:, 0, :].unsqueeze(1).unsqueeze(3).broadcast_to([128, 2, 32, nseq])
            a8i = A8[:, 1, :].unsqueeze(1).unsqueeze(3).broadcast_to([128, 2, 32, nseq])
            ta = sc[:, 0:2, 0:32 * nseq].rearrange("p a (g s) -> p a g s", s=nseq)
            tb = sc[:, 2:4, 0:32 * nseq].rearrange("p a (g s) -> p a g s", s=nseq)
            for j in range(J):
                hj = hv[:, :, :, :, j]
                DVE(lambda e, hj=hj: e.tensor_tensor(out=ta, in0=hj, in1=a8r, op=ALU.mult), r=["Hf", "A8"], w=["sc0"])
                DVE(lambda e, hj=hj: e.tensor_tensor(out=tb, in0=hj, in1=a8i, op=ALU.mult), r=["Hf", "A8"], w=["sc1"])
                DVE(lambda e: e.tensor_tensor(out=ta[:, 0], in0=ta[:, 0], in1=tb[:, 1], op=ALU.subtract), r=["sc0", "sc1"], w=["sc0"])
                DVE(lambda e: e.tensor_tensor(out=ta[:, 1], in0=ta[:, 1], in1=tb[:, 0], op=ALU.add), r=["sc0", "sc1"], w=["sc0"])
                DVE(lambda e, j=j: e.tensor_tensor(out=hv[:, :, :, :, j + 1], in0=ta, in1=sv[:, :, :, :, j], op=ALU.add), r=["sc0", "Ssb", "Hf"], w=["Hf"])
            return hv

        def ssm_scan_blocked(with_hist):
            S.phase = "scan"
            hv = Hf[:, 0:2 * (NCOL + 1) * 32].rearrange("p (a j g) -> p a j g", a=2, g=32)
            Lv = hv[:, :, 0:NCOL, :].rearrange("p a (b i) g -> p a b i g", i=8)
            Sv = Ssb[:].rearrange("p a g j -> p (a g j)").rearrange("p (a j g) -> p a j g", a=2, g=32).rearrange("p a (b i) g -> p a b i g", i=8)
            LT = hst[:, 0:256].rearrange("p (a b g) -> p a b g", a=2, b=4)
            RS = ["scA"]
            a8r = A8[:, 0, :].unsqueeze(1).unsqueeze(2).broadcast_to([128, 2, 4, 32])
            a8i = A8[:, 1, :].unsqueeze(1).unsqueeze(2).broadcast_to([128, 2, 4, 32])
            scf = sc[:].rearrange("p a b -> p (a b)")
            ta = scf[:, 0:256].rearrange("p (a b g) -> p a b g", a=2, b=4)
            tb = scf[:, 256:512].rearrange("p (a b g) -> p a b g", a=2, b=4)
            DVE(lambda e: e.memset(Lv[:, :, :, 0, :], 0.0), r=["Hf", "Hb"], w=["Hf"])
            DVE(lambda e: e.tensor_copy(out=Lv[:, :, :, 1, :], in_=Sv[:, :, :, 0, :]), r=["Ssb", "Hf"], w=["Hf"])
            for i in range(1, 8):
                X = Lv[:, :, :, i, :]
                o_re = Lv[:, 0, :, i + 1, :] if i < 7 else LT[:, 0]
                o_im = Lv[:, 1, :, i + 1, :] if i < 7 else LT[:, 1]
                orn = ["Hf"] if i < 7 else ["hst"]
                DVE(lambda e, X=X: e.tensor_tensor(out=ta, in0=X, in1=a8r, op=ALU.mult), r=["Hf", "A8"] + RS, w=RS)
                DVE(lambda e, X=X: e.tensor_tensor(out=tb, in0=X, in1=a8i, op=ALU.mult), r=["Hf", "A8"] + RS, w=["scB"])
                DVE(lambda e, i=i: e.tensor_tensor(out=ta, in0=ta, in1=Sv[:, :, :, i, :], op=ALU.add), r=RS + ["Ssb"], w=RS)
                DVE(lambda e, o_re=o_re: e.tensor_tensor(out=o_re, in0=ta[:, 0], in1=tb[:, 1], op=ALU.subtract), r=RS + ["scB"] + orn, w=orn)
                DVE(lambda e, o_im=o_im: e.tensor_tensor(out=o_im, in0=ta[:, 1], in1=tb[:, 0], op=ALU.add), r=RS + ["scB"] + orn, w=orn)
            C = scf[:, 0:320].rearrange("p (a b g) -> p a b g", a=2, b=5)
            ca = scf[:, 320:384].rearrange("p (a g) -> p a g", a=2)
            cb = scf[:, 384:448].rearrange("p (a g) -> p a g", a=2)
            a64r = A64[:, 0, :].unsqueeze(1).broadcast_to([128, 2, 32])
            a64i = A64[:, 1, :].unsqueeze(1).broadcast_to([128, 2, 32])
            DVE(lambda e: e.tensor_copy(out=C[:, :, 0, :], in_=hc[:]), r=["hc", "scB", "hst"] + RS, w=RS)
            for b_ in range(4):
                X = C[:, :, b_, :]
                DVE(lambda e, X=X: e.tensor_tensor(out=ca, in0=X, in1=a64r, op=ALU.mult), r=RS + ["A64"], w=["scC"])
                DVE(lambda e, X=X: e.tensor_tensor(out=cb, in0=X, in1=a64i, op=ALU.mult), r=RS + ["A64"], w=["scD"])
                DVE(lambda e, b_=b_: e.tensor_tensor(out=ca, in0=ca, in1=LT[:, :, b_, :], op=ALU.add), r=["scC", "hst"], w=["scC"])
                DVE(lambda e, b_=b_: e.tensor_tensor(out=C[:, 0, b_ + 1, :], in0=ca[:, 0], in1=cb[:, 1], op=ALU.subtract), r=["scC", "scD"] + RS, w=RS)
                DVE(lambda e, b_=b_: e.tensor_tensor(out=C[:, 1, b_ + 1, :], in0=ca[:, 1], in1=cb[:, 0], op=ALU.add), r=["scC", "scD"] + RS, w=RS)
            DVE(lambda e: e.tensor_copy(out=hc[:], in_=C[:, :, 4, :]), r=RS, w=["hc"])
            if with_hist:
                Ssf_ = Ssb[:].rearrange("p a g j -> p (a g j)")
                T1 = Ssf_[:, 0:1024].rearrange("p (b i g) -> p b i g", b=4, i=8)
                T2 = Ssf_[:, 1024:2048].rearrange("p (b i g) -> p b i g", b=4, i=8)
                Cr = C[:, 0, 0:4, :].unsqueeze(2).broadcast_to([128, 4, 8, 32])
                Ci = C[:, 1, 0:4, :].unsqueeze(2).broadcast_to([128, 4, 8, 32])
                Pr = P8t[:, 0, :, :].unsqueeze(1).broadcast_to([128, 4, 8, 32])
                Pi = P8t[:, 1, :, :].unsqueeze(1).broadcast_to([128, 4, 8, 32])
                Lr = hv[:, 0, 0:NCOL, :].rearrange("p (b i) g -> p b i g", i=8)
                Li = hv[:, 1, 0:NCOL, :].rearrange("p (b i) g -> p b i g", i=8)
                DVE(lambda e: e.tensor_tensor(out=T1, in0=Pr, in1=Cr, op=ALU.mult), r=RS + ["P8t", "Ssb", "Hf"], w=["Ssb"])
                DVE(lambda e: e.tensor_tensor(out=T2, in0=Pi, in1=Ci, op=ALU.mult), r=RS + ["P8t", "Ssb"], w=["Ssb"])
                DVE(lambda e: e.tensor_tensor(out=Lr, in0=Lr, in1=T1, op=ALU.add), r=["Ssb", "Hf"], w=["Hf"])
                DVE(lambda e: e.tensor_tensor(out=Lr, in0=Lr, in1=T2, op=ALU.subtract), r=["Ssb", "Hf"], w=["Hf"])
                DVE(lambda e: e.tensor_tensor(out=T1, in0=Pr, in1=Ci, op=ALU.mult), r=RS + ["P8t", "Ssb", "Hf"], w=["Ssb"])
                DVE(lambda e: e.tensor_tensor(out=T2, in0=Pi, in1=Cr, op=ALU.mult), r=RS + ["P8t", "Ssb"], w=["Ssb"])
                DVE(lambda e: e.tensor_tensor(out=Li, in0=Li, in1=T1, op=ALU.add), r=["Ssb", "Hf"], w=["Hf"])
                DVE(lambda e: e.tensor_tensor(out=Li, in0=Li, in1=T2, op=ALU.add), r=["Ssb", "Hf"], w=["Hf"])
                for ri in range(2):
                    ACT(lambda e, ri=ri: e.copy(out=Hb[:, ri], in_=hv[:, ri, 0:NCOL, :].rearrange("p j g -> p g j")), r=["Hf"], w=["Hb"])

        def ssm_y(c0, deep=False):
            S.phase = "ssmY"
            nsl = 4 if deep else 2
            for ct in range(8):
                si = state["yk2"] % nsl
                state["yk2"] += 1
                yt, ytn = yk_slots[si]
                wy = yt[:, 0:2048].rearrange("p (a b c) -> p a b c", a=4, b=16)
                kt = yt[:, 2048:3072].rearrange("p (a b) -> p a b", a=8)
                DMA("sp", "ykl_" + ytn, lambda e, ct=ct, yt=yt: e.dma_start(out=yt, in_=YK_d[ct]), r=["WY_d", "KT_d"], w=[ytn])
                bk, bkn = nb()
                PE(lambda e, ct=ct, bk=bk, kt=kt: e.matmul(bk[:, 0:TS], lhsT=kt[:, 0, :], rhs=uT[:, ct, c0:c0 + TS], start=True, stop=False, skip_group_check=True), r=["uT", ytn], w=[bkn])
                DVE(lambda e, ct=ct: e.tensor_tensor(out=um[:].rearrange("p a (j r) -> p a j r", r=8),
                                                  in0=uT[:, ct, c0:c0 + TS].rearrange("p (j r) -> p j r", r=8).unsqueeze(1).broadcast_to([128, 7, NCOL, 8]),
                                                  in1=maskl[:].unsqueeze(2).broadcast_to([128, 7, NCOL, 8]), op=ALU.mult), r=["uT", "maskl"], w=["um"])
                for q in range(4):
                    for r_ in range(8):
                        for ri in range(2):
                            PE(lambda e, ct=ct, bk=bk, wy=wy, q=q, r_=r_, ri=ri: e.matmul(
                                bk[32 * q:32 * q + 32, 0:TS].rearrange("p (j r) -> p j r", r=8)[:, :, r_], lhsT=wy[:, q, 2 * r_ + ri, :],
                                rhs=Hb[:, ri, 4 * ct + q, :], start=False, stop=False, skip_group_check=True, tile_position=(0, 32 * q)), r=["Hb", ytn], w=[bkn])
                for tau in range(1, 8):
                    PE(lambda e, ct=ct, bk=bk, kt=kt, tau=tau: e.matmul(
                        bk[:, tau:TS], lhsT=kt[:, tau, :], rhs=um[:, tau - 1, 0:TS - tau], start=False, stop=(tau == 7), skip_group_check=True), r=["um", ytn], w=[bkn])
                ACT(lambda e, ct=ct, bk=bk: e.activation(out=geluT[:, ct, c0:c0 + TS], in_=bk[:, 0:TS], func=AF.Gelu_apprx_tanh), r=[bkn], w=["geluT"])

        def glu(Tp):
            S.phase = "glu"
            for pn in range(2):
                wv, wn = load_panel(w_glu, 0, 8, pn * 512, 512)
                for m in range(4):
                    bk, bkn = nb()
                    for k in range(8):
                        PE(lambda e, k=k, m=m, bk=bk, wv=wv: e.matmul(bk[:, 0:Tp], lhsT=wv[:, k, m * 128:(m + 1) * 128], rhs=geluT[:, k, 0:Tp], start=(k == 0), stop=(k == 7)), r=["geluT"] + wn, w=[bkn])
                    co = pn * 4 + m
                    ACT(lambda e, bk=bk: e.activation(out=t2[:, 0:Tp], in_=bk[:, 0:Tp], func=AF.Tanh, scale=0.5), r=[bkn], w=["t2"])
                    DVE(lambda e, co=co: e.scalar_tensor_tensor(out=uT[:, co, 0:Tp], in0=t2[:, 0:Tp], scalar=1.0, in1=geluT[:, co, 0:Tp], op0=ALU.add, op1=ALU.mult), r=["t2", "geluT"], w=["uT"])

        def qk_norm_rope(h0, nh, is_k, rtile, qf, qfn, qkb, qkbn):
            v3 = qf[:, h0 * 64:(h0 + nh) * 64].rearrange("p (h d) -> p h d", d=64)
            t3 = tmpq[:, 0:nh * 64].rearrange("p (h d) -> p h d", d=64)
            DVE(lambda e: e.tensor_tensor(out=t3, in0=v3, in1=v3, op=ALU.mult), r=[qfn], w=["tmpq"])
            DVE(lambda e: e.tensor_reduce(out=st[:, 5, 0:nh], in_=t3, axis=AX.X, op=ALU.add), r=["tmpq"], w=["st"])
            rstd_act(st[:, 5, 0:nh], 1.0 / 64, st[:, 6, 0:nh], st[:, 7, 0:nh], ["st"])
            DVE(lambda e: e.tensor_tensor(out=v3, in0=v3, in1=st[:, 6, 0:nh].unsqueeze(2).broadcast_to([128, nh, 64]), op=ALU.mult), r=[qfn, "st"], w=[qfn])
            gn = knb if is_k else qnb
            gname = "knb" if is_k else "qnb"
            DVE(lambda e: e.tensor_tensor(out=v3, in0=v3, in1=gn[:].unsqueeze(1).broadcast_to([128, nh, 64]), op=ALU.mult), r=[qfn, gname], w=[qfn])
            x1 = v3[:, :, 0:8]
            x2 = v3[:, :, 8:16]
            x12 = v3[:, :, 0:16].rearrange("p h (a d) -> p h a d", a=2)
            cs = ropeT[:, rtile, 0:8].unsqueeze(1).unsqueeze(2).broadcast_to([128, nh, 2, 8])
            sn = ropeT[:, rtile, 8:16].unsqueeze(1).unsqueeze(2).broadcast_to([128, nh, 2, 8])
            rc = rt[:, 0:2, 0:nh * 8].rearrange("p a (h d) -> p h a d", d=8)
            rs = rt[:, 2:4, 0:nh * 8].rearrange("p a (h d) -> p h a d", d=8)
            DVE(lambda e: e.tensor_tensor(out=rc, in0=x12, in1=cs, op=ALU.mult), r=[qfn, "ropeT"], w=["rt"])
            DVE(lambda e: e.tensor_tensor(out=rs, in0=x12, in1=sn, op=ALU.mult), r=[qfn, "ropeT", "rt"], w=["rt"])
            DVE(lambda e: e.tensor_tensor(out=x1, in0=rc[:, :, 0, :], in1=rs[:, :, 1, :], op=ALU.subtract), r=["rt", qfn], w=[qfn])
            DVE(lambda e: e.tensor_tensor(out=x2, in0=rc[:, :, 1, :], in1=rs[:, :, 0, :], op=ALU.add), r=["rt", qfn], w=[qfn])
            ACT(lambda e: e.copy(out=qkb[:, h0 * 64:(h0 + nh) * 64], in_=qf[:, h0 * 64:(h0 + nh) * 64]), r=[qfn], w=[qkbn])

        def qkv_stage(tts, panels, rtile0, kcol0, vtile0, out_fn=None):
            S.phase = "qkv"
            pending = [None]

            def flush():
                if pending[0] is not None:
                    pending[0]()
                    pending[0] = None

            for pn in panels:
                wv, wn = load_panel(w_in, 0, 16, 1024 + pn * 512, 512)
                for tt in tts:
                    qi = state["q"] % 2
                    state["q"] += 1
                    qf, qfn, qkb, qkbn = qfs[qi], "qf%d" % qi, qkbs[qi], "qkb%d" % qi
                    bk, bkn = nb()
                    for k in range(16):
                        PE(lambda e, k=k, tt=tt, bk=bk, wv=wv: e.matmul(bk[:, :], lhsT=nT[:, k, tt * 128:(tt + 1) * 128], rhs=wv[:, k, :], start=(k == 0), stop=(k == 15)), r=["nT"] + wn, w=[bkn])
                    flush()
                    ACT(lambda e, bk=bk, qf=qf: e.copy(out=qf[:], in_=bk[:, :]), r=[bkn], w=[qfn])
                    if pn < 2:
                        qk_norm_rope(0, 8, False, rtile0 + tt, qf, qfn, qkb, qkbn)

                        def tail(tt=tt, pn=pn, qkb=qkb, qkbn=qkbn):
                            pt_, ptn = nt()
                            for j in range(8):
                                PE(lambda e, j=j, pt_=pt_, qkb=qkb: e.transpose(out=pt_[0:64, j * 128:(j + 1) * 128], in_=qkb[:, j * 64:(j + 1) * 64], identity=ident[:]), r=[qkbn, "ident"], w=[ptn])
                            ACT(lambda e, tt=tt, pn=pn, pt_=pt_: e.copy(out=qT[:, 2 * tt:2 * tt + 2, pn * 8:(pn + 1) * 8, :].rearrange("p c h q -> p h c q"),
                                                                 in_=pt_[0:64, :].rearrange("p (h c q) -> p h c q", h=8, c=2)), r=[ptn], w=["qT"])
                        pending[0] = tail
                    else:
                        qk_norm_rope(0, 4, True, rtile0 + tt, qf, qfn, qkb, qkbn)
                        ACT(lambda e, tt=tt, qf=qf: e.copy(out=vb[:, vtile0 + tt, :], in_=qf[:, 256:512]), r=[qfn], w=["vb"])
                        if out_fn is not None:
                            out_fn(tt, qf, qfn)

                        def tail(tt=tt, qkb=qkb, qkbn=qkbn):
                            pt_, ptn = nt()
                            for g in range(4):
                                PE(lambda e, g=g, pt_=pt_, qkb=qkb: e.transpose(out=pt_[0:64, g * 128:(g + 1) * 128], in_=qkb[:, g * 64:(g + 1) * 64], identity=ident[:]), r=[qkbn, "ident"], w=[ptn])
                            ACT(lambda e, tt=tt, pt_=pt_: e.copy(out=kT[:, :, kcol0 + tt * 128:kcol0 + (tt + 1) * 128], in_=pt_[0:64, 0:512].rearrange("p (g t) -> p g t", g=4)), r=[ptn], w=["kT"])
                        pending[0] = tail
            flush()

        def attention_part1(c, g, kA, biasA, kB, biasB):
            S.phase = "attn"
            ai = state["a"] % 2
            state["a"] += 1
            PT, PTn = PTs[ai], "PT%d" % ai
            bk, bkn = nb()
            qv = qT[:, c, 4 * g:4 * g + 4, :].rearrange("p h q -> p (h q)")
            PE(lambda e: e.matmul(bk[:, 0:256], lhsT=kA(g), rhs=qv, start=True, stop=True), r=["qT", "kT", "tA0", "tA1", "tA2", "tA3"], w=[bkn])
            PE(lambda e: e.matmul(bk[:, 256:512], lhsT=kB(g), rhs=qv, start=True, stop=True), r=["qT", "kT"], w=[bkn])
            ACT(lambda e: e.activation(out=PT[:, 0, :], in_=bk[:, 0:256], func=AF.Exp, scale=0.125, bias=bias5[:, biasA:biasA + 1]), r=[bkn, "bias5"], w=[PTn])
            ACT(lambda e: e.activation(out=PT[:, 1, :], in_=bk[:, 256:512], func=AF.Exp, scale=0.125, bias=bias5[:, biasB:biasB + 1]), r=[bkn, "bias5", PTn], w=[PTn])
            return ai

        def attention_part2(c, g, ai, vA, vB):
            PT, PTn, dtmp, dtn = PTs[ai], "PT%d" % ai, dtmps[ai], "dtmp%d" % ai
            b2, b2n = nb()
            for ph in range(2):
                for X, vf in ((0, vA), (1, vB)):
                    rhs = PT[:, X, :].rearrange("p (i a q) -> p i a q", i=2, a=2)[:, :, ph, :]
                    PE(lambda e, ph=ph, X=X, vf=vf, rhs=rhs: e.matmul(b2[64 * ph:64 * ph + 64, 0:128], lhsT=vf(g), rhs=rhs, start=(X == 0), stop=(X == 1), tile_position=(0, 64 * ph)), r=[PTn, "vb", "mixp"], w=[b2n])
                for X in (0, 1):
                    rhs = PT[:, X, :].rearrange("p (i a q) -> p i a q", i=2, a=2)[:, :, ph, :]
                    PE(lambda e, ph=ph, X=X, rhs=rhs: e.matmul(b2[64 * ph:64 * ph + 64, 128:256], lhsT=ones_b[:, :], rhs=rhs, start=(X == 0), stop=(X == 1), tile_position=(0, 64 * ph)), r=[PTn, "ones_b"], w=[b2n])
            DVE(lambda e: e.tensor_tensor(out=dtmp[:].rearrange("p (i q) -> p i q", i=2), in0=b2[:, 128:256].rearrange("p (i q) -> p i q", i=2),
                                          in1=skx[:, 2 * g:2 * g + 2].unsqueeze(2).broadcast_to([128, 2, 64]), op=ALU.add), r=[b2n, "skx"], w=[dtn])
            DVE(lambda e: e.reciprocal(out=dtmp[:], in_=dtmp[:]), r=[dtn], w=[dtn])
            DVE(lambda e: e.tensor_tensor(out=oT[:, 2 * g:2 * g + 2, c * 64:(c + 1) * 64], in0=b2[:, 0:128].rearrange("p (i q) -> p i q", i=2),
                                          in1=dtmp[:].rearrange("p (i q) -> p i q", i=2), op=ALU.mult), r=[b2n, dtn], w=["geluT"])

        def mixer_and_h(Tp, ntt):
            S.phase = "mixer"
            for mp in range(4):
                wv, wn = load_panel(w_gate, 0, 16, mp * 512, 512)
                for m in range(4):
                    bk, bkn = nb()
                    for k in range(16):
                        PE(lambda e, k=k, m=m, bk=bk, wv=wv: e.matmul(bk[:, 0:Tp], lhsT=wv[:, k, m * 128:(m + 1) * 128], rhs=nT[:, k, 0:Tp], start=(k == 0), stop=(k == 15)), r=["nT"] + wn, w=[bkn])
                    ACT(lambda e, m=m, bk=bk: e.activation(out=tA[:, m, 0:Tp], in_=bk[:, 0:Tp], func=AF.Tanh, scale=0.5), r=[bkn], w=["tA%d" % m])
                wv, wn = load_panel(w_brs, 0, 8, mp * 512, 512)
                for m in range(4):
                    bk, bkn = nb()
                    for k in range(8):
                        PE(lambda e, k=k, m=m, bk=bk, wv=wv: e.matmul(bk[:, 0:Tp], lhsT=wv[:, k, m * 128:(m + 1) * 128], rhs=uT[:, k, 0:Tp], start=(k == 0), stop=(k == 7)), r=["uT"] + wn, w=[bkn])
                    DVE(lambda e, m=m, bk=bk: e.scalar_tensor_tensor(out=t1[:, m, 0:Tp], in0=tA[:, m, 0:Tp], scalar=1.0, in1=bk[:, 0:Tp], op0=ALU.add, op1=ALU.mult), r=[bkn, "tA%d" % m], w=["t1_%d" % m])
                wv, wn = load_panel(w_gate, 0, 16, D + mp * 512, 512)
                for m in range(4):
                    bk, bkn = nb()
                    for k in range(16):
                        PE(lambda e, k=k, m=m, bk=bk, wv=wv: e.matmul(bk[:, 0:Tp], lhsT=wv[:, k, m * 128:(m + 1) * 128], rhs=nT[:, k, 0:Tp], start=(k == 0), stop=(k == 15)), r=["nT"] + wn, w=[bkn])
                    ACT(lambda e, m=m, bk=bk: e.activation(out=tA[:, m, 0:Tp], in_=bk[:, 0:Tp], func=AF.Tanh, scale=0.5), r=[bkn], w=["tA%d" % m])
                wv, wn = load_panel(w_bra, 0, 8, mp * 512, 512)
                for m in range(4):
                    bk, bkn = nb()
                    for k in range(8):
                        PE(lambda e, k=k, m=m, bk=bk, wv=wv: e.matmul(bk[:, 0:Tp], lhsT=wv[:, k, m * 128:(m + 1) * 128], rhs=oT[:, k, 0:Tp], start=(k == 0), stop=(k == 7)), r=["geluT"] + wn, w=[bkn])
                    DVE(lambda e, m=m, bk=bk: e.scalar_tensor_tensor(out=t2[:, 0:Tp], in0=tA[:, m, 0:Tp], scalar=1.0, in1=bk[:, 0:Tp], op0=ALU.add, op1=ALU.mult), r=[bkn, "tA%d" % m], w=["t2"])
                    DVE(lambda e, m=m: e.scalar_tensor_tensor(out=mixp[:, m, 0:Tp], in0=t2[:, 0:Tp], scalar=2.0, in1=t1[:, m, 0:Tp], op0=ALU.mult, op1=ALU.add), r=["t2", "t1_%d" % m], w=["mixp"])
                wv, wn = load_panel(w_out, mp * 4, 4, 0, D)
                for tt in range(ntt):
                    for nn in range(4):
                        bk, bkn = nb()
                        for k in range(4):
                            PE(lambda e, k=k, tt=tt, nn=nn, bk=bk, wv=wv: e.matmul(bk[:, :], lhsT=mixp[:, k, tt * 128:(tt + 1) * 128], rhs=wv[:, k, nn * 512:(nn + 1) * 512], start=(k == 0), stop=(k == 3)), r=["mixp"] + wn, w=[bkn])
                        DVE(lambda e, tt=tt, nn=nn, bk=bk: e.scalar_tensor_tensor(out=xh[:, tt, nn * 512:(nn + 1) * 512], in0=bk[:, :], scalar=0.25, in1=xh[:, tt, nn * 512:(nn + 1) * 512], op0=ALU.mult, op1=ALU.add), r=[bkn, "xh%d" % tt], w=["xh%d" % tt])

        def ffn(Tp, ntt):
            S.phase = "ffn"
            for fp in range(11):
                wv, wn = load_panel(w_fg, 0, 16, fp * 512, 512)
                for m in range(4):
                    bk, bkn = nb()
                    for k in range(16):
                        PE(lambda e, k=k, m=m, bk=bk, wv=wv: e.matmul(bk[:, 0:Tp], lhsT=wv[:, k, m * 128:(m + 1) * 128], rhs=nT[:, k, 0:Tp], start=(k == 0), stop=(k == 15)), r=["nT"] + wn, w=[bkn])
                    ACT(lambda e, m=m, bk=bk: e.activation(out=tA[:, m, 0:Tp], in_=bk[:, 0:Tp], func=AF.Tanh, scale=0.5), r=[bkn], w=["tA%d" % m])
                    DVE(lambda e, m=m, bk=bk: e.scalar_tensor_tensor(out=t1[:, m, 0:Tp], in0=tA[:, m, 0:Tp], scalar=1.0, in1=bk[:, 0:Tp], op0=ALU.add, op1=ALU.mult), r=[bkn, "tA%d" % m], w=["t1_%d" % m])
                wv, wn = load_panel(w_fu, 0, 16, fp * 512, 512)
                for m in range(4):
                    bk, bkn = nb()
                    for k in range(16):
                        PE(lambda e, k=k, m=m, bk=bk, wv=wv: e.matmul(bk[:, 0:Tp], lhsT=wv[:, k, m * 128:(m + 1) * 128], rhs=nT[:, k, 0:Tp], start=(k == 0), stop=(k == 15)), r=["nT"] + wn, w=[bkn])
                    DVE(lambda e, m=m, bk=bk: e.tensor_tensor(out=mixp[:, m, 0:Tp], in0=t1[:, m, 0:Tp], in1=bk[:, 0:Tp], op=ALU.mult), r=[bkn, "t1_%d" % m], w=["mixp"])
                wv, wn = load_panel(w_fd, fp * 4, 4, 0, D)
                for tt in range(ntt):
                    for nn in range(4):
                        bk, bkn = nb()
                        for k in range(4):
                            PE(lambda e, k=k, tt=tt, nn=nn, bk=bk, wv=wv: e.matmul(bk[:, :], lhsT=mixp[:, k, tt * 128:(tt + 1) * 128], rhs=wv[:, k, nn * 512:(nn + 1) * 512], start=(k == 0), stop=(k == 3)), r=["mixp"] + wn, w=[bkn])
                        DVE(lambda e, tt=tt, nn=nn, bk=bk: e.scalar_tensor_tensor(out=xh[:, tt, nn * 512:(nn + 1) * 512], in0=bk[:, :], scalar=0.5, in1=xh[:, tt, nn * 512:(nn + 1) * 512], op0=ALU.mult, op1=ALU.add), r=[bkn, "xh%d" % tt], w=["xh%d" % tt])

        xsem = {"n": 0}

        def load_x(src, tok0, Tp, ntt):
            i = xsem["n"] % 2
            xsem["n"] += 1
            for tt in range(ntt):
                DMA("sp", "xl%d_%d" % (i, tt), lambda e, tt=tt: e.dma_start(out=xh[:, tt, :], in_=src[tok0 + tt * 128:tok0 + (tt + 1) * 128, :]), w=["xh%d" % tt])

        def hb_cast(hv, nseq, J):
            for ri in range(2):
                ACT(lambda e, ri=ri: e.copy(out=Hb[:, ri].rearrange("p g (s j) -> p g s j", s=nseq), in_=hv[:, ri, :, :, 0:J]), r=["Hf"], w=["Hb"])

        def ssm_prompt_half(c0, with_y):
            ssm_S(c0, deep=with_y)
            ssm_scan_blocked(with_y)
            if with_y:
                ssm_y(c0, deep=True)

        try:
            register_panels()
            convert_panels(16)
            S.barrier()
            nctx = len(list(range(NPASS_C) if DBG["ctx"] is None else DBG["ctx"]))
            for ci in (range(NPASS_C) if DBG["ctx"] is None else DBG["ctx"]):
                load_x(xc, ci * T, T, NTT)
                stage("c_load")
                norm_to_T(g1T, "g1T", NTT)
                stage("c_norm")
                u_proj(T)
                convert_panels((44 + nctx - 1) // max(nctx, 1))
                stage("c_u")
                for hh in range(T // TS):
                    ssm_prompt_half(hh * TS, False)
                stage("c_scan")
                if ci == NPASS_C - 1:
                    qkv_stage([NTT - 1], [2], 18 - (NTT - 1), 128 - 128 * NTT, 1 - NTT)
                    stage("c_kv")

            convert_panels(len(ptodo))
            pq["q"] = "pool"
            osem = {"n": 0}
            for pi in (range(NPASS_P + 1) if DBG["main"] is None else DBG["main"]):
                sample = (pi == NPASS_P)
                Tp = 256 if sample else T
                ntt = Tp // 128
                tok0 = pi * T
                load_x(xm, tok0, Tp, ntt)
                norm_to_T(g1T, "g1T", ntt)
                u_proj(Tp)
                if not sample:
                    for hh in range(T // TS):
                        ssm_prompt_half(hh * TS, True)
                    if pi == NPASS_P - 1:
                        DMA("sp", "hfin", lambda e: e.dma_start(out=hfin, in_=hc[:].rearrange("p a g -> p (a g)")), r=["hc"])
                else:
                    ssm_S(0, deep=True, lay="agj")
                    hv = Hf[:, 0:2 * 32 * 4 * 9].rearrange("p (a g s j) -> p a g s j", a=2, g=32, s=4)
                    DMA("sp", "st0", lambda e: e.dma_start(out=hst[:], in_=st0), w=["hst"])
                    DVE(lambda e, hv=hv: e.tensor_copy(out=hv[:, :, :, :, 0], in_=hst[:].rearrange("p (a g s) -> p a g s", a=2, g=32)), r=["hst", "Hf", "Hb"], w=["Hf"])
                    hv = ssm_scan(4, 8)
                    hb_cast(hv, 4, 8)
                    DVE(lambda e, hv=hv: e.tensor_copy(out=hst[:].rearrange("p (a g s) -> p a g s", a=2, g=32), in_=hv[:, :, :, :, 8]), r=["Hf", "hst"], w=["hst"])
                    DMA("sp", "hsfin", lambda e: e.dma_start(out=hsfin, in_=hst[:]), r=["hst"])
                    ssm_y(0, deep=True)
                glu(Tp)
                stage("m_glu")

                def out_fn(tt, qf, qfn, pi=pi, sample=sample, ntt=ntt):
                    if (not sample) and pi == NPASS_P - 1 and tt == ntt - 1:
                        DMA("sp", "kwo", lambda e: e.dma_start(out=kwin, in_=qf[:, 0:256]), r=[qfn])
                        DMA("sp", "vwo", lambda e: e.dma_start(out=vwin, in_=qf[:, 256:512]), r=[qfn])
                    if sample:
                        for half in range(2):
                            s = 2 * tt + half
                            DMA("sp", "kso%d" % s, lambda e, half=half, s=s: e.dma_start(out=ks_o[s, 64:128, :], in_=qf[64 * half:64 * half + 64, 0:256]), r=[qfn])
                            DMA("sp", "vso%d" % s, lambda e, half=half, s=s: e.dma_start(out=vs_o[s, 64:128, :], in_=qf[64 * half:64 * half + 64, 256:512]), r=[qfn])
                rt0 = 16 if sample else pi * NTT
                qkv_stage(list(range(ntt)), [0, 1, 2], rt0, 128, 1, out_fn)
                stage("m_qkv")
                if sample:
                    TA_ALL = ["tA0", "tA1", "tA2", "tA3"]
                    for hf in range(2):
                        DMA("sp", "ckl", lambda e, hf=hf: e.dma_start(out=ckf[:], in_=ck[2 * hf:2 * hf + 2].rearrange("s p f -> p s f")), w=["tmpq"])
                        ACT(lambda e, hf=hf: e.copy(out=ckb[:, 2 * hf:2 * hf + 2, :], in_=ckf[:]), r=["tmpq"], w=["um4_0"])
                    for s in range(4):
                        pt_, ptn = nt()
                        for g in range(4):
                            PE(lambda e, s=s, g=g, pt_=pt_: e.transpose(out=pt_[0:64, g * 128:(g + 1) * 128], in_=ckb[:, s, g * 64:(g + 1) * 64], identity=ident[:]), r=["um4_0", "ident"], w=[ptn])
                        ACT(lambda e, s=s, pt_=pt_: e.copy(out=kTs[:, :, s, :], in_=pt_[0:64, 0:512].rearrange("p (g t) -> p g t", g=4)), r=[ptn] + TA_ALL, w=TA_ALL)
                    for hf in range(2):
                        DMA("sp", "cvl", lambda e, hf=hf: e.dma_start(out=ckf[:], in_=cv[2 * hf:2 * hf + 2].rearrange("s p f -> p s f")), r=["um4_0"], w=["tmpq"])
                        ACT(lambda e, hf=hf: e.copy(out=vcs[:, 2 * hf:2 * hf + 2, :], in_=ckf[:]), r=["tmpq"], w=["mixp"])
                    for s in range(4):
                        DMA("sp", "kcp%d" % s, lambda e, s=s: e.dma_start(out=ks_o[s, 0:64, :], in_=ck[s, 64:128, :]))
                        DMA("sp", "vcp%d" % s, lambda e, s=s: e.dma_start(out=vs_o[s, 0:64, :], in_=cv[s, 64:128, :]))
                apend = [None]
                for c in range(Tp // 64):
                    tt, par = c // 2, c % 2
                    if not sample:
                        gc = pi * (T // 64) + c
                        kA = (lambda g, tt=tt: kT[:, g, tt * 128:(tt + 1) * 128])
                        vA = (lambda g, tt=tt: vb[:, tt, g * 64:(g + 1) * 64])
                        if gc == 0:
                            bA = 0
                        elif gc == 1:
                            bA = 1
                        else:
                            bA = 4 if par == 0 else 3
                        bB = 2 if par == 0 else 4
                    else:
                        s = c
                        kA = (lambda g, s=s: kTs[:, g, s, :])
                        vA = (lambda g, s=s: vcs[:, s, g * 64:(g + 1) * 64])
                        bA = 4
                        bB = 2 if par == 0 else 3
                    kB = (lambda g, tt=tt: kT[:, g, 128 + tt * 128:128 + (tt + 1) * 128])
                    vB = (lambda g, tt=tt: vb[:, 1 + tt, g * 64:(g + 1) * 64])
                    for g in range(4):
                        ai = attention_part1(c, g, kA, bA, kB, bB)
                        if apend[0] is not None:
                            apend[0]()
                        apend[0] = (lambda c=c, g=g, ai=ai, vA=vA, vB=vB: attention_part2(c, g, ai, vA, vB))
                if apend[0] is not None:
                    apend[0]()
                    apend[0] = None
                stage("m_attn")
                if not sample:
                    ACT(lambda e: e.copy(out=kT[:, :, 0:128], in_=kT[:, :, T:T + 128]), r=["kT"], w=["kT"])
                    ACT(lambda e: e.copy(out=vb[:, 0, :], in_=vb[:, NTT, :]), r=["vb"], w=["vb"])
                mixer_and_h(Tp, ntt)
                stage("m_mix")
                norm_to_T(g2T, "g2T", ntt)
                ffn(Tp, ntt)
                i = osem["n"] % 2
                osem["n"] += 1
                for tt in range(ntt):
                    DMA("sp", "yo%d_%d" % (i, tt), lambda e, tok0=tok0, tt=tt: e.dma_start(out=ym[tok0 + tt * 128:tok0 + (tt + 1) * 128, :], in_=xh[:, tt, :]), r=["xh%d" % tt])
        except _Stop:
            pass

        S.finalize()
        sems = {e: es.enter_context(nc.semaphore("s_" + e)) for e in Sched.ENGS}
        dsems = {k: es.enter_context(nc.semaphore("d_" + k)) for k in dsem_names}
        with nc.Block() as block:
            @block.tensor
            def _(e):
                S.emit("pe", e, sems, dsems)

            @block.scalar
            def _(e):
                S.emit("act", e, sems, dsems)

            @block.vector
            def _(e):
                S.emit("dve", e, sems, dsems)

            @block.gpsimd
            def _(e):
                S.emit("pool", e, sems, dsems)

            @block.sync
            def _(e):
                S.emit("sp", e, sems, dsems)
                for k, v in S.final_dma.items():
                    e.wait_ge(dsems[k], v)
    _NC_CACHE["S"] = S
    return nc


def _rope_table(pos):
    half = 8
    inv = (np.float32(500000.0) ** (-np.arange(half, dtype=np.float32) * np.float32(2.0) / np.float32(16))).astype(np.float32)
    ang = pos.astype(np.float32)[:, None] * inv[None, :]
    return np.concatenate([np.cos(ang), np.sin(ang)], axis=1).astype(np.float32)


def _prep(x_prompt, x_sample, cache_k, cache_v, state_ssm_re, state_ssm_im,
           norm1, w_in, q_norm, k_norm, sinks,
           ssm_a_re, ssm_a_im, ssm_log_dt, ssm_b_re, ssm_b_im, ssm_c_re, ssm_c_im, ssm_d,
           w_glu, w_br_ssm, w_br_attn, w_gate, w_out, norm2,
           w_ffn_gate, w_ffn_up, w_ffn_down):
    f = lambda a: np.ascontiguousarray(np.asarray(a, dtype=np.float32))
    x_prompt, x_sample = f(x_prompt), f(x_sample)
    cache_k, cache_v = f(cache_k)[0], f(cache_v)[0]
    sre, sim = f(state_ssm_re)[0], f(state_ssm_im)[0]

    def gl_layout(a):
        a = a.reshape((32, 2, 64) + a.shape[2:])
        a = np.moveaxis(a, 0, 2)
        return np.ascontiguousarray(a.reshape((128, 32) + a.shape[3:]))

    shared = {
        "g1T": f(norm1)[0].reshape(16, 128).T.copy(),
        "g2T": f(norm2)[0].reshape(16, 128).T.copy(),
        "qnb": np.ascontiguousarray(np.broadcast_to(f(q_norm)[0][None, :], (128, 64))),
        "knb": np.ascontiguousarray(np.broadcast_to(f(k_norm)[0][None, :], (128, 64))),
        "are": gl_layout(f(ssm_a_re)[0]),
        "aim": gl_layout(f(ssm_a_im)[0]),
        "ldt": gl_layout(np.ascontiguousarray(np.broadcast_to(f(ssm_log_dt)[0][:, None], (64, 64)))),
        "bre": gl_layout(f(ssm_b_re)[0]).reshape(128, 512),
        "bim": gl_layout(f(ssm_b_im)[0]).reshape(128, 512),
        "dsk": f(ssm_d)[0].reshape(8, 128).T.copy(),
        "w_in": f(w_in)[0], "w_glu": f(w_glu)[0], "w_brs": f(w_br_ssm)[0], "w_bra": f(w_br_attn)[0],
        "w_gate": f(w_gate)[0], "w_out": f(w_out)[0], "w_fg": f(w_ffn_gate)[0], "w_fu": f(w_ffn_up)[0], "w_fd": f(w_ffn_down)[0],
    }
    sk = f(sinks)[0]
    skl = np.zeros((128, 8), np.float32)
    skl[0:64, :] = sk[0::2][None, :]
    skl[64:128, :] = sk[1::2][None, :]
    shared["skl"] = skl

    def c_layout(a):
        a = a.reshape(32, 2, 16, 64)
        a = np.transpose(a, (1, 2, 0, 3))
        return np.ascontiguousarray(a.reshape(32, 32 * 64))

    shared["cre"] = c_layout(f(ssm_c_re)[0])
    shared["cim"] = c_layout(f(ssm_c_im)[0])

    in_maps = []
    for c in range(NCORES):
        b, half = c // 2, c % 2
        m = dict(shared)
        m["xm"] = np.concatenate([x_prompt[b, half * 2048:(half + 1) * 2048], x_sample[4 * c:4 * c + 4].reshape(256, D)], axis=0)
        m["xc"] = x_prompt[b, 0:2048] if half == 1 else np.zeros((NCTX, D), np.float32)
        pos = np.concatenate([half * 2048 + np.arange(2048), np.tile(1024 + np.arange(64), 4), half * 2048 - 128 + np.arange(128)])
        tab = _rope_table(pos)
        m["rope"] = np.ascontiguousarray(tab.reshape(19, 128, 16).transpose(1, 0, 2))
        m["ck"] = np.ascontiguousarray(cache_k[4 * c:4 * c + 4].reshape(4, 128, 256))
        m["cv"] = np.ascontiguousarray(cache_v[4 * c:4 * c + 4].reshape(4, 128, 256))
        s0 = np.stack([gl_layout(np.moveaxis(sre[4 * c:4 * c + 4], 0, 2)), gl_layout(np.moveaxis(sim[4 * c:4 * c + 4], 0, 2))], axis=1)
        m["st0"] = np.ascontiguousarray(s0.reshape(128, 256))
        mb = np.zeros((128, 4), np.float32)
        if half == 0:
            mb[:, 0] = BIG
            mb[:, 1] = BIG
        else:
            mb[0:64, 1] = BIG
        mb[64:128, 2] = BIG
        mb[0:64, 3] = BIG
        m["maskb"] = mb
        in_maps.append(m)

    return in_maps


def _gather(R):
    def ungl(a):
        a = a.reshape((2, 64, 32) + a.shape[2:])
        a = np.moveaxis(a, 2, 0)
        return np.ascontiguousarray(a.reshape((64, 64) + a.shape[3:]))

    y_prompt = np.zeros((4, 4096, D), np.float32)
    y_sample = np.zeros((32, 64, D), np.float32)
    kwp = np.zeros((1, 4, 128, 4, 64), np.float32)
    vwp = np.zeros((1, 4, 128, 4, 64), np.float32)
    srp = np.zeros((1, 4, 64, 64), np.float32)
    sip = np.zeros((1, 4, 64, 64), np.float32)
    kws = np.zeros((1, 32, 128, 4, 64), np.float32)
    vws = np.zeros((1, 32, 128, 4, 64), np.float32)
    srs = np.zeros((1, 32, 64, 64), np.float32)
    sis = np.zeros((1, 32, 64, 64), np.float32)
    for c in range(NCORES):
        b, half = c // 2, c % 2
        r = R[c]
        y_prompt[b, half * 2048:(half + 1) * 2048] = r["ym"][0:2048]
        y_sample[4 * c:4 * c + 4] = r["ym"][2048:2304].reshape(4, 64, D)
        if half == 1:
            kwp[0, b] = r["kwin"].reshape(128, 4, 64)
            vwp[0, b] = r["vwin"].reshape(128, 4, 64)
            hf = r["hfin"].reshape(128, 2, 32)
            srp[0, b] = ungl(hf[:, 0])
            sip[0, b] = ungl(hf[:, 1])
        kws[0, 4 * c:4 * c + 4] = r["ks_o"].reshape(4, 128, 4, 64)
        vws[0, 4 * c:4 * c + 4] = r["vs_o"].reshape(4, 128, 4, 64)
        hs = r["hsfin"].reshape(128, 2, 32, 4)
        srs[0, 4 * c:4 * c + 4] = np.moveaxis(ungl(hs[:, 0]), 2, 0)
        sis[0, 4 * c:4 * c + 4] = np.moveaxis(ungl(hs[:, 1]), 2, 0)
    return (y_prompt, y_sample, kwp, vwp, srp, sip, kws, vws, srs, sis)


def kernel(x_prompt, x_sample, cache_k, cache_v, state_ssm_re, state_ssm_im,
           norm1, w_in, q_norm, k_norm, sinks,
           ssm_a_re, ssm_a_im, ssm_log_dt, ssm_b_re, ssm_b_im, ssm_c_re, ssm_c_im, ssm_d,
           w_glu, w_br_ssm, w_br_attn, w_gate, w_out, norm2,
           w_ffn_gate, w_ffn_up, w_ffn_down):
    in_maps = _prep(x_prompt, x_sample, cache_k, cache_v, state_ssm_re, state_ssm_im, norm1, w_in, q_norm, k_norm, sinks, ssm_a_re, ssm_a_im, ssm_log_dt, ssm_b_re, ssm_b_im, ssm_c_re, ssm_c_im, ssm_d, w_glu, w_br_ssm, w_br_attn, w_gate, w_out, norm2, w_ffn_gate, w_ffn_up, w_ffn_down)
    if "nc" not in _NC_CACHE:
        _NC_CACHE["nc"] = build_program()
    nc = _NC_CACHE["nc"]
    res = run_bass_kernel_spmd(nc, in_maps, core_ids=list(range(NCORES)))
    return _gather(res.results)
```

```python
import numpy as np
from contextlib import ExitStack
import concourse.bass as bass
import concourse.mybir as mybir
from concourse.bass_utils import run_bass_kernel_spmd

F32 = mybir.dt.float32
BF16 = mybir.dt.bfloat16
ALU = mybir.AluOpType
AF = mybir.ActivationFunctionType
AX = mybir.AxisListType

NCORES = 8
D = 2048
DS = 1024
DFF = 5632
T = 512
NTT = T // 128
TS = 256
NCOL = TS // 8
NMAIN = 2304
NCTX = 2048
NPASS_P = 2048 // T
NPASS_C = NCTX // T
EPS = 1e-6
BIG = -30000.0


class Op:
    __slots__ = ("eng", "fn", "deps", "signaled", "value", "sem", "is_dma", "idx", "phase")

    def __init__(self, eng, fn, is_dma=False, sem=None):
        self.eng = eng
        self.fn = fn
        self.deps = []
        self.signaled = False
        self.value = 0
        self.sem = sem
        self.is_dma = is_dma


class Res:
    __slots__ = ("w", "r")

    def __init__(self):
        self.w = None
        self.r = {}


class Sched:
    ENGS = ("pe", "act", "dve", "pool", "sp")

    def __init__(self):
        self.ops = {e: [] for e in self.ENGS}
        self.res = {}
        self.n = 0
        self.pending = {e: [] for e in self.ENGS}
        self.phase = "setup"

    def barrier(self):
        lasts = []
        for e in self.ENGS:
            comp = [o for o in self.ops[e] if not o.is_dma]
            if comp:
                lasts.append(comp[-1])
        for e in self.ENGS:
            self.pending[e] = list(lasts)

    def _res(self, k):
        r = self.res.get(k)
        if r is None:
            r = self.res[k] = Res()
        return r

    def op(self, eng, fn, reads=(), writes=(), dma_sem=None):
        o = Op(eng, fn, is_dma=dma_sem is not None, sem=dma_sem)
        o.idx = self.n
        o.phase = self.phase
        self.n += 1
        deps = {}
        for k in reads:
            r = self._res(k)
            if r.w is not None:
                deps[id(r.w)] = r.w
        for k in writes:
            r = self._res(k)
            if r.w is not None:
                deps[id(r.w)] = r.w
            for d in r.r.values():
                deps[id(d)] = d
        if self.pending[eng]:
            for d in self.pending[eng]:
                deps[id(d)] = d
            self.pending[eng] = []
        o.deps = list(deps.values())
        for k in reads:
            r = self._res(k)
            key = ("dma", o.idx) if o.is_dma else eng
            r.r[key] = o
        for k in writes:
            r = self._res(k)
            r.w = o
            r.r = {}
        self.ops[eng].append(o)
        return o

    def finalize(self):
        for e in self.ENGS:
            for o in self.ops[e]:
                for d in o.deps:
                    if d.is_dma:
                        d.signaled = True
                    elif d.eng == "pe" and o.eng == "pe" and not o.is_dma:
                        pass
                    else:
                        d.signaled = True
        cnt = {e: 0 for e in self.ENGS}
        dcnt = {}
        allops = sorted((o for e in self.ENGS for o in self.ops[e]), key=lambda o: o.idx)
        for o in allops:
            if o.is_dma:
                dcnt[o.sem] = dcnt.get(o.sem, 0) + 16
                o.value = dcnt[o.sem]
        for e in self.ENGS:
            for o in self.ops[e]:
                if (not o.is_dma) and o.signaled:
                    cnt[e] += 1
                    o.value = cnt[e]
        self.final_dma = dict(dcnt)

    def emit(self, eng_name, eng, sems, dsems):
        seen = {}
        for o in self.ops[eng_name]:
            need = {}
            for d in o.deps:
                if d.is_dma:
                    key = ("d", d.sem)
                    sem = dsems[d.sem]
                else:
                    if d.eng == "pe" and eng_name == "pe" and not o.is_dma:
                        continue
                    key = ("e", d.eng)
                    sem = sems[d.eng]
                if need.get(key, (None, 0))[1] < d.value:
                    need[key] = (sem, d.value)
            for key, (sem, v) in need.items():
                if seen.get(key, 0) >= v:
                    continue
                eng.wait_ge(sem, v)
                seen[key] = v
            ins = o.fn(eng)
            if o.is_dma:
                ins.then_inc(dsems[o.sem], 16)
            elif o.signaled:
                ins.then_inc(sems[eng_name], 1)


_NC_CACHE = {}
DBG = {"ctx": None, "main": None, "stop": None}


class _Stop(Exception):
    pass


def stage(name):
    if DBG["stop"] == name:
        raise _Stop()


def build_program():
    nc = bass.Bass("TRN2", target_bir_lowering=False)

    def din(name, shape, dt=F32):
        return nc.dram_tensor(name, list(shape), dt, kind="ExternalInput").ap()

    def dout(name, shape, dt=F32):
        return nc.dram_tensor(name, list(shape), dt, kind="ExternalOutput").ap()

    xm = din("xm", [NMAIN, D])
    xc = din("xc", [NCTX, D])
    rope = din("rope", [128, 19, 16])
    ck = din("ck", [4, 128, 256])
    cv = din("cv", [4, 128, 256])
    st0 = din("st0", [128, 2 * 32 * 4])
    maskb = din("maskb", [128, 4])
    g1T_d = din("g1T", [128, 16])
    g2T_d = din("g2T", [128, 16])
    qnb_d = din("qnb", [128, 64])
    knb_d = din("knb", [128, 64])
    skl_d = din("skl", [128, 8])
    are_d = din("are", [128, 32])
    aim_d = din("aim", [128, 32])
    ldt_d = din("ldt", [128, 32])
    bre_d = din("bre", [128, 32 * 16])
    bim_d = din("bim", [128, 32 * 16])
    cre_d = din("cre", [32, 32 * 64])
    cim_d = din("cim", [32, 32 * 64])
    dsk_d = din("dsk", [128, 8])
    w_in = din("w_in", [D, 2560])
    w_glu = din("w_glu", [DS, DS])
    w_brs = din("w_brs", [DS, D])
    w_bra = din("w_bra", [DS, D])
    w_gate = din("w_gate", [D, 2 * D])
    w_out = din("w_out", [D, D])
    w_fg = din("w_fg", [D, DFF])
    w_fu = din("w_fu", [D, DFF])
    w_fd = din("w_fd", [DFF, D])

    ym = dout("ym", [NMAIN, D])
    kwin = dout("kwin", [128, 256])
    vwin = dout("vwin", [128, 256])
    hfin = dout("hfin", [128, 64])
    ks_o = dout("ks_o", [4, 128, 256])
    vs_o = dout("vs_o", [4, 128, 256])
    hsfin = dout("hsfin", [128, 2 * 32 * 4])

    WS_d = nc.dram_tensor("WS_d", [8, 128, 16 * 128], BF16).ap()
    YK_d = nc.dram_tensor("YK_d", [8, 128, 3072], BF16).ap()
    WB_d = nc.dram_tensor("WB_d", [60, 128, 8192], BF16).ap()

    S = Sched()
    es = ExitStack()
    with es:
        def sb(name, shape, dt):
            return es.enter_context(nc.sbuf_tensor(name, list(shape), dt))

        def PE(fn, r=(), w=()):
            return S.op("pe", fn, r, w)

        def ACT(fn, r=(), w=()):
            return S.op("act", fn, r, w)

        def DVE(fn, r=(), w=()):
            return S.op("dve", fn, r, w)

        def POOL(fn, r=(), w=()):
            return S.op("pool", fn, r, w)

        dsem_names = []

        def DMA(eng, sem, fn, r=(), w=()):
            if sem not in dsem_names:
                dsem_names.append(sem)
            return S.op(eng, fn, r, w, dma_sem=sem)

        xh = sb("xh", [128, NTT, D], F32)
        nT = sb("nT", [128, 16, T], BF16)
        xnb = sb("xnb", [128, D], BF16)
        wpf = [sb("wp%d" % i, [128, 8192], BF16) for i in range(2)]
        uT = sb("uT", [128, 8, T], BF16)
        geluT = sb("geluT", [128, 8, T], BF16)
        oT = geluT
        maskl = sb("maskl", [128, 7, 8], BF16)
        um = sb("um", [128, 7, TS], BF16)
        um4s = [sb("um4_%d" % i, [128, 4, TS], BF16) for i in range(2)]
        um4 = um4s[0]
        rmask = sb("rmask", [128, 4], F32)
        mixp = sb("mixp", [128, 4, T], BF16)
        tA = sb("tA", [128, 4, T], BF16)
        t1 = sb("t1", [128, 4, T], BF16)
        t2 = sb("t2", [128, T], F32)
        qfs = [sb("qf%d" % i, [128, 512], F32) for i in range(2)]
        qkbs = [sb("qkb%d" % i, [128, 512], BF16) for i in range(2)]
        tmpq = sb("tmpq", [128, 512], F32)
        rt = sb("rt", [128, 4, 64], F32)
        st = sb("st", [128, 8, 20], F32)
        qT = sb("qT", [64, T // 64, 16, 64], BF16)
        kT = sb("kT", [64, 4, 128 + T], BF16)
        vb = sb("vb", [128, 1 + NTT, 256], BF16)
        kTs = tA[0:64].rearrange("p a b -> p (a b)")[:, 0:2048].rearrange("p (g s t) -> p g s t", g=4, s=4)
        vcs = mixp[:, :, 0:256]
        ckb = um4
        PTs = [sb("PT%d" % i, [128, 2, 256], BF16) for i in range(2)]
        dtmps = [sb("dtmp%d" % i, [128, 128], F32) for i in range(2)]
        Ssb = sb("Ssb", [128, 2, 32, NCOL], F32)
        Hf = sb("Hf", [128, 2 * 32 * 36], F32)
        Hb = sb("Hb", [128, 2, 32, NCOL], BF16)
        hc = sb("hc", [128, 2, 32], F32)
        hst = sb("hst", [128, 2 * 32 * 4], F32)
        sc = sb("sc", [128, 4, 128], F32)
        wsb = [sb("wsb%d" % i, [128, 16, 128], BF16) for i in range(2)]
        ykb = [sb("ykb%d" % i, [128, 3072], BF16) for i in range(2)]
        wyb = [ykb[i][:, 0:2048].rearrange("p (a b c) -> p a b c", a=4, b=16) for i in range(2)]
        ktb = [ykb[i][:, 2048:3072].rearrange("p (a b) -> p a b", a=8) for i in range(2)]
        ws_slots = [(wsb[0], "wsb0"), (wsb[1], "wsb1"),
                    (wpf[0][:, 4096:6144].rearrange("p (a b) -> p a b", a=16), "wp0h"), (wpf[1][:, 4096:6144].rearrange("p (a b) -> p a b", a=16), "wp1h")]
        ws_slots_ctx = [ws_slots[0], ws_slots[1],
                        (ykb[0][:, 0:2048].rearrange("p (a b) -> p a b", a=16), "ykb0"), (ykb[1][:, 0:2048].rearrange("p (a b) -> p a b", a=16), "ykb1")]
        yk_slots = [(ykb[0][:, :], "ykb0"), (ykb[1][:, :], "ykb1"), (wpf[0][:, 4096:7168], "wp0h"), (wpf[1][:, 4096:7168], "wp1h")]
        identf = sb("identf", [128, 128], F32)
        ident = sb("ident", [128, 128], BF16)
        ones_b = sb("ones_b", [128, 64], BF16)
        ropeT = sb("ropeT", [128, 19, 16], F32)
        g1T = sb("g1T_s", [128, 16], F32)
        g2T = sb("g2T_s", [128, 16], F32)
        qnb = sb("qnb_s", [128, 64], F32)
        knb = sb("knb_s", [128, 64], F32)
        skx = sb("skx", [128, 8], F32)
        bias5 = sb("bias5", [128, 5], F32)
        cst = sb("cst", [128, 4], F32)
        dsk = sb("dsk_s", [128, 8], F32)
        A8 = sb("A8", [128, 2, 32], F32)
        P8t = sb("P8t", [128, 2, 32, 8], F32)
        A64 = sb("A64", [128, 2, 32], F32)
        Pw = Hf[:, 0:576].rearrange("p (k a g) -> p k a g", k=9, a=2)
        dz = Hf[:, 576:960].rearrange("p (k g) -> p k g", k=12)
        Ssf = Ssb[:].rearrange("p a g j -> p (a g j)")
        Bn = Ssf[:, 0:1024].rearrange("p (a g c) -> p a g c", a=2, g=32)
        Bb = Ssf[:, 1024:2048].rearrange("p (a g c) -> p a g c", a=2, g=32)
        xhf = xh[:].rearrange("p t d -> p (t d)")
        Cn = xh[0:32].rearrange("p t d -> p (t d)")[:, 0:4096].rearrange("p (a g n) -> p a g n", a=2, g=32)
        ZC = wpf[0][0:32, 0:8192].rearrange("p (a g n) -> p a g n", a=2, g=32)
        CT = xhf[:, 4096:6144].rearrange("p (a g c) -> p a g c", a=2, g=32)
        CTb = wpf[1][:, 0:2048].rearrange("p (a g c) -> p a g c", a=2, g=32)
        Zt = wpf[1][:, 2048:4096].rearrange("p (k c) -> p k c", k=16)
        pbt4 = xhf[:, 6144:7168].rearrange("p (a k q c) -> p a k q c", a=4, k=4, q=4)
        wyt2 = xhf[:, 7168:8192].rearrange("p (a r q c) -> p a r q c", a=4, r=2, q=4)
        ktf = t2[:, 0:128]
        mblk = t2[:, 128:160]
        BD = t2[:, 256:384]

        pbank = [es.enter_context(nc.psum_tensor("pb%d" % i, [128, 512], F32)) for i in range(6)]
        ptb = [es.enter_context(nc.psum_tensor("pt%d" % i, [128, 1024], BF16)) for i in range(2)]
        state = {"b": 0, "t": 0, "w": 0, "ws": 0, "wy": 0, "kt": 0, "a": 0, "q": 0, "u": 0, "ws2": 0, "yk2": 0}

        def nb():
            i = state["b"] % 6
            state["b"] += 1
            return pbank[i], "pb%d" % i

        def nt():
            i = state["t"] % 2
            state["t"] += 1
            return ptb[i], "pt%d" % i

        pidx = {}
        ptodo = []
        pq = {"q": "sp"}

        def panel_key(W, r0, kc, c0, nw):
            return (id(W), r0, kc, c0, nw)

        def register_panels():
            lst = [(w_in, 0, 16, 0, 512), (w_in, 0, 16, 512, 512), (w_glu, 0, 8, 0, 512), (w_glu, 0, 8, 512, 512)]
            lst += [(w_in, 0, 16, 1024 + pn * 512, 512) for pn in range(3)]
            for mp in range(4):
                lst += [(w_gate, 0, 16, mp * 512, 512), (w_brs, 0, 8, mp * 512, 512), (w_gate, 0, 16, D + mp * 512, 512), (w_bra, 0, 8, mp * 512, 512), (w_out, mp * 4, 4, 0, D)]
            for fp in range(11):
                lst += [(w_fg, 0, 16, fp * 512, 512), (w_fu, 0, 16, fp * 512, 512), (w_fd, fp * 4, 4, 0, D)]
            for it in lst:
                pidx[panel_key(*it)] = len(pidx)
                ptodo.append(it)

        def convert_panels(n):
            for _ in range(min(n, len(ptodo))):
                W, r0, kc, c0, nw = ptodo.pop(0)
                idx = pidx[panel_key(W, r0, kc, c0, nw)]
                src = W[r0 * 128:(r0 + kc) * 128, c0:c0 + nw].rearrange("(k p) n -> p k n", p=128)
                dst = WB_d[idx, :, 0:kc * nw].rearrange("p (k n) -> p k n", n=nw)
                DMA("pool", "cv%d" % (idx % 16), lambda e, src=src, dst=dst: e.dma_start(out=dst, in_=src), r=(["WB%d" % (idx - 16)] if idx >= 16 else []), w=["WB%d" % idx])

        def load_panel(W, r0, kc, c0, nw):
            i = state["w"] % 2
            state["w"] += 1
            idx = pidx[panel_key(W, r0, kc, c0, nw)]
            view = wpf[i][:, 0:kc * nw].rearrange("p (k n) -> p k n", n=nw)
            names = ["wp%d" % i] + (["wp%dh" % i] if kc * nw > 4096 else [])
            flat = wpf[i][:, 0:kc * nw]
            src = WB_d[idx, :, 0:kc * nw]
            DMA(pq["q"], "wp%d_%s" % (i, pq["q"]), lambda e: e.dma_start(out=flat, in_=src), r=["WB%d" % idx], w=names)
            return view, names

        ckf = tmpq[:, 0:512].rearrange("p (s f) -> p s f", s=2)
        DMA("sp", "c0", lambda e: e.dma_start(out=ropeT[:], in_=rope), w=["ropeT"])
        DMA("sp", "c1", lambda e: e.dma_start(out=g1T[:], in_=g1T_d), w=["g1T"])
        DMA("sp", "c2", lambda e: e.dma_start(out=g2T[:], in_=g2T_d), w=["g2T"])
        DMA("sp", "c3", lambda e: e.dma_start(out=qnb[:], in_=qnb_d), w=["qnb"])
        DMA("sp", "c4", lambda e: e.dma_start(out=knb[:], in_=knb_d), w=["knb"])
        DMA("sp", "c5", lambda e: e.dma_start(out=skx[:], in_=skl_d), w=["skx"])
        DMA("sp", "c6", lambda e: e.dma_start(out=bias5[:, 0:4], in_=maskb), w=["bias5"])
        DMA("sp", "c7", lambda e: e.dma_start(out=dsk[:], in_=dsk_d), w=["dsk"])
        DMA("sp", "c8", lambda e: e.dma_start(out=dz[:, 0, :], in_=are_d), w=["dz0"])
        DMA("sp", "c9", lambda e: e.dma_start(out=dz[:, 1, :], in_=aim_d), w=["dz1"])
        DMA("sp", "c10", lambda e: e.dma_start(out=dz[:, 2, :], in_=ldt_d), w=["dz2"])
        DMA("sp", "c11", lambda e: e.dma_start(out=Bn[:, 0].rearrange("p a b -> p (a b)"), in_=bre_d), w=["Bn"])
        DMA("sp", "c12", lambda e: e.dma_start(out=Bn[:, 1].rearrange("p a b -> p (a b)"), in_=bim_d), w=["Bn"])
        DMA("sp", "c13", lambda e: e.dma_start(out=Cn[:, 0].rearrange("p a b -> p (a b)"), in_=cre_d), w=["Cn"])
        DMA("sp", "c14", lambda e: e.dma_start(out=Cn[:, 1].rearrange("p a b -> p (a b)"), in_=cim_d), w=["Cn"])

        POOL(lambda e: e.memset(identf[:], 0.0), w=["identf"])
        POOL(lambda e: e.affine_select(out=identf[:], in_=identf[:], pattern=[[-1, 128]], compare_op=ALU.not_equal, fill=1.0, base=0, channel_multiplier=1), r=["identf"], w=["identf"])
        POOL(lambda e: e.memset(ones_b[:], 1.0), w=["ones_b"])
        POOL(lambda e: e.memset(maskl[:], 0.0), w=["maskl"])
        POOL(lambda e: e.memset(rmask[:], 0.0), w=["rmask"])
        for q in range(3):
            POOL(lambda e, q=q: e.memset(rmask[32 * q:32 * q + 32, q:q + 1], 1.0), r=["rmask"], w=["rmask"])
        POOL(lambda e: e.memset(rmask[64:128, 3:4], 1.0), r=["rmask"], w=["rmask"])
        POOL(lambda e: e.memset(rmask[64:96, 3:4], 0.0), r=["rmask"], w=["rmask"])
        for tau in range(1, 8):
            POOL(lambda e, tau=tau: e.memset(maskl[:, tau - 1, 0:8 - tau], 1.0), r=["maskl"], w=["maskl"])
        POOL(lambda e: e.memset(cst[:, 0:1], EPS), w=["cst0"])
        POOL(lambda e: e.memset(cst[:, 1:2], float(np.pi / 2)), w=["cst1"])
        POOL(lambda e: e.memset(hc[:], 0.0), w=["hc"])
        POOL(lambda e: e.memset(mblk[:], 0.0), w=["mblk"])
        POOL(lambda e: e.memset(mblk[0:64, 0:16], 1.0), r=["mblk"], w=["mblk"])
        POOL(lambda e: e.memset(mblk[64:128, 16:32], 1.0), r=["mblk"], w=["mblk"])
        POOL(lambda e: e.memset(BD[:], 0.0), w=["BD"])
        for q in range(4):
            if q < 3:
                POOL(lambda e, q=q: e.memset(BD[32 * q:32 * q + 32, 32 * q:32 * q + 32], 1.0), r=["BD"], w=["BD"])
        POOL(lambda e: e.memset(BD[64:128, 96:128], 1.0), r=["BD"], w=["BD"])
        POOL(lambda e: e.memset(BD[64:96, 96:128], 0.0), r=["BD"], w=["BD"])
        POOL(lambda e: e.memset(Zt[:], 0.0), w=["Zt"])
        DVE(lambda e: e.tensor_copy(out=ident[:], in_=identf[:]), r=["identf"], w=["ident"])

        DVE(lambda e: e.tensor_tensor(out=tmpq[:, 0:64], in0=qnb[:], in1=qnb[:], op=ALU.mult), r=["qnb"], w=["tmpq"])
        DVE(lambda e: e.tensor_reduce(out=st[:, 0, 0:1], in_=tmpq[:, 0:64], axis=AX.X, op=ALU.max), r=["tmpq"], w=["st"])
        DVE(lambda e: e.tensor_tensor(out=tmpq[:, 64:128], in0=knb[:], in1=knb[:], op=ALU.mult), r=["knb"], w=["tmpq"])
        DVE(lambda e: e.tensor_reduce(out=st[:, 0, 1:2], in_=tmpq[:, 64:128], axis=AX.X, op=ALU.max), r=["tmpq", "st"], w=["st"])
        DVE(lambda e: e.tensor_tensor(out=st[:, 0, 2:3], in0=st[:, 0, 0:1], in1=st[:, 0, 1:2], op=ALU.mult), r=["st"], w=["st"])
        ACT(lambda e: e.activation(out=st[:, 0, 3:4], in_=st[:, 0, 2:3], func=AF.Ln, scale=64.0), r=["st"], w=["st"])
        ACT(lambda e: e.activation(out=st[:, 0, 4:5], in_=st[:, 0, 3:4], func=AF.Exp, scale=0.5), r=["st"], w=["st"])
        DVE(lambda e: e.tensor_scalar(out=cst[:, 2:3], in0=st[:, 0, 4:5], scalar1=-1.0, scalar2=None, op0=ALU.mult), r=["st"], w=["cst2"])
        DVE(lambda e: e.tensor_copy(out=bias5[:, 4:5], in_=cst[:, 2:3]), r=["cst2", "bias5"], w=["bias5"])
        DVE(lambda e: e.tensor_scalar(out=bias5[:, 0:4], in0=bias5[:, 0:4], scalar1=cst[:, 2:3], scalar2=None, op0=ALU.add), r=["cst2", "bias5"], w=["bias5"])
        ACT(lambda e: e.activation(out=skx[:], in_=skx[:], func=AF.Exp, bias=cst[:, 2:3], scale=1.0), r=["skx", "cst2"], w=["skx"])

        ACT(lambda e: e.activation(out=dz[:, 3, :], in_=dz[:, 2, :], func=AF.Exp), r=["dz2"], w=["dz3"])
        DVE(lambda e: e.tensor_tensor(out=dz[:, 4, :], in0=dz[:, 0, :], in1=dz[:, 3, :], op=ALU.mult), r=["dz0", "dz3"], w=["dz4"])
        DVE(lambda e: e.tensor_tensor(out=dz[:, 5, :], in0=dz[:, 1, :], in1=dz[:, 3, :], op=ALU.mult), r=["dz1", "dz3"], w=["dz5"])
        ACT(lambda e: e.activation(out=dz[:, 6, :], in_=dz[:, 4, :], func=AF.Exp, scale=1.0 / 32), r=["dz4"], w=["dz6"])
        ACT(lambda e: e.activation(out=dz[:, 7, :], in_=dz[:, 5, :], func=AF.Sin, scale=1.0 / 32), r=["dz5"], w=["dz7"])
        ACT(lambda e: e.activation(out=dz[:, 8, :], in_=dz[:, 5, :], func=AF.Sin, scale=1.0 / 32, bias=cst[:, 1:2]), r=["dz5", "cst1"], w=["dz8"])
        DVE(lambda e: e.tensor_tensor(out=Pw[:, 1, 0, :], in0=dz[:, 6, :], in1=dz[:, 8, :], op=ALU.mult), r=["dz6", "dz8"], w=["Pw"])
        DVE(lambda e: e.tensor_tensor(out=Pw[:, 1, 1, :], in0=dz[:, 6, :], in1=dz[:, 7, :], op=ALU.mult), r=["dz6", "dz7", "Pw"], w=["Pw"])

        def cmul(o_re, o_im, a_re, a_im, b_re, b_im, res):
            DVE(lambda e: e.tensor_tensor(out=dz[:, 9, :], in0=a_re, in1=b_re, op=ALU.mult), r=res, w=["dz9"])
            DVE(lambda e: e.tensor_tensor(out=dz[:, 10, :], in0=a_im, in1=b_im, op=ALU.mult), r=res, w=["dz10"])
            DVE(lambda e: e.tensor_tensor(out=o_re, in0=dz[:, 9, :], in1=dz[:, 10, :], op=ALU.subtract), r=["dz9", "dz10"] + res, w=res)
            DVE(lambda e: e.tensor_tensor(out=dz[:, 9, :], in0=a_re, in1=b_im, op=ALU.mult), r=res, w=["dz9"])
            DVE(lambda e: e.tensor_tensor(out=dz[:, 10, :], in0=a_im, in1=b_re, op=ALU.mult), r=res, w=["dz10"])
            DVE(lambda e: e.tensor_tensor(out=o_im, in0=dz[:, 9, :], in1=dz[:, 10, :], op=ALU.add), r=["dz9", "dz10"] + res, w=res)

        cur, oth = 1, 2
        for _ in range(5):
            cmul(Pw[:, oth, 0, :], Pw[:, oth, 1, :], Pw[:, cur, 0, :], Pw[:, cur, 1, :], Pw[:, cur, 0, :], Pw[:, cur, 1, :], ["Pw"])
            cur, oth = oth, cur
        DVE(lambda e: e.tensor_copy(out=Pw[:, 1], in_=Pw[:, 2]), r=["Pw"], w=["Pw"])
        DVE(lambda e: e.memset(Pw[:, 0, 0, :], 1.0), r=["Pw"], w=["Pw"])
        DVE(lambda e: e.memset(Pw[:, 0, 1, :], 0.0), r=["Pw"], w=["Pw"])
        for k in range(2, 9):
            cmul(Pw[:, k, 0, :], Pw[:, k, 1, :], Pw[:, k - 1, 0, :], Pw[:, k - 1, 1, :], Pw[:, 1, 0, :], Pw[:, 1, 1, :], ["Pw"])
        DVE(lambda e: e.tensor_copy(out=A8[:], in_=Pw[:, 8]), r=["Pw"], w=["A8"])
        DVE(lambda e: e.memset(P8t[:, 0, :, 0], 1.0), w=["P8t"])
        DVE(lambda e: e.memset(P8t[:, 1, :, 0], 0.0), r=["P8t"], w=["P8t"])
        DVE(lambda e: e.tensor_copy(out=P8t[:, :, :, 1], in_=A8[:]), r=["A8", "P8t"], w=["P8t"])
        for i in range(2, 8):
            cmul(P8t[:, 0, :, i], P8t[:, 1, :, i], P8t[:, 0, :, i - 1], P8t[:, 1, :, i - 1], A8[:, 0, :], A8[:, 1, :], ["P8t", "A8"])
        cmul(A64[:, 0, :], A64[:, 1, :], P8t[:, 0, :, 7], P8t[:, 1, :, 7], A8[:, 0, :], A8[:, 1, :], ["P8t", "A8", "A64"])
        DVE(lambda e: e.tensor_scalar(out=dz[:, 2, :], in0=Pw[:, 1, 0, :], scalar1=-1.0, scalar2=None, op0=ALU.add), r=["Pw", "dz2", "dz3"], w=["dz2"])
        DVE(lambda e: e.tensor_tensor(out=dz[:, 3, :], in0=dz[:, 0, :], in1=dz[:, 0, :], op=ALU.mult), r=["dz0", "dz3"], w=["dz3"])
        DVE(lambda e: e.tensor_tensor(out=dz[:, 4, :], in0=dz[:, 1, :], in1=dz[:, 1, :], op=ALU.mult), r=["dz1", "dz4", "dz6"], w=["dz4"])
        DVE(lambda e: e.tensor_tensor(out=dz[:, 3, :], in0=dz[:, 3, :], in1=dz[:, 4, :], op=ALU.add), r=["dz3", "dz4"], w=["dz3"])
        DVE(lambda e: e.reciprocal(out=dz[:, 3, :], in_=dz[:, 3, :]), r=["dz3"], w=["dz3"])
        DVE(lambda e: e.tensor_tensor(out=dz[:, 9, :], in0=dz[:, 2, :], in1=dz[:, 0, :], op=ALU.mult), r=["dz2", "dz0", "dz9"], w=["dz9"])
        DVE(lambda e: e.tensor_tensor(out=dz[:, 10, :], in0=Pw[:, 1, 1, :], in1=dz[:, 1, :], op=ALU.mult), r=["Pw", "dz1", "dz10"], w=["dz10"])
        DVE(lambda e: e.tensor_tensor(out=dz[:, 9, :], in0=dz[:, 9, :], in1=dz[:, 10, :], op=ALU.add), r=["dz9", "dz10"], w=["dz9"])
        DVE(lambda e: e.tensor_tensor(out=dz[:, 6, :], in0=dz[:, 9, :], in1=dz[:, 3, :], op=ALU.mult), r=["dz9", "dz3", "dz6", "dz7", "dz8"], w=["dz6"])
        DVE(lambda e: e.tensor_tensor(out=dz[:, 9, :], in0=Pw[:, 1, 1, :], in1=dz[:, 0, :], op=ALU.mult), r=["Pw", "dz0", "dz9", "dz6"], w=["dz9"])
        DVE(lambda e: e.tensor_tensor(out=dz[:, 10, :], in0=dz[:, 2, :], in1=dz[:, 1, :], op=ALU.mult), r=["dz2", "dz1", "dz10"], w=["dz10"])
        DVE(lambda e: e.tensor_tensor(out=dz[:, 9, :], in0=dz[:, 9, :], in1=dz[:, 10, :], op=ALU.subtract), r=["dz9", "dz10"], w=["dz9"])
        DVE(lambda e: e.tensor_tensor(out=dz[:, 7, :], in0=dz[:, 9, :], in1=dz[:, 3, :], op=ALU.mult), r=["dz9", "dz3", "dz7"], w=["dz7"])

        def bc16(ap2):
            return ap2.unsqueeze(2).broadcast_to([128, 32, 16])

        DVE(lambda e: e.tensor_tensor(out=Bb[:, 0], in0=Bn[:, 0], in1=bc16(dz[:, 6, :]), op=ALU.mult), r=["Bn", "dz6"], w=["Bb"])
        DVE(lambda e: e.tensor_tensor(out=Bb[:, 1], in0=Bn[:, 1], in1=bc16(dz[:, 7, :]), op=ALU.mult), r=["Bn", "dz7", "Bb"], w=["Bb"])
        DVE(lambda e: e.tensor_tensor(out=Bb[:, 0], in0=Bb[:, 0], in1=Bb[:, 1], op=ALU.subtract), r=["Bb"], w=["Bb"])
        DVE(lambda e: e.tensor_tensor(out=Bb[:, 1], in0=Bn[:, 1], in1=bc16(dz[:, 6, :]), op=ALU.mult), r=["Bn", "dz6", "Bb"], w=["Bb"])
        DVE(lambda e: e.tensor_tensor(out=Bn[:, 0], in0=Bn[:, 0], in1=bc16(dz[:, 7, :]), op=ALU.mult), r=["Bn", "dz7"], w=["Bn"])
        DVE(lambda e: e.tensor_tensor(out=Bb[:, 1], in0=Bb[:, 1], in1=Bn[:, 0], op=ALU.add), r=["Bb", "Bn"], w=["Bb"])

        for ri in range(2):
            ACT(lambda e, ri=ri: e.copy(out=ZC[:, ri, :, 0:64], in_=Cn[:, ri]), r=["Cn"], w=["ZC"])
            ACT(lambda e, ri=ri: e.copy(out=ZC[:, ri, :, 64:128], in_=Cn[:, ri]), r=["Cn", "ZC"], w=["ZC"])
        for ri in range(2):
            for half in range(2):
                pt_, ptn = nt()
                for g in range(16):
                    gh = half * 16 + g
                    PE(lambda e, ri=ri, gh=gh, g=g, pt_=pt_: e.transpose(out=pt_[:, g * 32:(g + 1) * 32], in_=ZC[:, ri, gh, :], identity=ident[0:32, 0:32]), r=["ZC", "ident"], w=[ptn])
                DVE(lambda e, ri=ri, half=half, pt_=pt_: e.tensor_tensor(
                    out=CT[:, ri, half * 16:(half + 1) * 16, :], in0=pt_[:, 0:512].rearrange("p (g c) -> p g c", c=32),
                    in1=mblk[:].unsqueeze(1).broadcast_to([128, 16, 32]), op=ALU.mult), r=[ptn, "mblk"], w=["CT"])
        ACT(lambda e: e.copy(out=CTb[:, 0], in_=CT[:, 0]), r=["CT"], w=["CTb"])
        DVE(lambda e: e.tensor_scalar(out=CTb[:, 1], in0=CT[:, 1], scalar1=-1.0, scalar2=None, op0=ALU.mult), r=["CT", "CTb"], w=["CTb"])

        for ct in range(8):
            gsl = slice(4 * ct, 4 * ct + 4)
            Ztv = Zt.rearrange("p (k r) c -> p k r c", r=2)
            for k4 in range(2):
                ksl = slice(4 * k4, 4 * k4 + 4)
                pr = Pw[:, ksl, 0, gsl].unsqueeze(3).broadcast_to([128, 4, 4, 16])
                pi = Pw[:, ksl, 1, gsl].unsqueeze(3).broadcast_to([128, 4, 4, 16])
                bre = Bb[:, 0, gsl, :].unsqueeze(1).broadcast_to([128, 4, 4, 16])
                bim = Bb[:, 1, gsl, :].unsqueeze(1).broadcast_to([128, 4, 4, 16])
                DVE(lambda e, pr=pr, bre=bre: e.tensor_tensor(out=pbt4[:, 0], in0=bre, in1=pr, op=ALU.mult), r=["Bb", "Pw"], w=["pbt"])
                DVE(lambda e, pi=pi, bim=bim: e.tensor_tensor(out=pbt4[:, 1], in0=bim, in1=pi, op=ALU.mult), r=["Bb", "Pw", "pbt"], w=["pbt"])
                DVE(lambda e, pi=pi, bre=bre: e.tensor_tensor(out=pbt4[:, 2], in0=bre, in1=pi, op=ALU.mult), r=["Bb", "Pw", "pbt"], w=["pbt"])
                DVE(lambda e, pr=pr, bim=bim: e.tensor_tensor(out=pbt4[:, 3], in0=bim, in1=pr, op=ALU.mult), r=["Bb", "Pw", "pbt"], w=["pbt"])
                for gl in range(2):
                    ps_ = slice(64 * gl, 64 * gl + 64)
                    zre = Ztv[ps_, ksl, 0, :].rearrange("p k (q g c) -> p k q g c", q=4, g=2)[:, :, :, gl, :]
                    zim = Ztv[ps_, ksl, 1, :].rearrange("p k (q g c) -> p k q g c", q=4, g=2)[:, :, :, gl, :]
                    DVE(lambda e, ps_=ps_, zre=zre: e.tensor_tensor(out=zre, in0=pbt4[ps_, 0], in1=pbt4[ps_, 1], op=ALU.subtract), r=["pbt", "Zt"], w=["Zt"])
                    DVE(lambda e, ps_=ps_, zim=zim: e.tensor_tensor(out=zim, in0=pbt4[ps_, 2], in1=pbt4[ps_, 3], op=ALU.add), r=["pbt", "Zt"], w=["Zt"])
            wsi = state["ws"] % 2
            state["ws"] += 1
            for h8 in range(2):
                pt_, ptn = nt()
                for j in range(8):
                    kri = h8 * 8 + j
                    PE(lambda e, kri=kri, j=j, pt_=pt_: e.transpose(out=pt_[:, j * 128:(j + 1) * 128], in_=Zt[:, kri, :], identity=ident[:]), r=["Zt", "ident"], w=[ptn])
                ACT(lambda e, h8=h8, pt_=pt_, wsi=wsi: e.copy(out=wsb[wsi][:, h8 * 8:(h8 + 1) * 8, :].rearrange("p a b -> p (a b)"), in_=pt_[:, :]), r=[ptn], w=["wsb%d" % wsi])
            DMA("sp", "wsd%d" % wsi, lambda e, ct=ct, wsi=wsi: e.dma_start(out=WS_d[ct], in_=wsb[wsi][:].rearrange("p a b -> p (a b)")), r=["wsb%d" % wsi], w=["WS_d"])
            kti = state["kt"] % 2
            state["kt"] += 1
            for tau in range(8):
                bk, bkn = nb()
                PE(lambda e, tau=tau, bk=bk, gsl=gsl: e.matmul(bk[:, 0:128], lhsT=Zt[:, 2 * tau, :], rhs=CTb[:, 0, gsl, :].rearrange("p a b -> p (a b)"), start=True, stop=False), r=["Zt", "CTb"], w=[bkn])
                PE(lambda e, tau=tau, bk=bk, gsl=gsl: e.matmul(bk[:, 0:128], lhsT=Zt[:, 2 * tau + 1, :], rhs=CTb[:, 1, gsl, :].rearrange("p a b -> p (a b)"), start=False, stop=True), r=["Zt", "CTb"], w=[bkn])
                if tau == 0:
                    DVE(lambda e, bk=bk: e.tensor_tensor(out=ktf[:], in0=bk[:, 0:128], in1=BD[:], op=ALU.mult), r=[bkn, "BD"], w=["ktf"])
                    DVE(lambda e, ct=ct, kti=kti: e.scalar_tensor_tensor(out=ktb[kti][:, 0, :], in0=identf[:], scalar=dsk[:, ct:ct + 1], in1=ktf[:], op0=ALU.mult, op1=ALU.add), r=["ktf", "identf", "dsk"], w=["ktb%d" % kti])
                else:
                    DVE(lambda e, bk=bk, tau=tau, kti=kti: e.tensor_tensor(out=ktb[kti][:, tau, :], in0=bk[:, 0:128], in1=BD[:], op=ALU.mult), r=[bkn, "BD"], w=["ktb%d" % kti])
            DMA("sp", "ktd%d" % kti, lambda e, ct=ct, kti=kti: e.dma_start(out=YK_d[ct, :, 2048:3072], in_=ktb[kti].rearrange("p a b -> p (a b)")), r=["ktb%d" % kti, "ykb%d" % kti], w=["KT_d"])
            wyi = state["wy"] % 2
            state["wy"] += 1
            wyv = wyb[wyi].rearrange("p q (r i) c -> p q r i c", i=2)
            for r2 in range(4):
                rsl = slice(2 * r2, 2 * r2 + 2)
                ksl = slice(2 * r2 + 1, 2 * r2 + 3)
                pr = Pw[:, ksl, 0, gsl].unsqueeze(3).broadcast_to([128, 2, 4, 32])
                pi = Pw[:, ksl, 1, gsl].unsqueeze(3).broadcast_to([128, 2, 4, 32])
                cre = CT[:, 0, gsl, :].unsqueeze(1).broadcast_to([128, 2, 4, 32])
                cim = CT[:, 1, gsl, :].unsqueeze(1).broadcast_to([128, 2, 4, 32])
                DVE(lambda e, pr=pr, cre=cre: e.tensor_tensor(out=wyt2[:, 0], in0=cre, in1=pr, op=ALU.mult), r=["CT", "Pw"], w=["wyt"])
                DVE(lambda e, pi=pi, cim=cim: e.tensor_tensor(out=wyt2[:, 1], in0=cim, in1=pi, op=ALU.mult), r=["CT", "Pw", "wyt"], w=["wyt"])
                DVE(lambda e, pi=pi, cre=cre: e.tensor_tensor(out=wyt2[:, 2], in0=cre, in1=pi, op=ALU.mult), r=["CT", "Pw", "wyt"], w=["wyt"])
                DVE(lambda e, pr=pr, cim=cim: e.tensor_tensor(out=wyt2[:, 3], in0=cim, in1=pr, op=ALU.mult), r=["CT", "Pw", "wyt"], w=["wyt"])
                o_re = wyv[:, :, rsl, 0, :].rearrange("p q r c -> p r q c")
                o_im = wyv[:, :, rsl, 1, :].rearrange("p q r c -> p r q c")
                DVE(lambda e, o_re=o_re: e.tensor_tensor(out=o_re, in0=wyt2[:, 0], in1=wyt2[:, 1], op=ALU.subtract), r=["wyt"], w=["wyb%d" % wyi])
                DVE(lambda e: e.tensor_tensor(out=wyt2[:, 2], in0=wyt2[:, 2], in1=wyt2[:, 3], op=ALU.add), r=["wyt"], w=["wyt"])
                DVE(lambda e, o_im=o_im: e.tensor_scalar(out=o_im, in0=wyt2[:, 2], scalar1=-1.0, scalar2=None, op0=ALU.mult), r=["wyt"], w=["wyb%d" % wyi])
            DMA("sp", "wyd%d" % wyi, lambda e, ct=ct, wyi=wyi: e.dma_start(out=YK_d[ct, :, 0:2048], in_=wyb[wyi].rearrange("p a b c -> p (a b c)")), r=["wyb%d" % wyi, "ykb%d" % wyi], w=["WY_d"])

        def rstd_act(x_ap, scale, y_ap, t_ap, res):
            ACT(lambda e: e.activation(out=t_ap, in_=x_ap, func=AF.Ln, scale=scale, bias=cst[:, 0:1]), r=res + ["cst0"], w=res)
            ACT(lambda e: e.activation(out=y_ap, in_=t_ap, func=AF.Exp, scale=-0.5), r=res, w=res)

        def norm_to_T(gT, gname, ntt):
            S.phase = "norm"
            DVE(lambda e: e.memset(st[:, 1, :], 0.0), r=["st"], w=["st"])
            for tt in range(ntt):
                ACT(lambda e, tt=tt: e.activation(out=xnb[:], in_=xh[:, tt, :], func=AF.Square, accum_out=st[:, 1, tt:tt + 1]), r=["xh"], w=["xnb", "st"])
            rstd_act(st[:, 1, 0:ntt], 1.0 / D, st[:, 3, 0:ntt], st[:, 4, 0:ntt], ["st"])
            for tt in range(ntt):
                DVE(lambda e, tt=tt: e.tensor_scalar(out=xnb[:], in0=xh[:, tt, :], scalar1=st[:, 3, tt:tt + 1], scalar2=None, op0=ALU.mult), r=["xh", "st"], w=["xnb"])
                for h8 in range(2):
                    pt_, ptn = nt()
                    for j in range(8):
                        k = h8 * 8 + j
                        PE(lambda e, k=k, j=j, pt_=pt_: e.transpose(out=pt_[:, j * 128:(j + 1) * 128], in_=xnb[:, k * 128:(k + 1) * 128], identity=ident[:]), r=["xnb", "ident"], w=[ptn])
                    DVE(lambda e, tt=tt, h8=h8, pt_=pt_: e.tensor_tensor(
                        out=nT[:, h8 * 8:(h8 + 1) * 8, tt * 128:(tt + 1) * 128], in0=pt_[:, :].rearrange("p (k t) -> p k t", t=128),
                        in1=gT[:, h8 * 8:(h8 + 1) * 8].unsqueeze(2).broadcast_to([128, 8, 128]), op=ALU.mult), r=[ptn, gname], w=["nT"])

        def u_proj(Tp):
            S.phase = "uproj"
            for pn in range(2):
                wv, wn = load_panel(w_in, 0, 16, pn * 512, 512)
                for m in range(4):
                    bk, bkn = nb()
                    for k in range(16):
                        PE(lambda e, k=k, m=m, bk=bk, wv=wv: e.matmul(bk[:, 0:Tp], lhsT=wv[:, k, m * 128:(m + 1) * 128], rhs=nT[:, k, 0:Tp], start=(k == 0), stop=(k == 15)), r=["nT"] + wn, w=[bkn])
                    ct = pn * 4 + m
                    ACT(lambda e, ct=ct, bk=bk: e.copy(out=uT[:, ct, 0:Tp], in_=bk[:, 0:Tp]), r=[bkn], w=["uT"])

        def ssm_S(c0, deep=False):
            S.phase = "ssmS"
            slots = ws_slots if deep else ws_slots_ctx
            for ct in range(8):
                si = state["ws2"] % 4
                state["ws2"] += 1
                wt, wtn = slots[si]
                DMA("sp", "wsl_" + wtn, lambda e, ct=ct, wt=wt: e.dma_start(out=wt.rearrange("p a b -> p (a b)"), in_=WS_d[ct]), r=["WS_d"], w=[wtn])
                ui = state["u"] % 2
                state["u"] += 1
                u4, u4n = um4s[ui], "um4_%d" % ui
                DVE(lambda e, ct=ct, u4=u4: e.tensor_tensor(out=u4[:].rearrange("p q (r j) -> p q r j", r=8),
                                                         in0=uT[:, ct, c0:c0 + TS].rearrange("p (j r) -> p r j", r=8).unsqueeze(1).broadcast_to([128, 4, 8, NCOL]),
                                                         in1=rmask[:, :].unsqueeze(2).unsqueeze(3).broadcast_to([128, 4, 8, NCOL]), op=ALU.mult), r=["uT", "rmask"], w=[u4n])
                bk, bkn = nb()
                for ri in range(2):
                    for r_ in range(8):
                        kri = (7 - r_) * 2 + ri
                        PE(lambda e, ri=ri, r_=r_, kri=kri, bk=bk, wt=wt, u4=u4: e.matmul(
                            bk[:, ri * 4 * NCOL:(ri + 1) * 4 * NCOL], lhsT=wt[:, kri, :],
                            rhs=u4[:].rearrange("p q (r j) -> p q r j", r=8)[:, :, r_, :],
                            start=(r_ == 0), stop=(r_ == 7)), r=[u4n, wtn], w=[bkn])
                ACT(lambda e, ct=ct, bk=bk: e.copy(out=Ssb[:, :, 4 * ct:4 * ct + 4, :], in_=bk[:, 0:8 * NCOL].rearrange("p (a q j) -> p a q j", a=2, q=4)), r=[bkn], w=["Ssb"])

        def ssm_scan(nseq, J):
            S.phase = "scan"
            hv = Hf[:, 0:2 * 32 * nseq * (J + 1)].rearrange("p (a g s j) -> p a g s j", a=2, g=32, s=nseq)
            sv = Ssb[:].rearrange("p a g (s j) -> p a g s j", s=nseq)
            a8r = A8[:, 0, :].unsqueeze(1).unsqueeze(3).broadcast_to([128, 2, 32, nseq])
            a8i = A8[:, 1, :].unsqueeze(1).unsqueeze(3).broadcast_to([128, 2, 32, nseq])
            ta = sc[:, 0:2, 0:32 * nseq].rearrange("p a (g s) -> p a g s", s=nseq)
            tb = sc[:, 2:4, 0:32 * nseq].rearrange("p a (g s) -> p a g s", s=nseq)
            for j in range(J):
                hj = hv[:, :, :, :, j]
                DVE(lambda e, hj=hj: e.tensor_tensor(out=ta, in0=hj, in1=a8r, op=ALU.mult), r=["Hf", "A8"], w=["sc0"])
                DVE(lambda e, hj=hj: e.tensor_tensor(out=tb, in0=hj, in1=a8i, op=ALU.mult), r=["Hf", "A8"], w=["sc1"])
                DVE(lambda e: e.tensor_tensor(out=ta[:, 0], in0=ta[:, 0], in1=tb[:, 1], op=ALU.subtract), r=["sc0", "sc1"], w=["sc0"])
                DVE(lambda e: e.tensor_tensor(out=ta[:, 1], in0=ta[:, 1], in1=tb[:, 0], op=ALU.add), r=["sc0", "sc1"], w=["sc0"])
                DVE(lambda e, j=j: e.tensor_tensor(out=hv[:, :, :, :, j + 1], in0=ta, in1=sv[:, :, :, :, j], op=ALU.add), r=["sc0", "Ssb", "Hf"], w=["Hf"])
            return hv

        def ssm_scan_blocked(with_hist):
            S.phase = "scan"
            hv4 = Hf[:, 0:2 * 32 * (NCOL + 1)].rearrange("p (a g j) -> p a g j", a=2, g=32)
            Lv = hv4[:, :, :, 0:NCOL].rearrange("p a g (b i) -> p a g b i", i=8)
            Sv = Ssb[:].rearrange("p a g (b i) -> p a g b i", i=8)
            LT = hst[:, 0:256].rearrange("p (a g b) -> p a g b", a=2, g=32)
            RS = ["scA"]
            a8r = A8[:, 0, :].unsqueeze(1).unsqueeze(3).broadcast_to([128, 2, 32, 4])
            a8i = A8[:, 1, :].unsqueeze(1).unsqueeze(3).broadcast_to([128, 2, 32, 4])
            scf = sc[:].rearrange("p a b -> p (a b)")
            ta = scf[:, 0:256].rearrange("p (a g b) -> p a g b", a=2, g=32)
            tb = scf[:, 256:512].rearrange("p (a g b) -> p a g b", a=2, g=32)
            DVE(lambda e: e.memset(Lv[:, :, :, :, 0], 0.0), r=["Hf", "Hb"], w=["Hf"])
            DVE(lambda e: e.tensor_copy(out=Lv[:, :, :, :, 1], in_=Sv[:, :, :, :, 0]), r=["Ssb", "Hf"], w=["Hf"])
            for i in range(1, 8):
                X = Lv[:, :, :, :, i]
                o_re = Lv[:, 0, :, :, i + 1] if i < 7 else LT[:, 0]
                o_im = Lv[:, 1, :, :, i + 1] if i < 7 else LT[:, 1]
                orn = ["Hf"] if i < 7 else ["hst"]
                DVE(lambda e, X=X: e.tensor_tensor(out=ta, in0=X, in1=a8r, op=ALU.mult), r=["Hf", "A8"] + RS, w=RS)
                DVE(lambda e, X=X: e.tensor_tensor(out=tb, in0=X, in1=a8i, op=ALU.mult), r=["Hf", "A8"] + RS, w=["scB"])
                DVE(lambda e, i=i: e.tensor_tensor(out=ta, in0=ta, in1=Sv[:, :, :, :, i], op=ALU.add), r=RS + ["Ssb"], w=RS)
                DVE(lambda e, o_re=o_re: e.tensor_tensor(out=o_re, in0=ta[:, 0], in1=tb[:, 1], op=ALU.subtract), r=RS + ["scB"] + orn, w=orn)
                DVE(lambda e, o_im=o_im: e.tensor_tensor(out=o_im, in0=ta[:, 1], in1=tb[:, 0], op=ALU.add), r=RS + ["scB"] + orn, w=orn)
            C = scf[:, 0:320].rearrange("p (a g b) -> p a g b", a=2, g=32)
            ca = scf[:, 320:384].rearrange("p (a g) -> p a g", a=2)
            cb = scf[:, 384:448].rearrange("p (a g) -> p a g", a=2)
            a64r = A64[:, 0, :].unsqueeze(1).broadcast_to([128, 2, 32])
            a64i = A64[:, 1, :].unsqueeze(1).broadcast_to([128, 2, 32])
            DVE(lambda e: e.tensor_copy(out=C[:, :, :, 0], in_=hc[:]), r=["hc", "scB", "hst"] + RS, w=RS)
            for b_ in range(4):
                X = C[:, :, :, b_]
                DVE(lambda e, X=X: e.tensor_tensor(out=ca, in0=X, in1=a64r, op=ALU.mult), r=RS + ["A64"], w=["scC"])
                DVE(lambda e, X=X: e.tensor_tensor(out=cb, in0=X, in1=a64i, op=ALU.mult), r=RS + ["A64"], w=["scD"])
                DVE(lambda e, b_=b_: e.tensor_tensor(out=ca, in0=ca, in1=LT[:, :, :, b_], op=ALU.add), r=["scC", "hst"], w=["scC"])
                DVE(lambda e, b_=b_: e.tensor_tensor(out=C[:, 0, :, b_ + 1], in0=ca[:, 0], in1=cb[:, 1], op=ALU.subtract), r=["scC", "scD"] + RS, w=RS)
                DVE(lambda e, b_=b_: e.tensor_tensor(out=C[:, 1, :, b_ + 1], in0=ca[:, 1], in1=cb[:, 0], op=ALU.add), r=["scC", "scD"] + RS, w=RS)
            DVE(lambda e: e.tensor_copy(out=hc[:], in_=C[:, :, :, 4]), r=RS, w=["hc"])
            if with_hist:
                Ssf_ = Ssb[:].rearrange("p a g j -> p (a g j)")
                T1 = Ssf_[:, 0:1024].rearrange("p (g b i) -> p g b i", g=32, b=4)
                T2 = Ssf_[:, 1024:2048].rearrange("p (g b i) -> p g b i", g=32, b=4)
                Cr = C[:, 0, :, 0:4].unsqueeze(3).broadcast_to([128, 32, 4, 8])
                Ci = C[:, 1, :, 0:4].unsqueeze(3).broadcast_to([128, 32, 4, 8])
                Pr = P8t[:, 0, :, :].unsqueeze(2).broadcast_to([128, 32, 4, 8])
                Pi = P8t[:, 1, :, :].unsqueeze(2).broadcast_to([128, 32, 4, 8])
                Lr = hv4[:, 0, :, 0:NCOL].rearrange("p g (b i) -> p g b i", i=8)
                Li = hv4[:, 1, :, 0:NCOL].rearrange("p g (b i) -> p g b i", i=8)
                DVE(lambda e: e.tensor_tensor(out=T1, in0=Pr, in1=Cr, op=ALU.mult), r=RS + ["P8t", "Ssb", "Hf"], w=["Ssb"])
                DVE(lambda e: e.tensor_tensor(out=T2, in0=Pi, in1=Ci, op=ALU.mult), r=RS + ["P8t", "Ssb"], w=["Ssb"])
                DVE(lambda e: e.tensor_tensor(out=Lr, in0=Lr, in1=T1, op=ALU.add), r=["Ssb", "Hf"], w=["Hf"])
                DVE(lambda e: e.tensor_tensor(out=Lr, in0=Lr, in1=T2, op=ALU.subtract), r=["Ssb", "Hf"], w=["Hf"])
                DVE(lambda e: e.tensor_tensor(out=T1, in0=Pr, in1=Ci, op=ALU.mult), r=RS + ["P8t", "Ssb", "Hf"], w=["Ssb"])
                DVE(lambda e: e.tensor_tensor(out=T2, in0=Pi, in1=Cr, op=ALU.mult), r=RS + ["P8t", "Ssb"], w=["Ssb"])
                DVE(lambda e: e.tensor_tensor(out=Li, in0=Li, in1=T1, op=ALU.add), r=["Ssb", "Hf"], w=["Hf"])
                DVE(lambda e: e.tensor_tensor(out=Li, in0=Li, in1=T2, op=ALU.add), r=["Ssb", "Hf"], w=["Hf"])
            return Hf[:, 0:2 * 32 * (NCOL + 1)].rearrange("p (a g s j) -> p a g s j", a=2, g=32, s=1)

        def ssm_y(c0, deep=False):
            S.phase = "ssmY"
            nsl = 4 if deep else 2
            for ct in range(8):
                si = state["yk2"] % nsl
                state["yk2"] += 1
                yt, ytn = yk_slots[si]
                wy = yt[:, 0:2048].rearrange("p (a b c) -> p a b c", a=4, b=16)
                kt = yt[:, 2048:3072].rearrange("p (a b) -> p a b", a=8)
                DMA("sp", "ykl_" + ytn, lambda e, ct=ct, yt=yt: e.dma_start(out=yt, in_=YK_d[ct]), r=["WY_d", "KT_d"], w=[ytn])
                bk, bkn = nb()
                PE(lambda e, ct=ct, bk=bk, kt=kt: e.matmul(bk[:, 0:TS], lhsT=kt[:, 0, :], rhs=uT[:, ct, c0:c0 + TS], start=True, stop=False, skip_group_check=True), r=["uT", ytn], w=[bkn])
                DVE(lambda e, ct=ct: e.tensor_tensor(out=um[:].rearrange("p a (j r) -> p a j r", r=8),
                                                  in0=uT[:, ct, c0:c0 + TS].rearrange("p (j r) -> p j r", r=8).unsqueeze(1).broadcast_to([128, 7, NCOL, 8]),
                                                  in1=maskl[:].unsqueeze(2).broadcast_to([128, 7, NCOL, 8]), op=ALU.mult), r=["uT", "maskl"], w=["um"])
                for q in range(4):
                    for r_ in range(8):
                        for ri in range(2):
                            PE(lambda e, ct=ct, bk=bk, wy=wy, q=q, r_=r_, ri=ri: e.matmul(
                                bk[32 * q:32 * q + 32, 0:TS].rearrange("p (j r) -> p j r", r=8)[:, :, r_], lhsT=wy[:, q, 2 * r_ + ri, :],
                                rhs=Hb[:, ri, 4 * ct + q, :], start=False, stop=False, skip_group_check=True, tile_position=(0, 32 * q)), r=["Hb", ytn], w=[bkn])
                for tau in range(1, 8):
                    PE(lambda e, ct=ct, bk=bk, kt=kt, tau=tau: e.matmul(
                        bk[:, tau:TS], lhsT=kt[:, tau, :], rhs=um[:, tau - 1, 0:TS - tau], start=False, stop=(tau == 7), skip_group_check=True), r=["um", ytn], w=[bkn])
                ACT(lambda e, ct=ct, bk=bk: e.activation(out=geluT[:, ct, c0:c0 + TS], in_=bk[:, 0:TS], func=AF.Gelu_apprx_tanh), r=[bkn], w=["geluT"])

        def glu(Tp):
            S.phase = "glu"
            for pn in range(2):
                wv, wn = load_panel(w_glu, 0, 8, pn * 512, 512)
                for m in range(4):
                    bk, bkn = nb()
                    for k in range(8):
                        PE(lambda e, k=k, m=m, bk=bk, wv=wv: e.matmul(bk[:, 0:Tp], lhsT=wv[:, k, m * 128:(m + 1) * 128], rhs=geluT[:, k, 0:Tp], start=(k == 0), stop=(k == 7)), r=["geluT"] + wn, w=[bkn])
                    co = pn * 4 + m
                    ACT(lambda e, bk=bk: e.activation(out=t2[:, 0:Tp], in_=bk[:, 0:Tp], func=AF.Tanh, scale=0.5), r=[bkn], w=["t2"])
                    DVE(lambda e, co=co: e.scalar_tensor_tensor(out=uT[:, co, 0:Tp], in0=t2[:, 0:Tp], scalar=1.0, in1=geluT[:, co, 0:Tp], op0=ALU.add, op1=ALU.mult), r=["t2", "geluT"], w=["uT"])

        def qk_norm_rope(h0, nh, is_k, rtile, qf, qfn, qkb, qkbn):
            v3 = qf[:, h0 * 64:(h0 + nh) * 64].rearrange("p (h d) -> p h d", d=64)
            t3 = tmpq[:, 0:nh * 64].rearrange("p (h d) -> p h d", d=64)
            DVE(lambda e: e.tensor_tensor(out=t3, in0=v3, in1=v3, op=ALU.mult), r=[qfn], w=["tmpq"])
            DVE(lambda e: e.tensor_reduce(out=st[:, 5, 0:nh], in_=t3, axis=AX.X, op=ALU.add), r=["tmpq"], w=["st"])
            rstd_act(st[:, 5, 0:nh], 1.0 / 64, st[:, 6, 0:nh], st[:, 7, 0:nh], ["st"])
            DVE(lambda e: e.tensor_tensor(out=v3, in0=v3, in1=st[:, 6, 0:nh].unsqueeze(2).broadcast_to([128, nh, 64]), op=ALU.mult), r=[qfn, "st"], w=[qfn])
            gn = knb if is_k else qnb
            gname = "knb" if is_k else "qnb"
            DVE(lambda e: e.tensor_tensor(out=v3, in0=v3, in1=gn[:].unsqueeze(1).broadcast_to([128, nh, 64]), op=ALU.mult), r=[qfn, gname], w=[qfn])
            x1 = v3[:, :, 0:8]
            x2 = v3[:, :, 8:16]
            x12 = v3[:, :, 0:16].rearrange("p h (a d) -> p h a d", a=2)
            cs = ropeT[:, rtile, 0:8].unsqueeze(1).unsqueeze(2).broadcast_to([128, nh, 2, 8])
            sn = ropeT[:, rtile, 8:16].unsqueeze(1).unsqueeze(2).broadcast_to([128, nh, 2, 8])
            rc = rt[:, 0:2, 0:nh * 8].rearrange("p a (h d) -> p h a d", d=8)
            rs = rt[:, 2:4, 0:nh * 8].rearrange("p a (h d) -> p h a d", d=8)
            DVE(lambda e: e.tensor_tensor(out=rc, in0=x12, in1=cs, op=ALU.mult), r=[qfn, "ropeT"], w=["rt"])
            DVE(lambda e: e.tensor_tensor(out=rs, in0=x12, in1=sn, op=ALU.mult), r=[qfn, "ropeT", "rt"], w=["rt"])
            DVE(lambda e: e.tensor_tensor(out=x1, in0=rc[:, :, 0, :], in1=rs[:, :, 1, :], op=ALU.subtract), r=["rt", qfn], w=[qfn])
            DVE(lambda e: e.tensor_tensor(out=x2, in0=rc[:, :, 1, :], in1=rs[:, :, 0, :], op=ALU.add), r=["rt", qfn], w=[qfn])
            ACT(lambda e: e.copy(out=qkb[:, h0 * 64:(h0 + nh) * 64], in_=qf[:, h0 * 64:(h0 + nh) * 64]), r=[qfn], w=[qkbn])

        def qkv_stage(tts, panels, rtile0, kcol0, vtile0, out_fn=None):
            S.phase = "qkv"
            pending = [None]

            def flush():
                if pending[0] is not None:
                    pending[0]()
                    pending[0] = None

            for pn in panels:
                wv, wn = load_panel(w_in, 0, 16, 1024 + pn * 512, 512)
                for tt in tts:
                    qi = state["q"] % 2
                    state["q"] += 1
                    qf, qfn, qkb, qkbn = qfs[qi], "qf%d" % qi, qkbs[qi], "qkb%d" % qi
                    bk, bkn = nb()
                    for k in range(16):
                        PE(lambda e, k=k, tt=tt, bk=bk, wv=wv: e.matmul(bk[:, :], lhsT=nT[:, k, tt * 128:(tt + 1) * 128], rhs=wv[:, k, :], start=(k == 0), stop=(k == 15)), r=["nT"] + wn, w=[bkn])
                    flush()
                    ACT(lambda e, bk=bk, qf=qf: e.copy(out=qf[:], in_=bk[:, :]), r=[bkn], w=[qfn])
                    if pn < 2:
                        qk_norm_rope(0, 8, False, rtile0 + tt, qf, qfn, qkb, qkbn)

                        def tail(tt=tt, pn=pn, qkb=qkb, qkbn=qkbn):
                            pt_, ptn = nt()
                            for j in range(8):
                                PE(lambda e, j=j, pt_=pt_, qkb=qkb: e.transpose(out=pt_[0:64, j * 128:(j + 1) * 128], in_=qkb[:, j * 64:(j + 1) * 64], identity=ident[:]), r=[qkbn, "ident"], w=[ptn])
                            ACT(lambda e, tt=tt, pn=pn, pt_=pt_: e.copy(out=qT[:, 2 * tt:2 * tt + 2, pn * 8:(pn + 1) * 8, :].rearrange("p c h q -> p h c q"),
                                                                 in_=pt_[0:64, :].rearrange("p (h c q) -> p h c q", h=8, c=2)), r=[ptn], w=["qT"])
                        pending[0] = tail
                    else:
                        qk_norm_rope(0, 4, True, rtile0 + tt, qf, qfn, qkb, qkbn)
                        ACT(lambda e, tt=tt, qf=qf: e.copy(out=vb[:, vtile0 + tt, :], in_=qf[:, 256:512]), r=[qfn], w=["vb"])
                        if out_fn is not None:
                            out_fn(tt, qf, qfn)

                        def tail(tt=tt, qkb=qkb, qkbn=qkbn):
                            pt_, ptn = nt()
                            for g in range(4):
                                PE(lambda e, g=g, pt_=pt_, qkb=qkb: e.transpose(out=pt_[0:64, g * 128:(g + 1) * 128], in_=qkb[:, g * 64:(g + 1) * 64], identity=ident[:]), r=[qkbn, "ident"], w=[ptn])
                            ACT(lambda e, tt=tt, pt_=pt_: e.copy(out=kT[:, :, kcol0 + tt * 128:kcol0 + (tt + 1) * 128], in_=pt_[0:64, 0:512].rearrange("p (g t) -> p g t", g=4)), r=[ptn], w=["kT"])
                        pending[0] = tail
            flush()

        def attention_part1(c, g, kA, biasA, kB, biasB):
            S.phase = "attn"
            ai = state["a"] % 2
            state["a"] += 1
            PT, PTn = PTs[ai], "PT%d" % ai
            bk, bkn = nb()
            qv = qT[:, c, 4 * g:4 * g + 4, :].rearrange("p h q -> p (h q)")
            PE(lambda e: e.matmul(bk[:, 0:256], lhsT=kA(g), rhs=qv, start=True, stop=True), r=["qT", "kT", "tA0", "tA1", "tA2", "tA3"], w=[bkn])
            PE(lambda e: e.matmul(bk[:, 256:512], lhsT=kB(g), rhs=qv, start=True, stop=True), r=["qT", "kT"], w=[bkn])
            ACT(lambda e: e.activation(out=PT[:, 0, :], in_=bk[:, 0:256], func=AF.Exp, scale=0.125, bias=bias5[:, biasA:biasA + 1]), r=[bkn, "bias5"], w=[PTn])
            ACT(lambda e: e.activation(out=PT[:, 1, :], in_=bk[:, 256:512], func=AF.Exp, scale=0.125, bias=bias5[:, biasB:biasB + 1]), r=[bkn, "bias5", PTn], w=[PTn])
            return ai

        def attention_part2(c, g, ai, vA, vB):
            PT, PTn, dtmp, dtn = PTs[ai], "PT%d" % ai, dtmps[ai], "dtmp%d" % ai
            b2, b2n = nb()
            for ph in range(2):
                for X, vf in ((0, vA), (1, vB)):
                    rhs = PT[:, X, :].rearrange("p (i a q) -> p i a q", i=2, a=2)[:, :, ph, :]
                    PE(lambda e, ph=ph, X=X, vf=vf, rhs=rhs: e.matmul(b2[64 * ph:64 * ph + 64, 0:128], lhsT=vf(g), rhs=rhs, start=(X == 0), stop=(X == 1), tile_position=(0, 64 * ph)), r=[PTn, "vb", "mixp"], w=[b2n])
                for X in (0, 1):
                    rhs = PT[:, X, :].rearrange("p (i a q) -> p i a q", i=2, a=2)[:, :, ph, :]
                    PE(lambda e, ph=ph, X=X, rhs=rhs: e.matmul(b2[64 * ph:64 * ph + 64, 128:256], lhsT=ones_b[:, :], rhs=rhs, start=(X == 0), stop=(X == 1), tile_position=(0, 64 * ph)), r=[PTn, "ones_b"], w=[b2n])
            DVE(lambda e: e.tensor_tensor(out=dtmp[:].rearrange("p (i q) -> p i q", i=2), in0=b2[:, 128:256].rearrange("p (i q) -> p i q", i=2),
                                          in1=skx[:, 2 * g:2 * g + 2].unsqueeze(2).broadcast_to([128, 2, 64]), op=ALU.add), r=[b2n, "skx"], w=[dtn])
            DVE(lambda e: e.reciprocal(out=dtmp[:], in_=dtmp[:]), r=[dtn], w=[dtn])
            DVE(lambda e: e.tensor_tensor(out=oT[:, 2 * g:2 * g + 2, c * 64:(c + 1) * 64], in0=b2[:, 0:128].rearrange("p (i q) -> p i q", i=2),
                                          in1=dtmp[:].rearrange("p (i q) -> p i q", i=2), op=ALU.mult), r=[b2n, dtn], w=["geluT"])

        def mixer_and_h(Tp, ntt):
            S.phase = "mixer"
            for mp in range(4):
                wv, wn = load_panel(w_gate, 0, 16, mp * 512, 512)
                for m in range(4):
                    bk, bkn = nb()
                    for k in range(16):
                        PE(lambda e, k=k, m=m, bk=bk, wv=wv: e.matmul(bk[:, 0:Tp], lhsT=wv[:, k, m * 128:(m + 1) * 128], rhs=nT[:, k, 0:Tp], start=(k == 0), stop=(k == 15)), r=["nT"] + wn, w=[bkn])
                    ACT(lambda e, m=m, bk=bk: e.activation(out=tA[:, m, 0:Tp], in_=bk[:, 0:Tp], func=AF.Tanh, scale=0.5), r=[bkn], w=["tA%d" % m])
                wv, wn = load_panel(w_brs, 0, 8, mp * 512, 512)
                for m in range(4):
                    bk, bkn = nb()
                    for k in range(8):
                        PE(lambda e, k=k, m=m, bk=bk, wv=wv: e.matmul(bk[:, 0:Tp], lhsT=wv[:, k, m * 128:(m + 1) * 128], rhs=uT[:, k, 0:Tp], start=(k == 0), stop=(k == 7)), r=["uT"] + wn, w=[bkn])
                    DVE(lambda e, m=m, bk=bk: e.scalar_tensor_tensor(out=t1[:, m, 0:Tp], in0=tA[:, m, 0:Tp], scalar=1.0, in1=bk[:, 0:Tp], op0=ALU.add, op1=ALU.mult), r=[bkn, "tA%d" % m], w=["t1_%d" % m])
                wv, wn = load_panel(w_gate, 0, 16, D + mp * 512, 512)
                for m in range(4):
                    bk, bkn = nb()
                    for k in range(16):
                        PE(lambda e, k=k, m=m, bk=bk, wv=wv: e.matmul(bk[:, 0:Tp], lhsT=wv[:, k, m * 128:(m + 1) * 128], rhs=nT[:, k, 0:Tp], start=(k == 0), stop=(k == 15)), r=["nT"] + wn, w=[bkn])
                    ACT(lambda e, m=m, bk=bk: e.activation(out=tA[:, m, 0:Tp], in_=bk[:, 0:Tp], func=AF.Tanh, scale=0.5), r=[bkn], w=["tA%d" % m])
                wv, wn = load_panel(w_bra, 0, 8, mp * 512, 512)
                for m in range(4):
                    bk, bkn = nb()
                    for k in range(8):
                        PE(lambda e, k=k, m=m, bk=bk, wv=wv: e.matmul(bk[:, 0:Tp], lhsT=wv[:, k, m * 128:(m + 1) * 128], rhs=oT[:, k, 0:Tp], start=(k == 0), stop=(k == 7)), r=["geluT"] + wn, w=[bkn])
                    DVE(lambda e, m=m, bk=bk: e.scalar_tensor_tensor(out=t2[:, 0:Tp], in0=tA[:, m, 0:Tp], scalar=1.0, in1=bk[:, 0:Tp], op0=ALU.add, op1=ALU.mult), r=[bkn, "tA%d" % m], w=["t2"])
                    DVE(lambda e, m=m: e.scalar_tensor_tensor(out=mixp[:, m, 0:Tp], in0=t2[:, 0:Tp], scalar=2.0, in1=t1[:, m, 0:Tp], op0=ALU.mult, op1=ALU.add), r=["t2", "t1_%d" % m], w=["mixp"])
                wv, wn = load_panel(w_out, mp * 4, 4, 0, D)
                for tt in range(ntt):
                    for nn in range(4):
                        bk, bkn = nb()
                        for k in range(4):
                            PE(lambda e, k=k, tt=tt, nn=nn, bk=bk, wv=wv: e.matmul(bk[:, :], lhsT=mixp[:, k, tt * 128:(tt + 1) * 128], rhs=wv[:, k, nn * 512:(nn + 1) * 512], start=(k == 0), stop=(k == 3)), r=["mixp"] + wn, w=[bkn])
                        DVE(lambda e, tt=tt, nn=nn, bk=bk: e.scalar_tensor_tensor(out=xh[:, tt, nn * 512:(nn + 1) * 512], in0=bk[:, :], scalar=0.25, in1=xh[:, tt, nn * 512:(nn + 1) * 512], op0=ALU.mult, op1=ALU.add), r=[bkn, "xh"], w=["xh"])

        def ffn(Tp, ntt):
            S.phase = "ffn"
            for fp in range(11):
                wv, wn = load_panel(w_fg, 0, 16, fp * 512, 512)
                for m in range(4):
                    bk, bkn = nb()
                    for k in range(16):
                        PE(lambda e, k=k, m=m, bk=bk, wv=wv: e.matmul(bk[:, 0:Tp], lhsT=wv[:, k, m * 128:(m + 1) * 128], rhs=nT[:, k, 0:Tp], start=(k == 0), stop=(k == 15)), r=["nT"] + wn, w=[bkn])
                    ACT(lambda e, m=m, bk=bk: e.activation(out=tA[:, m, 0:Tp], in_=bk[:, 0:Tp], func=AF.Tanh, scale=0.5), r=[bkn], w=["tA%d" % m])
                    DVE(lambda e, m=m, bk=bk: e.scalar_tensor_tensor(out=t1[:, m, 0:Tp], in0=tA[:, m, 0:Tp], scalar=1.0, in1=bk[:, 0:Tp], op0=ALU.add, op1=ALU.mult), r=[bkn, "tA%d" % m], w=["t1_%d" % m])
                wv, wn = load_panel(w_fu, 0, 16, fp * 512, 512)
                for m in range(4):
                    bk, bkn = nb()
                    for k in range(16):
                        PE(lambda e, k=k, m=m, bk=bk, wv=wv: e.matmul(bk[:, 0:Tp], lhsT=wv[:, k, m * 128:(m + 1) * 128], rhs=nT[:, k, 0:Tp], start=(k == 0), stop=(k == 15)), r=["nT"] + wn, w=[bkn])
                    DVE(lambda e, m=m, bk=bk: e.tensor_tensor(out=mixp[:, m, 0:Tp], in0=t1[:, m, 0:Tp], in1=bk[:, 0:Tp], op=ALU.mult), r=[bkn, "t1_%d" % m], w=["mixp"])
                wv, wn = load_panel(w_fd, fp * 4, 4, 0, D)
                for tt in range(ntt):
                    for nn in range(4):
                        bk, bkn = nb()
                        for k in range(4):
                            PE(lambda e, k=k, tt=tt, nn=nn, bk=bk, wv=wv: e.matmul(bk[:, :], lhsT=mixp[:, k, tt * 128:(tt + 1) * 128], rhs=wv[:, k, nn * 512:(nn + 1) * 512], start=(k == 0), stop=(k == 3)), r=["mixp"] + wn, w=[bkn])
                        DVE(lambda e, tt=tt, nn=nn, bk=bk: e.scalar_tensor_tensor(out=xh[:, tt, nn * 512:(nn + 1) * 512], in0=bk[:, :], scalar=0.5, in1=xh[:, tt, nn * 512:(nn + 1) * 512], op0=ALU.mult, op1=ALU.add), r=[bkn, "xh"], w=["xh"])

        xsem = {"n": 0}

        def load_x(src, tok0, Tp, ntt):
            i = xsem["n"] % 2
            xsem["n"] += 1
            DMA("sp", "xl%d" % i, lambda e: e.dma_start(out=xh[:, 0:ntt, :], in_=src[tok0:tok0 + Tp, :].rearrange("(t p) d -> p t d", p=128)), w=["xh"])

        def hb_cast(hv, nseq, J):
            for ri in range(2):
                ACT(lambda e, ri=ri: e.copy(out=Hb[:, ri].rearrange("p g (s j) -> p g s j", s=nseq), in_=hv[:, ri, :, :, 0:J]), r=["Hf"], w=["Hb"])

        def ssm_prompt_half(c0, with_y):
            ssm_S(c0, deep=with_y)
            hv = ssm_scan_blocked(with_y)
            if with_y:
                hb_cast(hv, 1, NCOL)
                ssm_y(c0, deep=True)

        try:
            register_panels()
            convert_panels(60)
            S.barrier()
            nctx = len(list(range(NPASS_C) if DBG["ctx"] is None else DBG["ctx"]))
            for ci in (range(NPASS_C) if DBG["ctx"] is None else DBG["ctx"]):
                load_x(xc, ci * T, T, NTT)
                stage("c_load")
                norm_to_T(g1T, "g1T", NTT)
                stage("c_norm")
                u_proj(T)
                convert_panels((44 + nctx - 1) // max(nctx, 1))
                stage("c_u")
                for hh in range(T // TS):
                    ssm_prompt_half(hh * TS, False)
                stage("c_scan")
                if ci == NPASS_C - 1:
                    qkv_stage([NTT - 1], [2], 18 - (NTT - 1), 128 - 128 * NTT, 1 - NTT)
                    stage("c_kv")

            convert_panels(len(ptodo))
            pq["q"] = "pool"
            osem = {"n": 0}
            for pi in (range(NPASS_P + 1) if DBG["main"] is None else DBG["main"]):
                sample = (pi == NPASS_P)
                Tp = 256 if sample else T
                ntt = Tp // 128
                tok0 = pi * T
                load_x(xm, tok0, Tp, ntt)
                norm_to_T(g1T, "g1T", ntt)
                u_proj(Tp)
                if not sample:
                    for hh in range(T // TS):
                        ssm_prompt_half(hh * TS, True)
                    if pi == NPASS_P - 1:
                        DMA("sp", "hfin", lambda e: e.dma_start(out=hfin, in_=hc[:].rearrange("p a g -> p (a g)")), r=["hc"])
                else:
                    ssm_S(0, deep=True)
                    hv = Hf[:, 0:2 * 32 * 4 * 9].rearrange("p (a g s j) -> p a g s j", a=2, g=32, s=4)
                    DMA("sp", "st0", lambda e: e.dma_start(out=hst[:], in_=st0), w=["hst"])
                    DVE(lambda e, hv=hv: e.tensor_copy(out=hv[:, :, :, :, 0], in_=hst[:].rearrange("p (a g s) -> p a g s", a=2, g=32)), r=["hst", "Hf", "Hb"], w=["Hf"])
                    hv = ssm_scan(4, 8)
                    hb_cast(hv, 4, 8)
                    DVE(lambda e, hv=hv: e.tensor_copy(out=hst[:].rearrange("p (a g s) -> p a g s", a=2, g=32), in_=hv[:, :, :, :, 8]), r=["Hf", "hst"], w=["hst"])
                    DMA("sp", "hsfin", lambda e: e.dma_start(out=hsfin, in_=hst[:]), r=["hst"])
                    ssm_y(0, deep=True)
                glu(Tp)
                stage("m_glu")

                def out_fn(tt, qf, qfn, pi=pi, sample=sample, ntt=ntt):
                    if (not sample) and pi == NPASS_P - 1 and tt == ntt - 1:
                        DMA("sp", "kwo", lambda e: e.dma_start(out=kwin, in_=qf[:, 0:256]), r=[qfn])
                        DMA("sp", "vwo", lambda e: e.dma_start(out=vwin, in_=qf[:, 256:512]), r=[qfn])
                    if sample:
                        for half in range(2):
                            s = 2 * tt + half
                            DMA("sp", "kso%d" % s, lambda e, half=half, s=s: e.dma_start(out=ks_o[s, 64:128, :], in_=qf[64 * half:64 * half + 64, 0:256]), r=[qfn])
                            DMA("sp", "vso%d" % s, lambda e, half=half, s=s: e.dma_start(out=vs_o[s, 64:128, :], in_=qf[64 * half:64 * half + 64, 256:512]), r=[qfn])
                rt0 = 16 if sample else pi * NTT
                qkv_stage(list(range(ntt)), [0, 1, 2], rt0, 128, 1, out_fn)
                stage("m_qkv")
                if sample:
                    TA_ALL = ["tA0", "tA1", "tA2", "tA3"]
                    for hf in range(2):
                        DMA("sp", "ckl", lambda e, hf=hf: e.dma_start(out=ckf[:], in_=ck[2 * hf:2 * hf + 2].rearrange("s p f -> p s f")), w=["tmpq"])
                        ACT(lambda e, hf=hf: e.copy(out=ckb[:, 2 * hf:2 * hf + 2, :], in_=ckf[:]), r=["tmpq"], w=["um4_0"])
                    for s in range(4):
                        pt_, ptn = nt()
                        for g in range(4):
                            PE(lambda e, s=s, g=g, pt_=pt_: e.transpose(out=pt_[0:64, g * 128:(g + 1) * 128], in_=ckb[:, s, g * 64:(g + 1) * 64], identity=ident[:]), r=["um4_0", "ident"], w=[ptn])
                        ACT(lambda e, s=s, pt_=pt_: e.copy(out=kTs[:, :, s, :], in_=pt_[0:64, 0:512].rearrange("p (g t) -> p g t", g=4)), r=[ptn] + TA_ALL, w=TA_ALL)
                    for hf in range(2):
                        DMA("sp", "cvl", lambda e, hf=hf: e.dma_start(out=ckf[:], in_=cv[2 * hf:2 * hf + 2].rearrange("s p f -> p s f")), r=["um4_0"], w=["tmpq"])
                        ACT(lambda e, hf=hf: e.copy(out=vcs[:, 2 * hf:2 * hf + 2, :], in_=ckf[:]), r=["tmpq"], w=["mixp"])
                    for s in range(4):
                        DMA("sp", "kcp%d" % s, lambda e, s=s: e.dma_start(out=ks_o[s, 0:64, :], in_=ck[s, 64:128, :]))
                        DMA("sp", "vcp%d" % s, lambda e, s=s: e.dma_start(out=vs_o[s, 0:64, :], in_=cv[s, 64:128, :]))
                apend = [None]
                for c in range(Tp // 64):
                    tt, par = c // 2, c % 2
                    if not sample:
                        gc = pi * (T // 64) + c
                        kA = (lambda g, tt=tt: kT[:, g, tt * 128:(tt + 1) * 128])
                        vA = (lambda g, tt=tt: vb[:, tt, g * 64:(g + 1) * 64])
                        if gc == 0:
                            bA = 0
                        elif gc == 1:
                            bA = 1
                        else:
                            bA = 4 if par == 0 else 3
                        bB = 2 if par == 0 else 4
                    else:
                        s = c
                        kA = (lambda g, s=s: kTs[:, g, s, :])
                        vA = (lambda g, s=s: vcs[:, s, g * 64:(g + 1) * 64])
                        bA = 4
                        bB = 2 if par == 0 else 3
                    kB = (lambda g, tt=tt: kT[:, g, 128 + tt * 128:128 + (tt + 1) * 128])
                    vB = (lambda g, tt=tt: vb[:, 1 + tt, g * 64:(g + 1) * 64])
                    for g in range(4):
                        ai = attention_part1(c, g, kA, bA, kB, bB)
                        if apend[0] is not None:
                            apend[0]()
                        apend[0] = (lambda c=c, g=g, ai=ai, vA=vA, vB=vB: attention_part2(c, g, ai, vA, vB))
                if apend[0] is not None:
                    apend[0]()
                    apend[0] = None
                stage("m_attn")
                if not sample:
                    ACT(lambda e: e.copy(out=kT[:, :, 0:128], in_=kT[:, :, T:T + 128]), r=["kT"], w=["kT"])
                    ACT(lambda e: e.copy(out=vb[:, 0, :], in_=vb[:, NTT, :]), r=["vb"], w=["vb"])
                mixer_and_h(Tp, ntt)
                stage("m_mix")
                norm_to_T(g2T, "g2T", ntt)
                ffn(Tp, ntt)
                i = osem["n"] % 2
                osem["n"] += 1
                DMA("sp", "yo%d" % i, lambda e, tok0=tok0, Tp=Tp, ntt=ntt: e.dma_start(out=ym[tok0:tok0 + Tp, :].rearrange("(t p) d -> p t d", p=128), in_=xh[:, 0:ntt, :]), r=["xh"])
        except _Stop:
            pass

        S.finalize()
        sems = {e: es.enter_context(nc.semaphore("s_" + e)) for e in Sched.ENGS}
        dsems = {k: es.enter_context(nc.semaphore("d_" + k)) for k in dsem_names}
        with nc.Block() as block:
            @block.tensor
            def _(e):
                S.emit("pe", e, sems, dsems)

            @block.scalar
            def _(e):
                S.emit("act", e, sems, dsems)

            @block.vector
            def _(e):
                S.emit("dve", e, sems, dsems)

            @block.gpsimd
            def _(e):
                S.emit("pool", e, sems, dsems)

            @block.sync
            def _(e):
                S.emit("sp", e, sems, dsems)
                for k, v in S.final_dma.items():
                    e.wait_ge(dsems[k], v)
    _NC_CACHE["S"] = S
    return nc


def _rope_table(pos):
    half = 8
    inv = (np.float32(500000.0) ** (-np.arange(half, dtype=np.float32) * np.float32(2.0) / np.float32(16))).astype(np.float32)
    ang = pos.astype(np.float32)[:, None] * inv[None, :]
    return np.concatenate([np.cos(ang), np.sin(ang)], axis=1).astype(np.float32)


def _prep(x_prompt, x_sample, cache_k, cache_v, state_ssm_re, state_ssm_im,
           norm1, w_in, q_norm, k_norm, sinks,
           ssm_a_re, ssm_a_im, ssm_log_dt, ssm_b_re, ssm_b_im, ssm_c_re, ssm_c_im, ssm_d,
           w_glu, w_br_ssm, w_br_attn, w_gate, w_out, norm2,
           w_ffn_gate, w_ffn_up, w_ffn_down):
    f = lambda a: np.ascontiguousarray(np.asarray(a, dtype=np.float32))
    x_prompt, x_sample = f(x_prompt), f(x_sample)
    cache_k, cache_v = f(cache_k)[0], f(cache_v)[0]
    sre, sim = f(state_ssm_re)[0], f(state_ssm_im)[0]

    def gl_layout(a):
        a = a.reshape((32, 2, 64) + a.shape[2:])
        a = np.moveaxis(a, 0, 2)
        return np.ascontiguousarray(a.reshape((128, 32) + a.shape[3:]))

    shared = {
        "g1T": f(norm1)[0].reshape(16, 128).T.copy(),
        "g2T": f(norm2)[0].reshape(16, 128).T.copy(),
        "qnb": np.ascontiguousarray(np.broadcast_to(f(q_norm)[0][None, :], (128, 64))),
        "knb": np.ascontiguousarray(np.broadcast_to(f(k_norm)[0][None, :], (128, 64))),
        "are": gl_layout(f(ssm_a_re)[0]),
        "aim": gl_layout(f(ssm_a_im)[0]),
        "ldt": gl_layout(np.ascontiguousarray(np.broadcast_to(f(ssm_log_dt)[0][:, None], (64, 64)))),
        "bre": gl_layout(f(ssm_b_re)[0]).reshape(128, 512),
        "bim": gl_layout(f(ssm_b_im)[0]).reshape(128, 512),
        "dsk": f(ssm_d)[0].reshape(8, 128).T.copy(),
        "w_in": f(w_in)[0], "w_glu": f(w_glu)[0], "w_brs": f(w_br_ssm)[0], "w_bra": f(w_br_attn)[0],
        "w_gate": f(w_gate)[0], "w_out": f(w_out)[0], "w_fg": f(w_ffn_gate)[0], "w_fu": f(w_ffn_up)[0], "w_fd": f(w_ffn_down)[0],
    }
    sk = f(sinks)[0]
    skl = np.zeros((128, 8), np.float32)
    skl[0:64, :] = sk[0::2][None, :]
    skl[64:128, :] = sk[1::2][None, :]
    shared["skl"] = skl

    def c_layout(a):
        a = a.reshape(32, 2, 16, 64)
        a = np.transpose(a, (1, 2, 0, 3))
        return np.ascontiguousarray(a.reshape(32, 32 * 64))

    shared["cre"] = c_layout(f(ssm_c_re)[0])
    shared["cim"] = c_layout(f(ssm_c_im)[0])

    in_maps = []
    for c in range(NCORES):
        b, half = c // 2, c % 2
        m = dict(shared)
        m["xm"] = np.concatenate([x_prompt[b, half * 2048:(half + 1) * 2048], x_sample[4 * c:4 * c + 4].reshape(256, D)], axis=0)
        m["xc"] = x_prompt[b, 0:2048] if half == 1 else np.zeros((NCTX, D), np.float32)
        pos = np.concatenate([half * 2048 + np.arange(2048), np.tile(1024 + np.arange(64), 4), half * 2048 - 128 + np.arange(128)])
        tab = _rope_table(pos)
        m["rope"] = np.ascontiguousarray(tab.reshape(19, 128, 16).transpose(1, 0, 2))
        m["ck"] = np.ascontiguousarray(cache_k[4 * c:4 * c + 4].reshape(4, 128, 256))
        m["cv"] = np.ascontiguousarray(cache_v[4 * c:4 * c + 4].reshape(4, 128, 256))
        s0 = np.stack([gl_layout(np.moveaxis(sre[4 * c:4 * c + 4], 0, 2)), gl_layout(np.moveaxis(sim[4 * c:4 * c + 4], 0, 2))], axis=1)
        m["st0"] = np.ascontiguousarray(s0.reshape(128, 256))
        mb = np.zeros((128, 4), np.float32)
        if half == 0:
            mb[:, 0] = BIG
            mb[:, 1] = BIG
        else:
            mb[0:64, 1] = BIG
        mb[64:128, 2] = BIG
        mb[0:64, 3] = BIG
        m["maskb"] = mb
        in_maps.append(m)

    return in_maps


def _gather(R):
    def ungl(a):
        a = a.reshape((2, 64, 32) + a.shape[2:])
        a = np.moveaxis(a, 2, 0)
        return np.ascontiguousarray(a.reshape((64, 64) + a.shape[3:]))

    y_prompt = np.zeros((4, 4096, D), np.float32)
    y_sample = np.zeros((32, 64, D), np.float32)
    kwp = np.zeros((1, 4, 128, 4, 64), np.float32)
    vwp = np.zeros((1, 4, 128, 4, 64), np.float32)
    srp = np.zeros((1, 4, 64, 64), np.float32)
    sip = np.zeros((1, 4, 64, 64), np.float32)
    kws = np.zeros((1, 32, 128, 4, 64), np.float32)
    vws = np.zeros((1, 32, 128, 4, 64), np.float32)
    srs = np.zeros((1, 32, 64, 64), np.float32)
    sis = np.zeros((1, 32, 64, 64), np.float32)
    for c in range(NCORES):
        b, half = c // 2, c % 2
        r = R[c]
        y_prompt[b, half * 2048:(half + 1) * 2048] = r["ym"][0:2048]
        y_sample[4 * c:4 * c + 4] = r["ym"][2048:2304].reshape(4, 64, D)
        if half == 1:
            kwp[0, b] = r["kwin"].reshape(128, 4, 64)
            vwp[0, b] = r["vwin"].reshape(128, 4, 64)
            hf = r["hfin"].reshape(128, 2, 32)
            srp[0, b] = ungl(hf[:, 0])
            sip[0, b] = ungl(hf[:, 1])
        kws[0, 4 * c:4 * c + 4] = r["ks_o"].reshape(4, 128, 4, 64)
        vws[0, 4 * c:4 * c + 4] = r["vs_o"].reshape(4, 128, 4, 64)
        hs = r["hsfin"].reshape(128, 2, 32, 4)
        srs[0, 4 * c:4 * c + 4] = np.moveaxis(ungl(hs[:, 0]), 2, 0)
        sis[0, 4 * c:4 * c + 4] = np.moveaxis(ungl(hs[:, 1]), 2, 0)
    return (y_prompt, y_sample, kwp, vwp, srp, sip, kws, vws, srs, sis)


def kernel(x_prompt, x_sample, cache_k, cache_v, state_ssm_re, state_ssm_im,
           norm1, w_in, q_norm, k_norm, sinks,
           ssm_a_re, ssm_a_im, ssm_log_dt, ssm_b_re, ssm_b_im, ssm_c_re, ssm_c_im, ssm_d,
           w_glu, w_br_ssm, w_br_attn, w_gate, w_out, norm2,
           w_ffn_gate, w_ffn_up, w_ffn_down):
    in_maps = _prep(x_prompt, x_sample, cache_k, cache_v, state_ssm_re, state_ssm_im, norm1, w_in, q_norm, k_norm, sinks, ssm_a_re, ssm_a_im, ssm_log_dt, ssm_b_re, ssm_b_im, ssm_c_re, ssm_c_im, ssm_d, w_glu, w_br_ssm, w_br_attn, w_gate, w_out, norm2, w_ffn_gate, w_ffn_up, w_ffn_down)
    if "nc" not in _NC_CACHE:
        _NC_CACHE["nc"] = build_program()
    nc = _NC_CACHE["nc"]
    res = run_bass_kernel_spmd(nc, in_maps, core_ids=list(range(NCORES)))
    return _gather(res.results)
```
